# Optimizing a Trainium2 kernel written in Bass

```python
import math
import numpy as np
import jax
import jax.numpy as jnp
from jax import lax

D_MODEL = 2048
BATCH = 16
SEQ = 256
DEPTH = 4
DEC_BATCH = 8
DEC_SEQ = 2048
PAST_LEN = 512

GRID_W = 64
N_MIXERS = 4
N_DIFF = (DEPTH + 3) // 4
N_NA = (DEPTH + 2) // 4
N_MLA = (DEPTH + 1) // 4
N_SSD = DEPTH // 4
D_FF = 5632
Q_BLOCK = 128
ROPE_BASE = 10000.0
EPS = 1e-6

DIFF_HEADS = 8
DIFF_HEAD_DIM = D_MODEL // (2 * DIFF_HEADS)

NA_HEADS = 16
NA_HEAD_DIM = D_MODEL // NA_HEADS
NA_WIN_ROWS = 8
NA_WIN_COLS = 16
NA_QCOLS = 16
NA_KCOLS = 32

MLA_HEADS = 16
MLA_Q_LORA = 512
MLA_KV_LORA = 256
MLA_NOPE = 128
MLA_ROPE = 64
MLA_V = 128

SSD_D_INNER = 2 * D_MODEL
SSD_HEAD_DIM = 64
SSD_HEADS = SSD_D_INNER // SSD_HEAD_DIM
SSD_GROUPS = 8
SSD_HPG = SSD_HEADS // SSD_GROUPS
SSD_STATE = 128
SSD_CONV_W = 7
SSD_CHUNK = 128
SSD_CONV_CH = SSD_D_INNER + 2 * SSD_GROUPS * SSD_STATE
SSD_IN = SSD_D_INNER + SSD_CONV_CH + 2 * SSD_HEADS

kernel_name = 'bidir_hybrid_diffusion_step'


def _rms(x, w):
    xf = x.astype(jnp.float32)
    y = xf * lax.rsqrt(jnp.mean(xf * xf, axis=-1, keepdims=True) + EPS)
    return (y * w.astype(jnp.float32)).astype(x.dtype)


def _modulation(cond, w, b):
    m = jax.nn.silu(cond) @ w + b
    return m.reshape(cond.shape[0], 1, 3, 3, D_MODEL)


def _pre(x, mod, k, g):
    return _rms(x, g) * (1.0 + mod[:, :, k, 1]) + mod[:, :, k, 0]


def _swiglu(h, w_gu, w_down):
    g, u = jnp.split(h @ w_gu, 2, axis=-1)
    return (jax.nn.silu(g) * u) @ w_down


def _rope_axis(x, pos):
    h = x.shape[-1] // 2
    inv = ROPE_BASE ** (-jnp.arange(h, dtype=jnp.float32) / h)
    ang = pos.astype(jnp.float32)[:, None] * inv[None]
    shp = (1, pos.shape[0]) + (1,) * (x.ndim - 3) + (h,)
    cos, sin = jnp.cos(ang).reshape(shp), jnp.sin(ang).reshape(shp)
    xf = x.astype(jnp.float32)
    x1, x2 = xf[..., :h], xf[..., h:]
    return jnp.concatenate([x1 * cos - x2 * sin, x1 * sin + x2 * cos], axis=-1)


def _axial_rope(x):
    t = jnp.arange(x.shape[1])
    d = x.shape[-1] // 2
    out = jnp.concatenate([_rope_axis(x[..., :d], t // GRID_W),
                           _rope_axis(x[..., d:], t % GRID_W)], axis=-1)
    return out.astype(x.dtype)


def _to_blocks(q):
    b, s = q.shape[:2]
    return jnp.moveaxis(q.reshape((b, s // Q_BLOCK, Q_BLOCK) + q.shape[2:]), 1, 0)


def _from_blocks(o):
    o = jnp.moveaxis(o, 0, 1)
    return o.reshape((o.shape[0], -1) + o.shape[3:])


def _attention(q, k, v, scale):
    def block(qb):
        s = jnp.einsum('bqhd,bkhd->bhqk', qb, k).astype(jnp.float32) * scale
        p = jax.nn.softmax(s, axis=-1).astype(v.dtype)
        return jnp.einsum('bhqk,bkhd->bqhd', p, v)
    return _from_blocks(lax.map(block, _to_blocks(q)))


def _diff_attention(q, k, v, lam, scale):
    def block(qb):
        s = jnp.einsum('bqhmd,bkhmd->bhmqk', qb, k).astype(jnp.float32) * scale
        p = jax.nn.softmax(s, axis=-1)
        w = (p[:, :, 0] - lam * p[:, :, 1]).astype(v.dtype)
        return jnp.einsum('bhqk,bkhe->bqhe', w, v)
    return _from_blocks(lax.map(block, _to_blocks(q)))


def _diff_project(h, w_qkv):
    b, l, _ = h.shape
    q, k, v = jnp.split(h @ w_qkv, 3, axis=-1)
    return (q.reshape(b, l, DIFF_HEADS, 2, DIFF_HEAD_DIM),
            k.reshape(b, l, DIFF_HEADS, 2, DIFF_HEAD_DIM),
            v.reshape(b, l, DIFF_HEADS, 2 * DIFF_HEAD_DIM))


def _diff_lambda(lam_p, lam_init):
    lp = lam_p.astype(jnp.float32)
    return jnp.exp(jnp.sum(lp[0] * lp[1])) - jnp.exp(jnp.sum(lp[2] * lp[3])) + lam_init


def _diff_out(o, lam_init, subln_w, w_o):
    o = _rms(o, subln_w) * (1.0 - lam_init)
    return o.reshape(o.shape[0], o.shape[1], -1) @ w_o


def _na_project(h, w_qkv):
    b, l, _ = h.shape
    q, k, v = jnp.split(h @ w_qkv, 3, axis=-1)
    shp = (b, l, NA_HEADS, NA_HEAD_DIM)
    return q.reshape(shp), k.reshape(shp), v.reshape(shp)


def _na_latent(q, k, v, ck, cv, rpb, scale):
    b, L, H, dh = q.shape
    R = L // GRID_W
    kh = min(NA_WIN_ROWS, R)
    nqb = GRID_W // NA_QCOLS
    qcol = np.arange(GRID_W).reshape(nqb, NA_QCOLS)
    kstart = np.clip(np.arange(nqb) * NA_QCOLS - NA_WIN_COLS // 2, 0, GRID_W - NA_KCOLS)
    kcol = kstart[:, None] + np.arange(NA_KCOLS)[None]
    cstart = np.clip(qcol - NA_WIN_COLS // 2, 0, GRID_W - NA_WIN_COLS)
    kc3 = kcol[:, None, :]
    col_ok = (kc3 >= cstart[..., None]) & (kc3 < cstart[..., None] + NA_WIN_COLS)
    dc = np.clip(kc3 - qcol[..., None] + NA_WIN_COLS - 1, 0, 2 * NA_WIN_COLS - 2)
    rpb_c = rpb[:, :, dc]
    kg = k.reshape(b, R, GRID_W, H, dh)
    vg = v.reshape(b, R, GRID_W, H, dh)
    qg = jnp.moveaxis(q.reshape(b, R, nqb, NA_QCOLS, H, dh), 1, 0)
    nloc = kh * NA_KCOLS

    def row(args):
        r, qr = args
        rs = jnp.clip(r - kh // 2, 0, R - kh)
        kr = lax.dynamic_slice_in_dim(kg, rs, kh, axis=1)[:, :, kcol]
        vr = lax.dynamic_slice_in_dim(vg, rs, kh, axis=1)[:, :, kcol]
        ridx = rs + jnp.arange(kh) - r + NA_WIN_ROWS - 1
        bias = jnp.transpose(rpb_c[:, ridx], (0, 2, 3, 1, 4)).astype(jnp.float32)
        s_loc = jnp.einsum('bjqhd,brjkhd->bhjqrk', qr, kr).astype(jnp.float32) * scale + bias
        s_loc = jnp.where(col_ok[:, :, None, :], s_loc, -1e30)
        s_ctx = jnp.einsum('bjqhd,bphd->bhjqp', qr, ck).astype(jnp.float32) * scale
        s = jnp.concatenate([s_loc.reshape(b, H, nqb, NA_QCOLS, nloc), s_ctx], axis=-1)
        p = jax.nn.softmax(s, axis=-1).astype(v.dtype)
        p_loc = p[..., :nloc].reshape(b, H, nqb, NA_QCOLS, kh, NA_KCOLS)
        o = (jnp.einsum('bhjqrk,brjkhd->bjqhd', p_loc, vr)
             + jnp.einsum('bhjqp,bphd->bjqhd', p[..., nloc:], cv))
        return o.reshape(b, GRID_W, H, dh)

    o = lax.map(row, (jnp.arange(R), qg))
    return jnp.moveaxis(o, 0, 1).reshape(b, L, H, dh)


def _mla_queries(h, w_dq, q_norm, w_uq):
    b, l, _ = h.shape
    return (_rms(h @ w_dq, q_norm) @ w_uq).reshape(b, l, MLA_HEADS, MLA_NOPE + MLA_ROPE)


def _mla_compress(h, w_dkv, kv_norm):
    ckv = h @ w_dkv
    return _rms(ckv[..., :MLA_KV_LORA], kv_norm), ckv[..., MLA_KV_LORA:]


def _mla_expand(ckv, krope, w_ukv):
    b, l, _ = ckv.shape
    kv = (ckv @ w_ukv).reshape(b, l, MLA_HEADS, MLA_NOPE + MLA_V)
    kr = jnp.broadcast_to(krope[:, :, None].astype(kv.dtype), (b, l, MLA_HEADS, MLA_ROPE))
    return jnp.concatenate([kv[..., :MLA_NOPE], kr], axis=-1), kv[..., MLA_NOPE:]


def _dwconv_silu(u, w, bias):
    y = lax.conv_general_dilated(u, w[:, None, :].astype(u.dtype), window_strides=(1,),
                                 padding=[(SSD_CONV_W // 2, SSD_CONV_W // 2)],
                                 dimension_numbers=('NWC', 'WIO', 'NWC'),
                                 feature_group_count=u.shape[-1])
    return jax.nn.silu(y + bias)


def _ssd_scan(x, dt, A, Bm, Cm, h0):
    b, L, G, J, P = x.shape
    nc, Q = L // SSD_CHUNK, SSD_CHUNK
    xf = x.astype(jnp.float32).reshape(b, nc, Q, G, J, P)
    dt = dt.reshape(b, nc, Q, G, J)
    Bf = Bm.astype(jnp.float32).reshape(b, nc, Q, G, -1)
    Cf = Cm.astype(jnp.float32).reshape(b, nc, Q, G, -1)
    a_cum = jnp.cumsum(dt * A, axis=2)
    causal = jnp.tril(jnp.ones((Q, Q), bool))[None, None, :, :, None, None]
    seg = a_cum[:, :, :, None] - a_cum[:, :, None, :]
    lmat = jnp.exp(jnp.where(causal, seg, -jnp.inf))
    xdt = xf * dt[..., None]
    cb = jnp.einsum('bcqgn,bcsgn->bcqsg', Cf, Bf)
    y_diag = jnp.einsum('bcqsg,bcqsgj,bcsgjp->bcqgjp', cb, lmat, xdt)
    to_end = jnp.exp(a_cum[:, :, -1:] - a_cum)
    states = jnp.einsum('bcsgn,bcsgj,bcsgjp->bcgjpn', Bf, to_end, xdt)
    chunk_decay = jnp.exp(a_cum[:, :, -1])

    def step(h, inp):
        st, dcy = inp
        return h * dcy[..., None, None] + st, h

    h_last, h_in = lax.scan(step, h0, (jnp.moveaxis(states, 1, 0), jnp.moveaxis(chunk_decay, 1, 0)))
    h_in = jnp.moveaxis(h_in, 0, 1)
    y_off = jnp.einsum('bcqgn,bcgjpn,bcqgj->bcqgjp', Cf, h_in, jnp.exp(a_cum))
    return (y_diag + y_off).reshape(b, L, G, J, P), h_last


def _ssd_mix(h, h0, w_in, conv_w, conv_b, a_log, dt_bias, d_skip, norm_w, w_out):
    b, l, _ = h.shape
    proj = h @ w_in
    z = proj[..., :SSD_D_INNER]
    xbc = _dwconv_silu(proj[..., SSD_D_INNER:SSD_D_INNER + SSD_CONV_CH], conv_w, conv_b)
    gn = SSD_GROUPS * SSD_STATE
    x = xbc[..., :SSD_D_INNER].reshape(b, l, SSD_GROUPS, SSD_HPG, SSD_HEAD_DIM)
    Bm = xbc[..., SSD_D_INNER:SSD_D_INNER + gn].reshape(b, l, SSD_GROUPS, SSD_STATE)
    Cm = xbc[..., SSD_D_INNER + gn:].reshape(b, l, SSD_GROUPS, SSD_STATE)
    dt = proj[..., SSD_D_INNER + SSD_CONV_CH:].reshape(b, l, 2, SSD_HEADS)
    dt = jax.nn.softplus(dt.astype(jnp.float32) + dt_bias.astype(jnp.float32))
    dt = dt.reshape(b, l, 2, SSD_GROUPS, SSD_HPG)
    A = -jnp.exp(a_log.astype(jnp.float32)).reshape(2, SSD_GROUPS, SSD_HPG)
    h0 = h0.astype(jnp.float32).reshape(b, 2, SSD_GROUPS, SSD_HPG, SSD_HEAD_DIM, SSD_STATE)
    flip = lambda t: jnp.flip(t, axis=1)
    y_f, h_f = _ssd_scan(x, dt[:, :, 0], A[0], Bm, Cm, h0[:, 0])
    y_b, h_b = _ssd_scan(flip(x), flip(dt[:, :, 1]), A[1], flip(Bm), flip(Cm), h0[:, 1])
    dsk = d_skip.astype(jnp.float32).reshape(SSD_GROUPS, SSD_HPG)[..., None]
    y = y_f + flip(y_b) + dsk * x.astype(jnp.float32)
    y = y.reshape(b, l, SSD_D_INNER).astype(h.dtype)
    out = _rms(y * jax.nn.silu(z), norm_w) @ w_out
    h_T = jnp.stack([h_f, h_b], axis=1).reshape(b, 2, SSD_HEADS, SSD_HEAD_DIM, SSD_STATE)
    return out, h_T


def setup_inputs(seed: int = 0) -> dict:
    key = jax.random.key(seed)
    ks = iter(jax.random.split(key, 64))
    f32 = jnp.float32
    D = D_MODEL

    def nrm(shape, scale=1.0):
        return scale * jax.random.normal(next(ks), shape, f32)

    def gain(shape):
        return 1.0 + nrm(shape, 0.05)

    a_log = jnp.log(jax.random.uniform(next(ks), (N_SSD, 2, SSD_HEADS), f32, 1.0, 16.0))
    dt0 = jnp.exp(jax.random.uniform(next(ks), (N_SSD, 2, SSD_HEADS), f32,
                                     math.log(1e-3), math.log(1e-1)))
    dt_bias = dt0 + jnp.log(-jnp.expm1(-dt0))
    return {
        'x_prompt': nrm((BATCH, SEQ, D)),
        'x_sample': nrm((DEC_BATCH, DEC_SEQ, D)),
        'cache_diff_k': nrm((DEC_BATCH, N_DIFF, PAST_LEN, DIFF_HEADS, 2, DIFF_HEAD_DIM)),
        'cache_diff_v': nrm((DEC_BATCH, N_DIFF, PAST_LEN, DIFF_HEADS, 2 * DIFF_HEAD_DIM)),
        'cache_na_k': nrm((DEC_BATCH, N_NA, PAST_LEN, NA_HEADS, NA_HEAD_DIM)),
        'cache_na_v': nrm((DEC_BATCH, N_NA, PAST_LEN, NA_HEADS, NA_HEAD_DIM)),
        'cache_mla_ckv': nrm((DEC_BATCH, N_MLA, PAST_LEN, MLA_KV_LORA)),
        'cache_mla_krope': nrm((DEC_BATCH, N_MLA, PAST_LEN, MLA_ROPE)),
        'state_ssd': nrm((DEC_BATCH, N_SSD, 2, SSD_HEADS, SSD_HEAD_DIM, SSD_STATE), 0.1),
        'c': nrm((DEC_BATCH, D)),
        'c_ctx': nrm((D,)),
        'ada_w': nrm((DEPTH, D, 9 * D), 0.5 * D ** -0.5),
        'ada_b': nrm((DEPTH, 9 * D), 0.02),
        'norm_w': gain((DEPTH, 3, D)),
        'ffn_w_gu': nrm((DEPTH, 2, D, 2 * D_FF), D ** -0.5),
        'ffn_w_down': nrm((DEPTH, 2, D_FF, D), D_FF ** -0.5),
        'final_norm_w': gain((D,)),
        'diff_w_qkv': nrm((N_DIFF, D, 3 * D), D ** -0.5),
        'diff_w_o': nrm((N_DIFF, D, D), D ** -0.5),
        'diff_lambda': nrm((N_DIFF, 4, DIFF_HEAD_DIM), 0.1),
        'diff_subln_w': gain((N_DIFF, 2 * DIFF_HEAD_DIM)),
        'na_w_qkv': nrm((N_NA, D, 3 * D), D ** -0.5),
        'na_w_o': nrm((N_NA, D, D), D ** -0.5),
        'na_rpb': nrm((N_NA, NA_HEADS, 2 * NA_WIN_ROWS - 1, 2 * NA_WIN_COLS - 1), 0.02),
        'mla_w_dq': nrm((N_MLA, D, MLA_Q_LORA), D ** -0.5),
        'mla_q_norm': gain((N_MLA, MLA_Q_LORA)),
        'mla_w_uq': nrm((N_MLA, MLA_Q_LORA, MLA_HEADS * (MLA_NOPE + MLA_ROPE)), MLA_Q_LORA ** -0.5),
        'mla_w_dkv': nrm((N_MLA, D, MLA_KV_LORA + MLA_ROPE), D ** -0.5),
        'mla_kv_norm': gain((N_MLA, MLA_KV_LORA)),
        'mla_w_ukv': nrm((N_MLA, MLA_KV_LORA, MLA_HEADS * (MLA_NOPE + MLA_V)), MLA_KV_LORA ** -0.5),
        'mla_w_o': nrm((N_MLA, MLA_HEADS * MLA_V, D), (MLA_HEADS * MLA_V) ** -0.5),
        'ssd_w_in': nrm((N_SSD, D, SSD_IN), D ** -0.5),
        'ssd_conv_w': nrm((N_SSD, SSD_CONV_W, SSD_CONV_CH), SSD_CONV_W ** -0.5),
        'ssd_conv_b': nrm((N_SSD, SSD_CONV_CH), 0.02),
        'ssd_a_log': a_log,
        'ssd_dt_bias': dt_bias,
        'ssd_d': gain((N_SSD, SSD_HEADS)),
        'ssd_norm_w': gain((N_SSD, SSD_D_INNER)),
        'ssd_w_out': nrm((N_SSD, SSD_D_INNER, D), SSD_D_INNER ** -0.5),
    }


def reference(x_prompt, x_sample, cache_diff_k, cache_diff_v, cache_na_k, cache_na_v,
              cache_mla_ckv, cache_mla_krope, state_ssd, c, c_ctx,
              ada_w, ada_b, norm_w, ffn_w_gu, ffn_w_down, final_norm_w,
              diff_w_qkv, diff_w_o, diff_lambda, diff_subln_w,
              na_w_qkv, na_w_o, na_rpb,
              mla_w_dq, mla_q_norm, mla_w_uq, mla_w_dkv, mla_kv_norm, mla_w_ukv, mla_w_o,
              ssd_w_in, ssd_conv_w, ssd_conv_b, ssd_a_log, ssd_dt_bias, ssd_d, ssd_norm_w, ssd_w_out):
    xp, xs = x_prompt, x_sample
    bp, bs = xp.shape[0], xs.shape[0]
    new_dk, new_dv, new_nk, new_nv, new_ckv, new_kr, new_h = [], [], [], [], [], [], []
    for i in range(DEPTH):
        kind, j = i % N_MIXERS, i // N_MIXERS
        mp = _modulation(c_ctx[None], ada_w[i], ada_b[i])
        ms = _modulation(c, ada_w[i], ada_b[i])
        xp = xp + 0.5 * mp[:, :, 0, 2] * _swiglu(_pre(xp, mp, 0, norm_w[i, 0]), ffn_w_gu[i, 0], ffn_w_down[i, 0])
        xs = xs + 0.5 * ms[:, :, 0, 2] * _swiglu(_pre(xs, ms, 0, norm_w[i, 0]), ffn_w_gu[i, 0], ffn_w_down[i, 0])
        hp = _pre(xp, mp, 1, norm_w[i, 1])
        hs = _pre(xs, ms, 1, norm_w[i, 1])
        if kind == 0:
            lam_init = 0.8 - 0.6 * math.exp(-0.3 * i)
            lam = _diff_lambda(diff_lambda[j], lam_init)
            scale = DIFF_HEAD_DIM ** -0.5
            q, k, v = _diff_project(hp, diff_w_qkv[j])
            o_p = _diff_out(_diff_attention(q, k, v, lam, scale), lam_init, diff_subln_w[j], diff_w_o[j])
            new_dk.append(k)
            new_dv.append(v)
            q, k, v = _diff_project(hs, diff_w_qkv[j])
            k_all = jnp.concatenate([_axial_rope(k), cache_diff_k[:, j].astype(k.dtype)], axis=1)
            v_all = jnp.concatenate([v, cache_diff_v[:, j].astype(v.dtype)], axis=1)
            o_s = _diff_out(_diff_attention(_axial_rope(q), k_all, v_all, lam, scale),
                            lam_init, diff_subln_w[j], diff_w_o[j])
        elif kind == 1:
            scale = NA_HEAD_DIM ** -0.5
            q, k, v = _na_project(hp, na_w_qkv[j])
            o_p = _attention(q, k, v, scale).reshape(bp, -1, D_MODEL) @ na_w_o[j]
            new_nk.append(k)
            new_nv.append(v)
            q, k, v = _na_project(hs, na_w_qkv[j])
            o = _na_latent(q, k, v, cache_na_k[:, j].astype(k.dtype), cache_na_v[:, j].astype(v.dtype),
                           na_rpb[j], scale)
            o_s = o.reshape(bs, -1, D_MODEL) @ na_w_o[j]
        elif kind == 2:
            scale = (MLA_NOPE + MLA_ROPE) ** -0.5
            q = _mla_queries(hp, mla_w_dq[j], mla_q_norm[j], mla_w_uq[j])
            ckv, kr = _mla_compress(hp, mla_w_dkv[j], mla_kv_norm[j])
            k, v = _mla_expand(ckv, kr, mla_w_ukv[j])
            o_p = _attention(q, k, v, scale).reshape(bp, -1, MLA_HEADS * MLA_V) @ mla_w_o[j]
            new_ckv.append(ckv)
            new_kr.append(kr)
            q = _mla_queries(hs, mla_w_dq[j], mla_q_norm[j], mla_w_uq[j])
            q = jnp.concatenate([q[..., :MLA_NOPE], _axial_rope(q[..., MLA_NOPE:])], axis=-1)
            ckv, kr = _mla_compress(hs, mla_w_dkv[j], mla_kv_norm[j])
            kr = _axial_rope(kr[:, :, None])[:, :, 0]
            k, v = _mla_expand(ckv, kr, mla_w_ukv[j])
            k_c, v_c = _mla_expand(cache_mla_ckv[:, j].astype(ckv.dtype), cache_mla_krope[:, j], mla_w_ukv[j])
            o = _attention(q, jnp.concatenate([k, k_c], axis=1), jnp.concatenate([v, v_c], axis=1), scale)
            o_s = o.reshape(bs, -1, MLA_HEADS * MLA_V) @ mla_w_o[j]
        else:
            h0 = jnp.zeros((bp, 2, SSD_HEADS, SSD_HEAD_DIM, SSD_STATE), jnp.float32)
            o_p, h_T = _ssd_mix(hp, h0, ssd_w_in[j], ssd_conv_w[j], ssd_conv_b[j], ssd_a_log[j],
                                ssd_dt_bias[j], ssd_d[j], ssd_norm_w[j], ssd_w_out[j])
            new_h.append(h_T.astype(xp.dtype))
            o_s, _ = _ssd_mix(hs, state_ssd[:, j], ssd_w_in[j], ssd_conv_w[j], ssd_conv_b[j], ssd_a_log[j],
                              ssd_dt_bias[j], ssd_d[j], ssd_norm_w[j], ssd_w_out[j])
        xp = xp + mp[:, :, 1, 2] * o_p
        xs = xs + ms[:, :, 1, 2] * o_s
        xp = xp + 0.5 * mp[:, :, 2, 2] * _swiglu(_pre(xp, mp, 2, norm_w[i, 2]), ffn_w_gu[i, 1], ffn_w_down[i, 1])
        xs = xs + 0.5 * ms[:, :, 2, 2] * _swiglu(_pre(xs, ms, 2, norm_w[i, 2]), ffn_w_gu[i, 1], ffn_w_down[i, 1])
    y_prompt = _rms(xp, final_norm_w)
    y_sample = _rms(xs, final_norm_w)
    new_diff_k = jnp.stack(new_dk, axis=1)
    new_diff_v = jnp.stack(new_dv, axis=1)
    new_na_k = jnp.stack(new_nk, axis=1)
    new_na_v = jnp.stack(new_nv, axis=1)
    new_mla_ckv = jnp.stack(new_ckv, axis=1)
    new_mla_krope = jnp.stack(new_kr, axis=1)
    new_state_ssd = jnp.stack(new_h, axis=1)
    return (y_prompt, y_sample, new_diff_k, new_diff_v, new_na_k, new_na_v, new_mla_ckv, new_mla_krope, new_state_ssd)
```

```python
import contextlib
import numpy as np
import ml_dtypes
import concourse.bass as bass
import concourse.mybir as mybir
from concourse.bass_utils import run_bass_kernel_spmd

F32 = mybir.dt.float32
BF16 = mybir.dt.bfloat16
AF = mybir.ActivationFunctionType
ALU = mybir.AluOpType

D = 2048
DC = 16
T = 2560
TS = 2048
DFF = 5632
FC = 44
EPS = 1e-6
DEPTH = 4

ENGS = ("pe", "act", "dve", "pool", "sp")
N_DMA_SEMS = 12


class _Op:
    __slots__ = ("eng", "fn", "deps", "dma", "signal", "sem", "val")

    def __init__(self, eng, fn, dma):
        self.eng, self.fn, self.dma = eng, fn, dma
        self.deps = []
        self.signal = dma
        self.sem = None
        self.val = 0


class Sched:
    def __init__(self, nc, stack):
        self.nc = nc
        self.esem = {e: stack.enter_context(nc.semaphore("s_" + e)) for e in ENGS}
        self.ecount = {e: 0 for e in ENGS}
        qs = ("sp", "pool", "act")
        self.dsem = {q: [stack.enter_context(nc.semaphore("d_%s%d" % (q, i)))
                         for i in range(N_DMA_SEMS)] for q in qs}
        self.dcount = {q: [0] * N_DMA_SEMS for q in qs}
        self.dlast = {q: [None] * N_DMA_SEMS for q in qs}
        self.drr = {q: 0 for q in qs}
        self.waited = {e: {} for e in ENGS}
        self.ops = []
        self.lastw = {}
        self.readers = {}
        self.n_ins = 0

    def op(self, eng, fn, reads=(), writes=(), dma=False):
        o = _Op(eng, fn, dma)
        deps = set()
        for r in reads:
            w = self.lastw.get(r)
            if w is not None:
                deps.add(w)
            if isinstance(r, str) and r.startswith("ps"):
                for rd in self.readers.get(r, ()):
                    if rd.eng != eng:
                        deps.add(rd)
        for w_ in writes:
            w = self.lastw.get(w_)
            if w is not None:
                deps.add(w)
            for rd in self.readers.get(w_, ()):
                deps.add(rd)
        if dma:
            q = eng
            i = self.drr[q]
            self.drr[q] = (i + 1) % N_DMA_SEMS
            prev = self.dlast[q][i]
            if prev is not None:
                deps.add(prev)
            self.dcount[q][i] += 16
            o.sem, o.val = self.dsem[q][i], self.dcount[q][i]
            self.dlast[q][i] = o
        for d in deps:
            if d is o:
                continue
            if (not d.dma) and (not dma) and d.eng == "pe" and eng == "pe":
                continue
            o.deps.append(d)
            d.signal = True
        for r in reads:
            self.readers.setdefault(r, []).append(o)
        for w_ in writes:
            self.lastw[w_] = o
            self.readers[w_] = []
        self.ops.append(o)
        return o

    def dma(self, q, out, in_, reads=(), writes=()):
        return self.op(q, lambda e: e.dma_start(out=out, in_=in_), reads, writes, dma=True)

    def flush(self):
        nc = self.nc
        ops = self.ops
        self.ops = []
        if not ops:
            return
        self.nflush = getattr(self, "nflush", 0) + 1
        mf = getattr(self, "max_flush", None)
        if mf is not None and self.nflush > mf:
            self.lastw = {}
            self.readers = {}
            return
        for o in ops:
            if not o.dma and o.signal:
                self.ecount[o.eng] += 1
                o.sem, o.val = self.esem[o.eng], self.ecount[o.eng]
        by_eng = {e: [o for o in ops if o.eng == e] for e in ENGS}
        for e in ENGS:
            comp = [o for o in by_eng[e] if not o.dma]
            if comp and not comp[-1].signal:
                o = comp[-1]
                o.signal = True
                self.ecount[e] += 1
                o.sem, o.val = self.esem[e], self.ecount[e]
        targets = [(self.esem[e], self.ecount[e]) for e in ENGS if self.ecount[e] > 0]
        for q in self.dsem:
            for i in range(N_DMA_SEMS):
                if self.dcount[q][i] > 0:
                    targets.append((self.dsem[q][i], self.dcount[q][i]))
        for q in self.dlast:
            self.dlast[q] = [None] * N_DMA_SEMS

        def emit(eng_name):
            def body(e):
                wd = self.waited[eng_name]
                for o in by_eng[eng_name]:
                    for d in o.deps:
                        if wd.get(d.sem, 0) < d.val:
                            e.wait_ge(d.sem, d.val)
                            wd[d.sem] = d.val
                    ins = o.fn(e)
                    self.n_ins += 1
                    if o.signal:
                        ins.then_inc(o.sem, 16 if o.dma else 1)
                for s, v in targets:
                    if wd.get(s, 0) < v:
                        e.wait_ge(s, v)
                        wd[s] = v
            return body

        with nc.Block() as block:
            block.tensor(emit("pe"))
            block.scalar(emit("act"))
            block.vector(emit("dve"))
            block.gpsimd(emit("pool"))
            block.sync(emit("sp"))
        self.lastw = {}
        self.readers = {}


_uid = [0]


def uid():
    _uid[0] += 1
    return _uid[0]


class Ring:
    def __init__(self, ctx, st, name, shape, dtype, n):
        u = uid()
        self.tiles = [st.enter_context(ctx.nc.sbuf_tensor("%s_%d_%d" % (name, u, i), shape, dtype))
                      for i in range(n)]
        self.keys = ["%s_%d_%d" % (name, u, i) for i in range(n)]
        self.i = 0
        self.n = n

    def next(self):
        t, k = self.tiles[self.i], self.keys[self.i]
        self.i = (self.i + 1) % self.n
        return t, k


class Ctx:
    pass


class PV:
    def __init__(self):
        self.cols = {}
        self.n = 0
        self.parts = []

    def add(self, name, arr):
        arr = np.ascontiguousarray(arr, dtype=np.float32)
        assert arr.shape[0] == 128
        self.cols[name] = (self.n, arr.shape[1])
        self.n += arr.shape[1]
        self.parts.append(arr)

    def build(self):
        return np.concatenate(self.parts, axis=1)


def fm(v):
    v = np.asarray(v, dtype=np.float32)
    c = v.shape[-1] // 128
    return np.ascontiguousarray(v.reshape(-1, 128).T)


def pv_layout(inp, core, kinds=()):
    pv = PV()
    pv.add("ada_b", fm(inp["ada_b"]))
    pv.add("norm_w", fm(inp["norm_w"]))
    pv.add("final_w", fm(inp["final_norm_w"]))
    cond = np.stack([inp["c"][core], inp["c_ctx"]], 0)
    pv.add("cond", fm(cond).reshape(128, 2, 16).transpose(0, 2, 1).reshape(128, 32))
    if 0 in kinds:
        pv.add("dlam", np.broadcast_to(inp["diff_lambda"][0].reshape(1, 512), (128, 512)))
        pv.add("subln", fm(inp["diff_subln_w"][0]))
    if 3 in kinds:
        cw = np.stack([fm(inp["ssd_conv_w"][0][w]) for w in range(7)], -1)
        pv.add("conv_w", cw.reshape(128, 48 * 7))
        pv.add("conv_b", fm(inp["ssd_conv_b"][0]))
        pv.add("ssd_nw", fm(inp["ssd_norm_w"][0]))
        pv.add("dt_bias", inp["ssd_dt_bias"][0].reshape(128, 1))
        pv.add("a_log", inp["ssd_a_log"][0].reshape(128, 1))
        pv.add("ssd_d", np.broadcast_to(inp["ssd_d"][0].reshape(1, 64), (128, 64)))
    if 2 in kinds:
        pv.add("q_norm", fm(inp["mla_q_norm"][0]))
        pv.add("kv_norm", fm(inp["mla_kv_norm"][0]))
    return pv


def pvap(ctx, name, col, n=1):
    o, k = ctx.pvcols[name]
    assert col + n <= k
    return ctx.pvt[:, o + col:o + col + n]


def fmv(ap):
    return ap.rearrange("c p t -> p c t")


def psum_next(ctx, lo=0, hi=6):
    i = ctx.ps_i
    if i < lo or i >= hi:
        i = lo
    ctx.ps_i = i + 1 if i + 1 < hi else lo
    return ctx.ps[i], "ps%d" % i


def gemm(ctx, W, KC, blocks, src, tok_tiles, epi, Wc, nw=3, tsub=512):
    S = ctx.S
    with contextlib.ExitStack() as st:
        maxT = max(tn for _, tn in tok_tiles)
        wring = Ring(ctx, st, "gw", [128, KC, Wc], BF16, nw)
        sring = Ring(ctx, st, "gs", [128, KC, maxT], BF16, 2 if KC * maxT * 4 <= 70000 else 1)
        for (t0, tn) in tok_tiles:
            stile, sk = sring.next()
            S.dma("sp", stile[:, :, :tn], src[:, :, t0:t0 + tn], writes=[sk])
            for segs, items in blocks:
                wt, wk0 = wring.next()
                o = 0
                wk = []
                for si, (c0, ncol) in enumerate(segs):
                    wk.append("%s_%d" % (wk0, si))
                    S.dma("pool", wt[:, :, o:o + ncol],
                          W[:, c0:c0 + ncol].rearrange("(c p) n -> p c n", p=128), writes=[wk[-1]])
                    o += ncol
                for (ci, off, w) in items:
                    for s0 in range(0, tn, tsub):
                        sn = min(tsub, tn - s0)
                        ps, pk = psum_next(ctx)
                        for kc in range(KC):
                            S.op("pe", lambda e, ps=ps, wt=wt, stile=stile, kc=kc, off=off, w=w, s0=s0, sn=sn:
                                 e.matmul(ps[:w, :sn], wt[:, kc, off:off + w], stile[:, kc, s0:s0 + sn],
                                          start=(kc == 0), stop=(kc == KC - 1)),
                                 reads=wk + [sk], writes=[pk])
                        epi(ci, w, t0 + s0, sn, ps, pk)
        ctx.flush()


def simple_blocks(chunks, Wc):
    blocks = []
    cur = None
    for ci, (c0, w) in enumerate(chunks):
        if cur is not None and cur[0][0][0] + cur[0][0][1] == c0 and cur[0][0][1] + w <= Wc:
            segs, items = cur
            items.append((ci, segs[0][1], w))
            segs[0] = (segs[0][0], segs[0][1] + w)
        else:
            cur = ([(c0, w)], [(ci, 0, w)])
            blocks.append(cur)
    return blocks


def rms_phase(ctx, src, C, nfeat, dst, a_fn, b_fn, tok0=0, tok1=T, out_f32=False, dst2=None, dst2_range=None, dst_off=0):
    S = ctx.S
    with contextlib.ExitStack() as st:
        xr = Ring(ctx, st, "rx", [128, C, 512], F32, 2)
        sr = Ring(ctx, st, "rsq", [128, C, 512], BF16, 1)
        hr = Ring(ctx, st, "rh", [128, C, 512], F32 if out_f32 else BF16, 2)
        rr = Ring(ctx, st, "rr", [128, 512], F32, 2)
        tiles = [(t0, min(512, tok1 - t0)) for t0 in range(tok0, tok1, 512)]

        def stage_a(t0, tn):
            xt, xk = xr.next()
            sq, sqk = sr.next()
            rs, rk = rr.next()
            xks = ["%s_%d" % (xk, c) for c in range(C)]
            S.dma("sp", xt[:, :, :tn], src[:, :, t0:t0 + tn].rearrange("c p t -> p c t"), writes=xks)
            S.op("act", lambda e: e.activation(sq[:, :, :tn], xt[:, :, :tn], AF.Square), reads=xks, writes=[sqk])
            ps, pk = psum_next(ctx, 6, 8)
            for c in range(C):
                S.op("pe", lambda e, c=c: e.matmul(ps[:, :tn], ctx.ones_bf[:], sq[:, c, :tn], start=(c == 0), stop=(c == C - 1)),
                     reads=[sqk, "const"], writes=[pk])
            S.op("act", lambda e: e.activation(rs[:, :tn], ps[:, :tn], AF.Sqrt, bias=ctx.eps_ap, scale=1.0 / nfeat),
                 reads=[pk, "const"], writes=[rk])
            S.op("dve", lambda e: e.reciprocal(rs[:, :tn], rs[:, :tn]), reads=[rk], writes=[rk])
            return (xt, xks, rs, rk)

        def stage_b(t0, tn, a_out):
            xt, xks, rs, rk = a_out
            cond = 1 if t0 >= TS else 0
            ht, hk = hr.next()
            hks = ["%s_%d" % (hk, c) for c in range(C)]
            for c in range(C):
                if b_fn is None:
                    S.op("dve", lambda e, c=c: e.scalar_tensor_tensor(ht[:, c, :tn], xt[:, c, :tn], a_fn(c, cond), rs[:, :tn], ALU.mult, ALU.mult),
                         reads=[xks[c], rk, "mod"], writes=[hks[c]])
                else:
                    S.op("dve", lambda e, c=c: e.scalar_tensor_tensor(xt[:, c, :tn], xt[:, c, :tn], a_fn(c, cond), rs[:, :tn], ALU.mult, ALU.mult),
                         reads=[xks[c], rk, "mod"], writes=[xks[c]])
                    S.op("act", lambda e, c=c: e.activation(ht[:, c, :tn], xt[:, c, :tn], AF.Identity, bias=b_fn(c, cond)),
                         reads=[xks[c], "mod"], writes=[hks[c]])
            S.dma("sp", dst[:, :, t0 - dst_off:t0 - dst_off + tn].rearrange("c p t -> p c t"), ht[:, :, :tn], reads=hks)

        prev = None
        for (t0, tn) in tiles:
            cur = (t0, tn, stage_a(t0, tn))
            if prev is not None:
                stage_b(*prev)
            prev = cur
        stage_b(*prev)
        ctx.flush()


def resid_epi(ctx, st, gate_fn):
    S = ctx.S
    xr = Ring(ctx, st, "ex", [128, 512], F32, 4)

    def epi(ci, w, t0, tn, ps, pk):
        cond = 1 if t0 >= TS else 0
        xt, xk = xr.next()
        S.dma("sp", xt[:, :tn], ctx.xT[ci, :, t0:t0 + tn], writes=[xk])
        S.op("dve", lambda e: e.scalar_tensor_tensor(xt[:, :tn], ps[:, :tn], gate_fn(ci, cond), xt[:, :tn],
                                                      ALU.mult, ALU.add),
             reads=[pk, xk, "mod"], writes=[xk])
        S.dma("sp", ctx.xT[ci, :, t0:t0 + tn], xt[:, :tn], reads=[xk])
    return epi


def mod_phase(ctx, layer):
    S = ctx.S
    W = ctx.w["ada_w"][layer]

    def epi(ci, w, t0, tn, ps, pk):
        S.op("dve", lambda e: e.tensor_scalar(ctx.modt[:, ci, :], ps[:, 0:2], pvap(ctx, "ada_b", layer * 144 + ci), None, ALU.add),
             reads=[pk, "pv"], writes=["modraw"])
    chunks = [(ci * 128, 128) for ci in range(144)]
    gemm(ctx, W, DC, simple_blocks(chunks, 512), ctx.condT, [(0, 2)], epi, Wc=512)
    for k in range(3):
        for cond in range(2):
            nw = pvap(ctx, "norm_w", (layer * 3 + k) * 16, 16)
            sc = ctx.modt[:, k * 48 + 16:k * 48 + 32, cond]
            sh = ctx.modt[:, k * 48 + 0:k * 48 + 16, cond]
            gt = ctx.modt[:, k * 48 + 32:k * 48 + 48, cond]
            S.op("dve", lambda e, k=k, cond=cond, sc=sc, nw=nw:
                 e.scalar_tensor_tensor(ctx.A[:, k, :, cond], sc, 1.0, nw, ALU.add, ALU.mult),
                 reads=["modraw", "pv"], writes=["mod"])
            S.op("dve", lambda e, k=k, cond=cond, sh=sh: e.tensor_copy(ctx.B[:, k, :, cond], sh),
                 reads=["modraw"], writes=["mod"])
            S.op("dve", lambda e, k=k, cond=cond, gt=gt:
                 e.tensor_scalar(ctx.G[:, k, :, cond], gt, 0.5 if k != 1 else 1.0, None, ALU.mult),
                 reads=["modraw"], writes=["mod"])
    ctx.flush()


def ffn_phase(ctx, layer, which):
    S = ctx.S
    k = 0 if which == 0 else 2
    rms_phase(ctx, ctx.xT, DC, D, ctx.hT,
              lambda c, cond: ctx.A[:, k, c, cond:cond + 1], lambda c, cond: ctx.B[:, k, c, cond:cond + 1])
    Wgu = ctx.w["ffn_w_gu"][layer, which]
    blocks = []
    for j2 in range(FC // 2):
        segs = [(j2 * 256, 256), (DFF + j2 * 256, 256)]
        items = []
        for jj in range(2):
            items.append((2 * (2 * j2 + jj), jj * 128, 128))
            items.append((2 * (2 * j2 + jj) + 1, 256 + jj * 128, 128))
        blocks.append((segs, items))
    with contextlib.ExitStack() as st:
        sgr = Ring(ctx, st, "sg", [128, T], F32, 2)
        ar = Ring(ctx, st, "fa", [128, T], BF16, 2)
        state = {}

        def epi(ci, w, t0, tn, ps, pk):
            j = ci // 2
            if ci % 2 == 0:
                if t0 == 0:
                    state["sg"] = sgr.next()
                sg, sgk = state["sg"]
                S.op("act", lambda e: e.activation(sg[:, t0:t0 + tn], ps[:, :tn], AF.Silu), reads=[pk], writes=[sgk])
            else:
                sg, sgk = state["sg"]
                if t0 == 0:
                    state["a"] = ar.next()
                at, ak = state["a"]
                S.op("dve", lambda e: e.tensor_tensor(at[:, t0:t0 + tn], sg[:, t0:t0 + tn], ps[:, :tn], ALU.mult),
                     reads=[pk, sgk], writes=[ak])
                if t0 + tn == T:
                    S.dma("sp", ctx.aT[j], at[:], reads=[ak])
        gemm(ctx, Wgu, DC, blocks, fmv(ctx.hT), [(0, T)], epi, Wc=512)
    with contextlib.ExitStack() as st:
        epi = resid_epi(ctx, st, lambda ci, cond: ctx.G[:, k, ci, cond:cond + 1])
        Wd = ctx.w["ffn_w_down"][layer, which]
        chunks = [(ci * 128, 128) for ci in range(DC)]
        gemm(ctx, Wd, FC, simple_blocks(chunks, 256), fmv(ctx.aT), [(0, 1024), (1024, 1024), (2048, 512)], epi, Wc=256, nw=2)


def store_epi(ctx, st, dst_fn):
    S = ctx.S
    rb = Ring(ctx, st, "seb", [128, 512], BF16, 4)
    rf = Ring(ctx, st, "sef", [128, 512], F32, 3)
    cnt = [0]

    def epi(ci, w, t0, tn, ps, pk):
        dsts = sorted(dst_fn(ci, t0, tn), key=lambda x: 0 if x[1] == F32 else 1)
        first = None
        for (dst, dt) in dsts:
            tl, tk = (rb if dt == BF16 else rf).next()
            if first is None:
                cnt[0] += 1
                if cnt[0] % 2 == 0:
                    S.op("act", lambda e, tl=tl: e.activation(tl[:w, :tn], ps[:w, :tn], AF.Identity), reads=[pk], writes=[tk])
                else:
                    S.op("dve", lambda e, tl=tl: e.tensor_copy(tl[:w, :tn], ps[:w, :tn]), reads=[pk], writes=[tk])
                first = (tl, tk)
            else:
                S.op("pool", lambda e, tl=tl, f=first[0]: e.tensor_copy(tl[:w, :tn], f[:w, :tn]), reads=[first[1]], writes=[tk])
            S.dma("sp", dst, tl[:w, :tn], reads=[tk])
    return epi


def vtrans_phase(ctx, srcT, nchunks, ntok, vtok, tok_off=0, col_off=0):
    S = ctx.S
    with contextlib.ExitStack() as st:
        vt = st.enter_context(ctx.nc.sbuf_tensor("vtT_%d" % uid(), [128, nchunks, ntok], BF16))
        rr = Ring(ctx, st, "vrow", [128, nchunks * 128], BF16, 3)
        half = (nchunks + 1) // 2
        for c0 in range(0, nchunks, half):
            S.dma("sp", vt[:, c0:c0 + half, :], srcT[:, c0:c0 + half, :], writes=["vt%d" % c0])
        vkeys = ["vt%d" % c0 for c0 in range(0, nchunks, half)]
        for tb in range(ntok // 128):
            row, rk = rr.next()
            for c8 in range(0, nchunks, 8):
                ps, pk = psum_next(ctx, 0, 8)
                psb = ps.bitcast(BF16)
                n8 = min(8, nchunks - c8)
                for c in range(c8, c8 + n8):
                    S.op("pe", lambda e, psb=psb, c=c, c8=c8, tb=tb:
                         e.transpose(psb[:, (c - c8) * 128:(c - c8 + 1) * 128], vt[:, c, tb * 128:(tb + 1) * 128], ctx.ident_bf),
                         reads=vkeys + ["const"], writes=[pk])
                if (c8 // 8) % 2 == 0:
                    S.op("dve", lambda e, psb=psb, row=row, c8=c8, n8=n8:
                         e.tensor_copy(row[:, c8 * 128:(c8 + n8) * 128], psb[:, :n8 * 128]), reads=[pk], writes=[rk + "a"])
                else:
                    S.op("act", lambda e, psb=psb, row=row, c8=c8, n8=n8:
                         e.activation(row[:, c8 * 128:(c8 + n8) * 128], psb[:, :n8 * 128], AF.Identity), reads=[pk], writes=[rk + "b"])
            S.dma("sp", vtok[tok_off + tb * 128:tok_off + (tb + 1) * 128, col_off:col_off + nchunks * 128], row[:], reads=[rk + "a", rk + "b"])
        ctx.flush()


def rope_phase(ctx, items, rope_tab, R, d):
    S = ctx.S
    with contextlib.ExitStack() as st:
        cs = st.enter_context(ctx.nc.sbuf_tensor("ropecs_%d" % uid(), [d, 2, TS], F32))
        S.dma("sp", cs[:, 0, :], rope_tab[0], writes=["cs0"])
        S.dma("sp", cs[:, 1, :], rope_tab[1], writes=["cs1"])
        xr = Ring(ctx, st, "ropx", [d, TS], BF16, 3)
        t1r = Ring(ctx, st, "ropt1", [d, 512], F32, 3)
        t2r = Ring(ctx, st, "ropt2", [d, 512], F32, 3)
        orr = Ring(ctx, st, "ropo", [d, TS], BF16, 3)
        for ap in items:
            xt, xk = xr.next()
            ot, ok = orr.next()
            S.dma("sp", xt[:], ap, writes=[xk])
            for s0 in range(0, TS, 512):
                ps, pk = psum_next(ctx, 0, 8)
                t1, k1 = t1r.next()
                t2, k2 = t2r.next()
                S.op("pe", lambda e, ps=ps, xt=xt, s0=s0: e.matmul(ps[:d, :512], R, xt[:, s0:s0 + 512], start=True, stop=True),
                     reads=[xk, "const"], writes=[pk])
                S.op("pool", lambda e, t1=t1, xt=xt, s0=s0: e.tensor_tensor(t1[:], xt[:, s0:s0 + 512], cs[:, 0, s0:s0 + 512], ALU.mult),
                     reads=[xk, "cs0"], writes=[k1])
                S.op("dve", lambda e, t2=t2, ps=ps, s0=s0: e.tensor_tensor(t2[:], ps[:d, :512], cs[:, 1, s0:s0 + 512], ALU.mult),
                     reads=[pk, "cs1"], writes=[k2])
                S.op("dve", lambda e, ot=ot, t1=t1, t2=t2, s0=s0: e.tensor_tensor(ot[:, s0:s0 + 512], t1[:], t2[:], ALU.add),
                     reads=[k1, k2], writes=[ok])
            S.dma("sp", ap, ot[:], reads=[ok])
        ctx.flush()


def attention(ctx, units, scale, out_dt):
    S = ctx.S
    with contextlib.ExitStack() as st:
        npc = max(len(u["pieces"]) for u in units)
        nkmax = max(sum(v.shape[0] for v in u["vsegs"]) for u in units)
        nqmax = max(u["nq"] for u in units)
        dvmax = max(u["dv"] for u in units)
        kr = [Ring(ctx, st, "atk%d" % i, [128, nkmax], BF16, 2) for i in range(npc)]
        qr = [Ring(ctx, st, "atq%d" % i, [128, nqmax], BF16, 2) for i in range(npc)]
        vr = Ring(ctx, st, "atv", [128, nkmax // 128, dvmax], BF16, 2)
        pr = Ring(ctx, st, "atp", [128, 512], BF16, 4)
        rdr = Ring(ctx, st, "atrd", [128, 512], F32, 2)
        otr = Ring(ctx, st, "ato", [128, dvmax // 128, 512], out_dt, 2)
        sbank = [0]
        aset = [0]

        def ld(dst, src, key):
            q = "sp" if src.dtype == BF16 else "pool"
            S.dma(q, dst, src, writes=[key])

        def load_unit(u):
            nq, dv = u["nq"], u["dv"]
            L = dict(ktiles=[], qtiles=[], kkeys=[], qkeys=[])
            for pi, (d, q_ap, k_aps) in enumerate(u["pieces"]):
                kt, kk = kr[pi].next()
                qt, qk = qr[pi].next()
                o = 0
                kks = []
                for si, ka in enumerate(k_aps):
                    n = ka.shape[1]
                    ld(kt[:d, o:o + n], ka, "%s_%d" % (kk, si))
                    kks.append("%s_%d" % (kk, si))
                    o += n
                ld(qt[:d, :nq], q_ap, qk)
                L["ktiles"].append(kt); L["qtiles"].append(qt); L["kkeys"].append(kks); L["qkeys"].append(qk)
            vt, vk = vr.next()
            o = 0
            vks = []
            for si, va in enumerate(u["vsegs"]):
                n = va.shape[0]
                ld(vt[:, o // 128:(o + n) // 128, :dv], va.rearrange("(c p) d -> p c d", p=128), "%s_%d" % (vk, si))
                vks.append("%s_%d" % (vk, si))
                o += n
            L["vt"], L["vks"] = vt, vks
            return L

        steps = []
        for ui, u in enumerate(units):
            nk = sum(v.shape[0] for v in u["vsegs"])
            for q0 in range(0, u["nq"], 512):
                for kc in range(nk // 128):
                    steps.append((ui, q0, kc, nk // 128))
        loaded = {}
        state = {}

        def emit_s(i):
            ui, q0, kc, nkc = steps[i]
            u = units[ui]
            if ui not in loaded:
                loaded[ui] = load_unit(u)
                loaded.pop(ui - 2, None)
            L = loaded[ui]
            qn = min(512, u["nq"] - q0)
            sps, spk = ctx.ps[sbank[0]], "ps%d" % sbank[0]
            sbank[0] ^= 1
            npieces = len(u["pieces"])
            for pi, (d, _, _) in enumerate(u["pieces"]):
                S.op("pe", lambda e, kt=L["ktiles"][pi], qt=L["qtiles"][pi], d=d, pi=pi:
                     e.matmul(sps[:, :qn], kt[:d, kc * 128:(kc + 1) * 128], qt[:d, q0:q0 + qn],
                              start=(pi == 0), stop=(pi == npieces - 1)),
                     reads=L["kkeys"][pi] + [L["qkeys"][pi]], writes=[spk])
            state[i] = (sps, spk, qn)

        def emit_rest(i):
            ui, q0, kc, nkc = steps[i]
            u = units[ui]
            L = loaded[ui]
            sps, spk, qn = state.pop(i)
            ndv = u["dv"] // 128
            if kc == 0:
                base = 2 + 3 * aset[0]
                aset[0] ^= 1
                state["acc"] = ([(ctx.ps[base + j], "ps%d" % (base + j)) for j in range(ndv)], (ctx.ps[base + 2], "ps%d" % (base + 2)))
            obanks, dbank = state["acc"]
            pt, pk_ = pr.next()
            S.op("act", lambda e: e.activation(pt[:, :qn], sps[:, :qn], AF.Exp, scale=scale), reads=[spk], writes=[pk_])
            vt = L["vt"]
            for j in range(ndv):
                S.op("pe", lambda e, j=j, ob=obanks[j][0]:
                     e.matmul(ob[:, :qn], vt[:, kc, j * 128:(j + 1) * 128], pt[:, :qn], start=(kc == 0), stop=(kc == nkc - 1)),
                     reads=L["vks"] + [pk_], writes=[obanks[j][1]])
            S.op("pe", lambda e, db=dbank[0]: e.matmul(db[:, :qn], ctx.ones_bf, pt[:, :qn], start=(kc == 0), stop=(kc == nkc - 1)),
                 reads=[pk_, "const"], writes=[dbank[1]])
            if kc == nkc - 1:
                rd, rdk = rdr.next()
                ot, otk = otr.next()
                S.op("dve", lambda e, db=dbank[0]: e.reciprocal(rd[:, :qn], db[:, :qn]), reads=[dbank[1]], writes=[rdk])
                for j in range(ndv):
                    S.op("dve", lambda e, j=j, ob=obanks[j][0]: e.tensor_tensor(ot[:, j, :qn], ob[:, :qn], rd[:, :qn], ALU.mult),
                         reads=[obanks[j][1], rdk], writes=[otk])
                S.dma("sp", u["out"][:, q0:q0 + qn].rearrange("(c p) t -> p c t", p=128), ot[:, :ndv, :qn], reads=[otk])

        n = len(steps)
        emit_s(0)
        for i in range(n):
            if i + 1 < n:
                emit_s(i + 1)
            emit_rest(i)
        ctx.flush()


def wo_phase(ctx, W, KC, src, k_gate=1):
    with contextlib.ExitStack() as st:
        epi = resid_epi(ctx, st, lambda ci, cond: ctx.G[:, k_gate, ci, cond:cond + 1])
        chunks = [(ci * 128, 128) for ci in range(DC)]
        tiles = [(0, T)] if KC <= 16 else [(0, 1280), (1280, 768), (2048, 512)]
        gemm(ctx, W, KC, simple_blocks(chunks, 512 if KC <= 16 else 256), fmv(src), tiles, epi, Wc=512 if KC <= 16 else 256)


def qkv_phase(ctx, W, kout, vout):
    with contextlib.ExitStack() as st:
        def dst(ci, t0, tn):
            r = [(ctx.qkvT[ci, :, t0:t0 + tn], BF16)]
            if ci >= 16 and t0 >= TS:
                o = kout if ci < 32 else vout
                r.append((o[(ci - 16) % 16, :, t0 - TS:t0 - TS + tn], F32))
            return r
        epi = store_epi(ctx, st, dst)
        chunks = [(ci * 128, 128) for ci in range(48)]
        gemm(ctx, W, DC, simple_blocks(chunks, 512), fmv(ctx.hT), [(0, T)], epi, Wc=512)


def diff_layer(ctx, layer):
    S = ctx.S
    nc = ctx.nc
    lam_init = 0.8 - 0.6 * float(np.exp(-0.3 * layer))
    rms_phase(ctx, ctx.xT, DC, D, ctx.hT,
              lambda c, cond: ctx.A[:, 1, c, cond:cond + 1], lambda c, cond: ctx.B[:, 1, c, cond:cond + 1])
    qkv_phase(ctx, ctx.w["diff_w_qkv"][0], ctx.out["new_diff_kT"], ctx.out["new_diff_vT"])
    rope_phase(ctx, [ctx.qkvT[ci, :, 0:TS] for ci in range(32)], ctx.c["rope128"], ctx.R128, 128)
    vtrans_phase(ctx, fmv(ctx.qkvT)[:, 32:48, :], 16, T, ctx.vtok)
    scale = 128 ** -0.5
    units = []
    ck = ctx.c["cache_diff_kT"]
    cv = ctx.c["cache_diff_v"]
    for u in range(16):
        h = u // 2
        units.append(dict(pieces=[(128, ctx.qkvT[u, :, 0:TS], [ctx.qkvT[16 + u, :, 0:TS], ck[u]])],
                          vsegs=[ctx.vtok[0:TS, h * 256:(h + 1) * 256], cv[:, h * 256:(h + 1) * 256]],
                          nq=TS, dv=256, out=ctx.oacc[u, :, 0:TS]))
    for p in range(2):
        a, b = TS + p * 256, TS + (p + 1) * 256
        for u in range(16):
            h = u // 2
            units.append(dict(pieces=[(128, ctx.qkvT[u, :, a:b], [ctx.qkvT[16 + u, :, a:b]])],
                              vsegs=[ctx.vtok[a:b, h * 256:(h + 1) * 256]],
                              nq=256, dv=256, out=ctx.oacc[u, :, a:b]))
    attention(ctx, units, scale, F32)
    with contextlib.ExitStack() as st:
        E = st.enter_context
        sm = E(nc.sbuf_tensor("dlsm_%d" % uid(), [128, 8], F32))
        tmp = E(nc.sbuf_tensor("dltmp_%d" % uid(), [128, 256], F32))
        dl = lambda i: pvap(ctx, "dlam", i * 128, 128)
        S.op("dve", lambda e: e.tensor_tensor(tmp[:, 0:128], dl(0), dl(1), ALU.mult), reads=["pv"], writes=["dltmp"])
        S.op("dve", lambda e: e.tensor_tensor(tmp[:, 128:256], dl(2), dl(3), ALU.mult), reads=["pv"], writes=["dltmp"])
        S.op("dve", lambda e: e.reduce_sum(sm[:, 0:1], tmp[:, 0:128], mybir.AxisListType.X), reads=["dltmp"], writes=["sm0"])
        S.op("dve", lambda e: e.reduce_sum(sm[:, 1:2], tmp[:, 128:256], mybir.AxisListType.X), reads=["dltmp"], writes=["sm1"])
        S.op("act", lambda e: e.activation(sm[:, 2:4], sm[:, 0:2], AF.Exp), reads=["sm0", "sm1"], writes=["sm2"])
        S.op("dve", lambda e: e.scalar_tensor_tensor(sm[:, 4:5], sm[:, 3:4], -lam_init, sm[:, 2:3], ALU.add, ALU.subtract),
             reads=["sm2"], writes=["sm4"])
        S.op("dve", lambda e: e.tensor_scalar(sm[:, 5:7], pvap(ctx, "subln", 0, 2), 1.0 - lam_init, None, ALU.mult),
             reads=["pv"], writes=["sm5"])
        o0r = Ring(ctx, st, "dlo0", [128, 2, 512], F32, 2)
        o1r = Ring(ctx, st, "dlo1", [128, 2, 512], F32, 2)
        sqr = Ring(ctx, st, "dlsq", [128, 2, 512], BF16, 2)
        rsr = Ring(ctx, st, "dlrs", [128, 512], F32, 2)
        obr = Ring(ctx, st, "dlob", [128, 2, 512], BF16, 2)
        for h in range(8):
            for t0 in range(0, T, 512):
                o0, k0 = o0r.next(); o1, k1 = o1r.next(); sq, sqk = sqr.next(); rs, rk = rsr.next(); ob, obk = obr.next()
                S.dma("sp", o0[:], ctx.oacc[2 * h, :, t0:t0 + 512].rearrange("(c p) t -> p c t", p=128), writes=[k0])
                S.dma("sp", o1[:], ctx.oacc[2 * h + 1, :, t0:t0 + 512].rearrange("(c p) t -> p c t", p=128), writes=[k1])
                S.op("dve", lambda e, o0=o0, o1=o1: e.scalar_tensor_tensor(o0[:], o1[:], sm[:, 4:5], o0[:], ALU.mult, ALU.add),
                     reads=[k0, k1, "sm4"], writes=[k0])
                S.op("act", lambda e, sq=sq, o0=o0: e.activation(sq[:], o0[:], AF.Square), reads=[k0], writes=[sqk])
                ps, pk = psum_next(ctx, 6, 8)
                for c in range(2):
                    S.op("pe", lambda e, ps=ps, sq=sq, c=c: e.matmul(ps[:, :512], ctx.ones_bf, sq[:, c, :], start=(c == 0), stop=(c == 1)),
                         reads=[sqk, "const"], writes=[pk])
                S.op("act", lambda e, rs=rs, ps=ps: e.activation(rs[:], ps[:, :512], AF.Sqrt, bias=ctx.eps_ap, scale=1.0 / 256),
                     reads=[pk, "const"], writes=[rk])
                S.op("dve", lambda e, rs=rs: e.reciprocal(rs[:], rs[:]), reads=[rk], writes=[rk])
                for c in range(2):
                    S.op("dve", lambda e, ob=ob, o0=o0, rs=rs, c=c:
                         e.scalar_tensor_tensor(ob[:, c, :], o0[:, c, :], sm[:, 5 + c:6 + c], rs[:], ALU.mult, ALU.mult),
                         reads=[k0, rk, "sm5"], writes=[obk])
                S.dma("sp", fmv(ctx.oT)[:, 2 * h:2 * h + 2, t0:t0 + 512], ob[:], reads=[obk])
        ctx.flush()
    wo_phase(ctx, ctx.w["diff_w_o"][0], DC, ctx.oT)


def na_classes():
    out = []
    for r in range(32):
        rs = min(max(r - 4, 0), 24)
        cls = r if r < 4 else (4 if r < 28 else 5 + (r - 28))
        out.append((rs, cls))
    return out


def na_host_tables(rpb):
    kc = np.arange(64)[:, None]
    qc = np.arange(64)[None, :]
    dc = np.clip(kc - qc + 15, 0, 30)
    cstart = np.clip(qc - 8, 0, 48)
    ok = (kc >= cstart) & (kc < cstart + 16)
    bias = np.zeros((16, 128, 9, 4, 64), np.float32)
    mask = np.zeros((128, 9, 4, 64), np.float32)
    cls_r = {}
    for r, (rs, cls) in enumerate(na_classes()):
        cls_r[cls] = (r, rs)
    for cls, (r, rs) in cls_r.items():
        for j in range(4):
            for w2 in range(2):
                wr = 2 * j + w2
                ridx = rs + wr - r + 7
                bias[:, w2 * 64:(w2 + 1) * 64, cls, j, :] = rpb[:, ridx][:, dc]
                mask[w2 * 64:(w2 + 1) * 64, cls, j, :] = ok
    return bias.reshape(16, 128, 2304), mask.reshape(128, 2304).astype(ml_dtypes.bfloat16)


def na_layer(ctx, layer):
    S = ctx.S
    nc = ctx.nc
    rms_phase(ctx, ctx.xT, DC, D, ctx.hT,
              lambda c, cond: ctx.A[:, 1, c, cond:cond + 1], lambda c, cond: ctx.B[:, 1, c, cond:cond + 1])
    qkv_phase(ctx, ctx.w["na_w_qkv"][0], ctx.out["new_na_kT"], ctx.out["new_na_vT"])
    vtrans_phase(ctx, fmv(ctx.qkvT)[:, 32:48, :], 16, T, ctx.vtok)
    scale = 128 ** -0.5
    units = []
    for p in range(2):
        a, b = TS + p * 256, TS + (p + 1) * 256
        for h in range(16):
            units.append(dict(pieces=[(128, ctx.qkvT[h, :, a:b], [ctx.qkvT[16 + h, :, a:b]])],
                              vsegs=[ctx.vtok[a:b, h * 128:(h + 1) * 128]],
                              nq=256, dv=128, out=ctx.oT[h, :, a:b]))
    attention(ctx, units, scale, BF16)
    rows = na_classes()
    with contextlib.ExitStack() as st:
        E = st.enter_context
        maskt = E(nc.sbuf_tensor("namask_%d" % uid(), [128, 2304], BF16))
        S.dma("sp", maskt[:], ctx.c["na_mask"], writes=["namask"])
        qr = Ring(ctx, st, "naq", [128, TS], BF16, 2)
        kr = Ring(ctx, st, "nak", [128, TS], BF16, 2)
        ver = Ring(ctx, st, "nave", [128, 16, 128], BF16, 2)
        vor = Ring(ctx, st, "navo", [128, 15, 128], BF16, 2)
        ckr = Ring(ctx, st, "nack", [128, 512], BF16, 2)
        cvr = Ring(ctx, st, "nacv", [128, 4, 128], BF16, 2)
        br = Ring(ctx, st, "nab", [128, 2304], F32, 1)
        er = Ring(ctx, st, "nae", [128, 2304], BF16, 2)
        pr = Ring(ctx, st, "nap", [128, 512], BF16, 4)
        rdr = Ring(ctx, st, "nard", [128, 512], F32, 2)
        otr = Ring(ctx, st, "nao", [128, TS], BF16, 2)
        sb = [0]
        aset = [0]
        heads = {}

        def load_head(h):
            qt, qk = qr.next(); kt, kk = kr.next(); ve, vek = ver.next(); vo, vok = vor.next()
            ck, ckk = ckr.next(); cv, cvk = cvr.next(); bt, bk = br.next(); et, ek = er.next(); ot, otk = otr.next()
            S.dma("sp", qt[:], ctx.qkvT[h, :, 0:TS], writes=[qk])
            S.dma("sp", kt[:], ctx.qkvT[16 + h, :, 0:TS], writes=[kk])
            S.dma("sp", ve[:], ctx.vtok[0:TS, h * 128:(h + 1) * 128].rearrange("(c p) d -> p c d", p=128), writes=[vek])
            S.dma("sp", vo[:], ctx.vtok[64:TS - 64, h * 128:(h + 1) * 128].rearrange("(c p) d -> p c d", p=128), writes=[vok])
            S.dma("pool", ck[:], ctx.c["cache_na_kT"][h], writes=[ckk])
            S.dma("pool", cv[:], ctx.c["cache_na_v"][:, h * 128:(h + 1) * 128].rearrange("(c p) d -> p c d", p=128), writes=[cvk])
            S.dma("sp", bt[:], ctx.c["na_bias"][h], writes=[bk])
            S.op("act", lambda e: e.activation(et[:], bt[:], AF.Exp), reads=[bk], writes=[ek])
            S.op("dve", lambda e: e.tensor_tensor(et[:], et[:], maskt[:], ALU.mult), reads=[ek, "namask"], writes=[ek])
            return dict(qt=qt, qk=qk, kt=kt, kk=kk, ve=ve, vek=vek, vo=vo, vok=vok, ck=ck, ckk=ckk, cv=cv, cvk=cvk, et=et, ek=ek, ot=ot, otk=otk)

        steps = [(h, r) for h in range(16) for r in range(32)]
        pend = {}

        def emit_s(i):
            h, r = steps[i]
            if h not in heads:
                heads[h] = load_head(h)
                heads.pop(h - 2, None)
            H = heads[h]
            rs, cls = rows[r]
            sps, spk = ctx.ps[sb[0]], "ps%d" % sb[0]
            sb[0] ^= 1
            qs = H["qt"][:, r * 64:(r + 1) * 64]
            vch = []
            for j in range(4):
                k0 = (rs + 2 * j) * 64
                S.op("pe", lambda e, k0=k0, j=j: e.matmul(sps[:, j * 64:(j + 1) * 64], H["kt"][:, k0:k0 + 128], qs, start=True, stop=True),
                     reads=[H["kk"], H["qk"]], writes=[spk])
                rr_ = rs + 2 * j
                vch.append((H["ve"][:, rr_ // 2, :], H["vek"]) if rr_ % 2 == 0 else (H["vo"][:, (rr_ - 1) // 2, :], H["vok"]))
            for j in range(4):
                S.op("pe", lambda e, j=j: e.matmul(sps[:, (4 + j) * 64:(5 + j) * 64], H["ck"][:, j * 128:(j + 1) * 128], qs, start=True, stop=True),
                     reads=[H["ckk"], H["qk"]], writes=[spk])
                vch.append((H["cv"][:, j, :], H["cvk"]))
            pend[i] = (sps, spk, vch)

        def emit_rest(i):
            h, r = steps[i]
            H = heads[h]
            rs, cls = rows[r]
            sps, spk, vch = pend.pop(i)
            q8, r8 = r // 8, r % 8
            if r8 == 0:
                base = 2 + 2 * aset[0]
                aset[0] ^= 1
                pend["acc"] = ((ctx.ps[base], "ps%d" % base), (ctx.ps[base + 1], "ps%d" % (base + 1)))
            (ob, obk), (db, dbk) = pend["acc"]
            pt, pk_ = pr.next()
            et, ek = H["et"], H["ek"]
            S.op("act", lambda e: e.activation(pt[:], sps[:], AF.Exp, scale=scale), reads=[spk], writes=[pk_])
            S.op("dve", lambda e: e.tensor_tensor(pt[:, 0:256], pt[:, 0:256], et[:, cls * 256:(cls + 1) * 256], ALU.mult),
                 reads=[pk_, ek], writes=[pk_])
            for j in range(8):
                S.op("pe", lambda e, v=vch[j][0], j=j: e.matmul(ob[:, r8 * 64:(r8 + 1) * 64], v, pt[:, j * 64:(j + 1) * 64], start=(j == 0), stop=(j == 7)),
                     reads=[vch[j][1], pk_], writes=[obk])
            for j in range(8):
                S.op("pe", lambda e, j=j: e.matmul(db[:, r8 * 64:(r8 + 1) * 64], ctx.ones_bf, pt[:, j * 64:(j + 1) * 64], start=(j == 0), stop=(j == 7)),
                     reads=[pk_, "const"], writes=[dbk])
            if r8 == 7:
                ot, otk = H["ot"], H["otk"]
                rd, rdk = rdr.next()
                S.op("dve", lambda e: e.reciprocal(rd[:], db[:]), reads=[dbk], writes=[rdk])
                S.op("dve", lambda e: e.tensor_tensor(ot[:, q8 * 512:(q8 + 1) * 512], ob[:], rd[:], ALU.mult),
                     reads=[obk, rdk], writes=[otk])
                if r == 31:
                    S.dma("sp", ctx.oT[h, :, 0:TS], ot[:], reads=[otk])

        n = len(steps)
        emit_s(0)
        for i in range(n):
            if i + 1 < n:
                emit_s(i + 1)
            emit_rest(i)
        ctx.flush()
    wo_phase(ctx, ctx.w["na_w_o"][0], DC, ctx.oT)


def mla_layer(ctx, layer):
    S = ctx.S
    nc = ctx.nc
    TK = T + 512
    rms_phase(ctx, ctx.xT, DC, D, ctx.hT,
              lambda c, cond: ctx.A[:, 1, c, cond:cond + 1], lambda c, cond: ctx.B[:, 1, c, cond:cond + 1])
    with contextlib.ExitStack() as st:
        epi = store_epi(ctx, st, lambda ci, t0, tn: [(ctx.cqT[ci, :, t0:t0 + tn], F32)])
        gemm(ctx, ctx.w["mla_w_dq"][0], DC, simple_blocks([(c * 128, 128) for c in range(4)], 512), fmv(ctx.hT), [(0, T)], epi, Wc=512)
    rms_phase(ctx, ctx.cqT, 4, 512, ctx.cqnT, lambda c, cond: pvap(ctx, "q_norm", c), None)
    with contextlib.ExitStack() as st:
        def dstq(ci, t0, tn):
            h = ci // 2
            if ci % 2 == 0:
                return [(ctx.mlaq[h, 0:128, t0:t0 + tn], BF16)]
            return [(ctx.mlaq[h, 128:192, t0:t0 + tn], BF16)]
        epi = store_epi(ctx, st, dstq)
        chunks = []
        for h in range(16):
            chunks += [(h * 192, 128), (h * 192 + 128, 64)]
        gemm(ctx, ctx.w["mla_w_uq"][0], 4, simple_blocks(chunks, 384), fmv(ctx.cqnT), [(0, T)], epi, Wc=384)
    with contextlib.ExitStack() as st:
        def dstkv(ci, t0, tn):
            if ci < 2:
                return [(ctx.ckvraw[ci, :, t0:t0 + tn], F32)]
            r = [(ctx.krT[0:64, t0:t0 + tn], BF16)]
            if t0 >= TS:
                r.append((ctx.out["new_mla_kropeT"][0:64, t0 - TS:t0 - TS + tn], F32))
            return r
        epi = store_epi(ctx, st, dstkv)
        gemm(ctx, ctx.w["mla_w_dkv"][0], DC, simple_blocks([(0, 128), (128, 128), (256, 64)], 320), fmv(ctx.hT), [(0, T)], epi, Wc=320)
    rms_phase(ctx, ctx.ckvraw, 2, 256, ctx.ckvT, lambda c, cond: pvap(ctx, "kv_norm", c), None)
    rms_phase(ctx, ctx.ckvraw, 2, 256, ctx.out["new_mla_ckvT"], lambda c, cond: pvap(ctx, "kv_norm", c), None,
              tok0=TS, tok1=T, out_f32=True, dst_off=TS)
    with contextlib.ExitStack() as st:
        t1 = st.enter_context(nc.sbuf_tensor("mlac1_%d" % uid(), [128, 2, 512], BF16))
        t2 = st.enter_context(nc.sbuf_tensor("mlac2_%d" % uid(), [64, 512], BF16))
        S.dma("pool", t1[:], fmv(ctx.c["cache_mla_ckvT"]), writes=["mc1"])
        S.dma("pool", t2[:], ctx.c["cache_mla_kropeT"], writes=["mc2"])
        S.dma("sp", fmv(ctx.ckvT)[:, :, T:TK], t1[:], reads=["mc1"])
        S.dma("sp", ctx.krT[0:64, T:TK], t2[:], reads=["mc2"])
        ctx.flush()
    rope_phase(ctx, [ctx.mlaq[h, 128:192, 0:TS] for h in range(16)] + [ctx.krT[0:64, 0:TS]], ctx.c["rope64"], ctx.R64, 64)
    with contextlib.ExitStack() as st:
        def dstx(ci, t0, tn):
            h = ci // 2
            return [(ctx.mlakv[h if ci % 2 == 0 else 16 + h, :, t0:t0 + tn], BF16)]
        epi = store_epi(ctx, st, dstx)
        chunks = [(c * 128, 128) for c in range(32)]
        gemm(ctx, ctx.w["mla_w_ukv"][0], 2, simple_blocks(chunks, 512), fmv(ctx.ckvT), [(0, TK)], epi, Wc=512)
    vtrans_phase(ctx, fmv(ctx.mlakv)[:, 16:32, :], 16, TK, ctx.vtok)
    scale = 192 ** -0.5
    units = []
    for h in range(16):
        units.append(dict(pieces=[(128, ctx.mlaq[h, 0:128, 0:TS], [ctx.mlakv[h, :, 0:TS], ctx.mlakv[h, :, T:TK]]),
                                  (64, ctx.mlaq[h, 128:192, 0:TS], [ctx.krT[0:64, 0:TS], ctx.krT[0:64, T:TK]])],
                          vsegs=[ctx.vtok[0:TS, h * 128:(h + 1) * 128], ctx.vtok[T:TK, h * 128:(h + 1) * 128]],
                          nq=TS, dv=128, out=ctx.oT[h, :, 0:TS]))
    for p in range(2):
        a, b = TS + p * 256, TS + (p + 1) * 256
        for h in range(16):
            units.append(dict(pieces=[(128, ctx.mlaq[h, 0:128, a:b], [ctx.mlakv[h, :, a:b]]),
                                      (64, ctx.mlaq[h, 128:192, a:b], [ctx.krT[0:64, a:b]])],
                              vsegs=[ctx.vtok[a:b, h * 128:(h + 1) * 128]],
                              nq=256, dv=128, out=ctx.oT[h, :, a:b]))
    attention(ctx, units, scale, BF16)
    wo_phase(ctx, ctx.w["mla_w_o"][0], DC, ctx.oT)


SEQS = [(0, 2048), (2048, 256), (2304, 256)]
PADOFF = [3, 2054, 2313]
TP = T + 12


def ssd_layer(ctx, layer):
    S = ctx.S
    nc = ctx.nc
    rms_phase(ctx, ctx.xT, DC, D, ctx.hT,
              lambda c, cond: ctx.A[:, 1, c, cond:cond + 1], lambda c, cond: ctx.B[:, 1, c, cond:cond + 1])
    with contextlib.ExitStack() as st:
        rb = Ring(ctx, st, "sib", [128, 512], BF16, 3)
        rb2 = Ring(ctx, st, "sib2", [128, 512], BF16, 3)
        rf = Ring(ctx, st, "sif", [128, 512], F32, 2)

        def epi(ci, w, t0, tn, ps, pk):
            if ci < 32:
                tl, tk = rb.next()
                S.op("act", lambda e: e.activation(tl[:, :tn], ps[:, :tn], AF.Silu), reads=[pk], writes=[tk])
                S.dma("sp", ctx.szT[ci, :, t0:t0 + tn], tl[:, :tn], reads=[tk])
            elif ci < 80:
                tl, tk = rb2.next()
                S.op("dve", lambda e: e.tensor_copy(tl[:, :tn], ps[:, :tn]), reads=[pk], writes=[tk])
                S.dma("sp", ctx.xbcraw[ci - 32, :, t0:t0 + tn], tl[:, :tn], reads=[tk])
            else:
                tl, tk = rf.next()
                S.op("dve", lambda e: e.tensor_copy(tl[:, :tn], ps[:, :tn]), reads=[pk], writes=[tk])
                S.dma("sp", ctx.dtraw[:, t0:t0 + tn], tl[:, :tn], reads=[tk])
        chunks = [(c * 128, 128) for c in range(81)]
        gemm(ctx, ctx.w["ssd_w_in"][0], DC, simple_blocks(chunks, 512), fmv(ctx.hT), [(0, T)], epi, Wc=512)
    with contextlib.ExitStack() as st:
        rr = Ring(ctx, st, "cvr", [128, TP], BF16, 3)
        orr = Ring(ctx, st, "cvo", [128, TP], BF16, 3)
        dgr = Ring(ctx, st, "cvd", [128, 7, 128], BF16, 2)
        for i in range(3):
            S.op("dve", lambda e, t=rr.tiles[i]: e.memset(t[:], 0.0), writes=[rr.keys[i]] + [rr.keys[i] + "s%d" % j for j in range(3)])
        R0, R1 = 3, TP - 3
        for c in range(48):
            rt, rk = rr.next(); ot, ok = orr.next(); dg, dgk = dgr.next()
            lk = []
            for si, (a, n) in enumerate(SEQS):
                S.dma("sp", rt[:, PADOFF[si]:PADOFF[si] + n], ctx.xbcraw[c, :, a:a + n], writes=[rk + "s%d" % si])
                lk.append(rk + "s%d" % si)
            for w in range(7):
                S.op("dve", lambda e, dg=dg, w=w, c=c: e.tensor_scalar(dg[:, w, :], ctx.ident_bf, pvap(ctx, "conv_w", c * 7 + w), None, ALU.mult),
                     reads=["const", "pv"], writes=[dgk])
            for s0 in range(R0, R1, 512):
                sn = min(512, R1 - s0)
                ps, pk = psum_next(ctx, 0, 8)
                for w in range(7):
                    S.op("pe", lambda e, ps=ps, dg=dg, rt=rt, w=w, s0=s0, sn=sn:
                         e.matmul(ps[:, :sn], dg[:, w, :], rt[:, s0 + w - 3:s0 + w - 3 + sn], start=(w == 0), stop=(w == 6)),
                         reads=lk + [rk, dgk], writes=[pk])
                S.op("act", lambda e, ot=ot, ps=ps, c=c, s0=s0, sn=sn:
                     e.activation(ot[:, s0:s0 + sn], ps[:, :sn], AF.Silu, bias=pvap(ctx, "conv_b", c)),
                     reads=[pk, "pv"], writes=[ok])
            for si, (a, n) in enumerate(SEQS):
                dst = ctx.xcT[c, :, a:a + n] if c < 32 else ctx.bcT[c - 32, :, a:a + n]
                S.dma("sp", dst, ot[:, PADOFF[si]:PADOFF[si] + n], reads=[ok])
        ctx.flush()
    with contextlib.ExitStack() as st:
        E = st.enter_context
        raw = E(nc.sbuf_tensor("dtr_%d" % uid(), [128, T], F32))
        sp = E(nc.sbuf_tensor("dts_%d" % uid(), [128, T], F32))
        da = E(nc.sbuf_tensor("dta_%d" % uid(), [128, T], F32))
        ea = E(nc.sbuf_tensor("dte_%d" % uid(), [128, 2], F32))
        tk_ = E(nc.sbuf_tensor("dtt_%d" % uid(), [128, 2, 20, 128], F32))
        S.dma("sp", raw[:], ctx.dtraw, writes=["dtr"])
        S.op("act", lambda e: e.activation(raw[:], raw[:], AF.Exp, bias=pvap(ctx, "dt_bias", 0)), reads=["dtr", "pv"], writes=["dtr"])
        S.op("act", lambda e: e.activation(sp[:], raw[:], AF.Ln, bias=ctx.one_ap), reads=["dtr", "const"], writes=["dts"])
        S.op("act", lambda e: e.activation(ea[:, 0:1], pvap(ctx, "a_log", 0), AF.Exp), reads=["pv"], writes=["dte"])
        S.op("dve", lambda e: e.tensor_scalar(da[:], sp[:], ea[:, 0:1], -1.0, ALU.mult, ALU.mult), reads=["dts", "dte"], writes=["dta"])
        for which, (src, sk) in enumerate(((sp, "dts"), (da, "dta"))):
            for c4 in range(0, 20, 4):
                ps, pk = psum_next(ctx, 0, 8)
                for c in range(c4, c4 + 4):
                    S.op("pe", lambda e, ps=ps, src=src, c=c, c4=c4:
                         e.transpose(ps[:, (c - c4) * 128:(c - c4 + 1) * 128], src[:, c * 128:(c + 1) * 128], ctx.ident_f32),
                         reads=[sk, "constf"], writes=[pk])
                S.op("dve", lambda e, ps=ps, which=which, c4=c4:
                     e.tensor_copy(tk_[:, which, c4:c4 + 4, :].rearrange("p c j -> p (c j)"), ps[:, :512]), reads=[pk], writes=["dtt"])
        S.dma("sp", ctx.dttok, tk_[:], reads=["dtt"])
        ctx.flush()
    vtrans_phase(ctx, fmv(ctx.xcT)[:, 0:16, :], 16, T, ctx.xtok, col_off=0)
    vtrans_phase(ctx, fmv(ctx.xcT)[:, 16:32, :], 16, T, ctx.xtok, col_off=2048)
    vtrans_phase(ctx, fmv(ctx.bcT)[:, 0:8, :], 8, T, ctx.btok)
    ssd_scan(ctx)
    with contextlib.ExitStack() as st:
        TN = 256
        yr = Ring(ctx, st, "gny", [128, 32, TN], BF16, 2)
        zr = Ring(ctx, st, "gnz", [128, 32, TN], BF16, 2)
        gr = Ring(ctx, st, "gng", [128, 32, TN], F32, 1)
        sr = Ring(ctx, st, "gns", [128, 32, TN], BF16, 1)
        orr = Ring(ctx, st, "gno", [128, 32, TN], BF16, 2)
        rsr = Ring(ctx, st, "gnr", [128, TN], F32, 2)
        for t0 in range(0, T, TN):
            yt, yk = yr.next(); zt, zk = zr.next(); gt, gk = gr.next(); sq, sqk = sr.next(); ot, ok = orr.next(); rs, rk = rsr.next()
            S.dma("sp", yt[:], fmv(ctx.yT)[:, :, t0:t0 + TN], writes=[yk])
            S.dma("sp", zt[:], fmv(ctx.szT)[:, :, t0:t0 + TN], writes=[zk])
            S.op("dve", lambda e, gt=gt, yt=yt, zt=zt: e.tensor_tensor(gt[:], yt[:], zt[:], ALU.mult), reads=[yk, zk], writes=[gk])
            S.op("act", lambda e, sq=sq, gt=gt: e.activation(sq[:], gt[:], AF.Square), reads=[gk], writes=[sqk])
            ps, pk = psum_next(ctx, 6, 8)
            for c in range(32):
                S.op("pe", lambda e, ps=ps, sq=sq, c=c: e.matmul(ps[:, :TN], ctx.ones_bf, sq[:, c, :], start=(c == 0), stop=(c == 31)),
                     reads=[sqk, "const"], writes=[pk])
            S.op("act", lambda e, rs=rs, ps=ps: e.activation(rs[:], ps[:, :TN], AF.Sqrt, bias=ctx.eps_ap, scale=1.0 / 4096),
                 reads=[pk, "const"], writes=[rk])
            S.op("dve", lambda e, rs=rs: e.reciprocal(rs[:], rs[:]), reads=[rk], writes=[rk])
            for c in range(32):
                eng = "dve"
                S.op(eng, lambda e, ot=ot, gt=gt, rs=rs, c=c:
                     e.scalar_tensor_tensor(ot[:, c, :], gt[:, c, :], pvap(ctx, "ssd_nw", c), rs[:], ALU.mult, ALU.mult),
                     reads=[gk, rk, "pv"], writes=[ok + eng])
            S.dma("sp", fmv(ctx.ygT)[:, :, t0:t0 + TN], ot[:], reads=[ok + "dve"])
        ctx.flush()
    wo_phase(ctx, ctx.w["ssd_w_out"][0], 32, ctx.ygT)


def ssd_scan(ctx):
    S = ctx.S
    nc = ctx.nc
    with contextlib.ExitStack() as st:
        E = st.enter_context
        cst = E(nc.sbuf_tensor("sscst_%d" % uid(), [128, 640], F32))
        S.dma("sp", cst[:], ctx.c["ssdc"][:, 0:640], writes=["sscst"])
        U = [cst[:, 0:128], cst[:, 128:256]]
        Tm = [cst[:, 256:384], cst[:, 384:512]]
        dtt = E(nc.sbuf_tensor("ssdtt_%d" % uid(), [128, 2, 20, 128], F32))
        S.dma("sp", dtt[:], ctx.dttok, writes=["ssdtt"])
        BT = E(nc.sbuf_tensor("ssBT_%d" % uid(), [128, T], BF16))
        CT = E(nc.sbuf_tensor("ssCT_%d" % uid(), [128, T], BF16))
        Bk = E(nc.sbuf_tensor("ssBk_%d" % uid(), [128, 20, 128], BF16))
        Xk = E(nc.sbuf_tensor("ssXk_%d" % uid(), [128, 20, 512], BF16))
        ybuf = E(nc.sbuf_tensor("ssyb_%d" % uid(), [64, 8, T], BF16))
        idsk = E(nc.sbuf_tensor("ssid_%d" % uid(), [128, 8, 128], BF16))
        hT = E(nc.sbuf_tensor("sshT_%d" % uid(), [128, 2, 8, 64], F32))
        hTb = E(nc.sbuf_tensor("sshTb_%d" % uid(), [128, 2, 8, 64], BF16))
        aTr = Ring(ctx, st, "ssaT", [128, 8, 128], F32, 3)
        Lmr = Ring(ctx, st, "ssLm", [128, 8, 128], F32, 3)
        Edr = Ring(ctx, st, "ssEd", [128, 8, 128], F32, 3)
        cbr = Ring(ctx, st, "sscb", [128, 1, 128], F32, 3)
        MTr = Ring(ctx, st, "ssMT", [128, 128], BF16, 10)
        Cdr = Ring(ctx, st, "ssCd", [128, 128], BF16, 10)
        Bwr = Ring(ctx, st, "ssBw", [128, 128], BF16, 10)
        w8r = Ring(ctx, st, "ssw8", [128, 8], F32, 3)
        PS = lambda i: (ctx.ps[i], "ps%d" % i)

        def group_setup(g):
            S.dma("sp", BT[:], ctx.bcT[g], writes=["ssBT"])
            S.dma("sp", CT[:], ctx.bcT[8 + g], writes=["ssCT"])
            S.dma("sp", Bk[:], ctx.btok[:, g * 128:(g + 1) * 128].rearrange("(c p) n -> p c n", p=128), writes=["ssBk"])
            S.dma("sp", Xk[:], ctx.xtok[:, g * 512:(g + 1) * 512].rearrange("(c p) n -> p c n", p=128), writes=["ssXk"])
            for hh in range(8):
                S.op("dve", lambda e, hh=hh: e.tensor_scalar(idsk[:, hh, :], ctx.ident_bf, pvap(ctx, "ssd_d", g * 8 + hh), None, ALU.mult),
                     reads=["const", "pv"], writes=["ssid"])

        def stage1(it):
            g, si, dr, c = it["g"], it["si"], it["dr"], it["c"]
            tsl = slice(c * 128, (c + 1) * 128)
            cps, cpk = PS(2)
            cb, cbk = cbr.next()
            S.op("pe", lambda e: e.matmul(cps[:, 0:128], BT[:, tsl], CT[:, tsl], start=True, stop=True),
                 reads=["ssBT", "ssCT"], writes=[cpk])
            S.op("dve", lambda e: e.tensor_tensor(cb[:, 0, :], cps[:, 0:128], Tm[dr], ALU.mult), reads=[cpk, "sscst"], writes=[cbk])
            aT, aTk = aTr.next()
            for hh in range(8):
                col = dr * 64 + g * 8 + hh
                S.op("act", lambda e, hh=hh, col=col: e.activation(aT[:, hh, :], Tm[dr], AF.Identity, scale=dtt[:, 1, c, col:col + 1]),
                     reads=["sscst", "ssdtt"], writes=[aTk])
            Lm, Lmk = Lmr.next()
            Ed, Edk = Edr.next()
            for half in range(2):
                sps, spk = PS(0 if half == 0 else 7)
                aps, apk = PS(1 if half == 0 else 0)
                rhs = aT[:, half * 4:(half + 1) * 4, :].rearrange("p h q -> p (h q)")
                S.op("pe", lambda e, sps=sps, rhs=rhs: e.matmul(sps[:, :512], U[dr], rhs, start=True, stop=True),
                     reads=[aTk, "sscst"], writes=[spk])
                S.op("pe", lambda e, aps=aps, rhs=rhs: e.matmul(aps[:, :512], cst[:, 512:640], rhs, start=True, stop=True),
                     reads=[aTk, "sscst"], writes=[apk])
                S.op("act", lambda e, sps=sps, half=half:
                     e.activation(Lm[:, half * 4:(half + 1) * 4, :].rearrange("p h q -> p (h q)"), sps[:, :512], AF.Exp),
                     reads=[spk], writes=[Lmk])
                S.op("act", lambda e, aps=aps, half=half:
                     e.activation(Ed[:, half * 4:(half + 1) * 4, :].rearrange("p h q -> p (h q)"), aps[:, :512], AF.Exp),
                     reads=[apk], writes=[Edk])
            return dict(cb=cb, cbk=cbk, Lm=Lm, Lmk=Lmk, Ed=Ed, Edk=Edk)

        ykeys = []
        sbank_ = [0]

        def stage2(it, s1):
            g, si, dr, c = it["g"], it["si"], it["dr"], it["c"]
            tsl = slice(c * 128, (c + 1) * 128)
            cb, cbk, Lm, Lmk, Ed, Edk = s1["cb"], s1["cbk"], s1["Lm"], s1["Lmk"], s1["Ed"], s1["Edk"]
            qe = 127 if dr == 0 else 0
            hk, hbk = "sshT%d" % dr, "sshTb%d" % dr
            if it["seq_start"]:
                if si == 0:
                    for d_ in range(2):
                        S.dma("sp", hT[:, d_, :, :], ctx.c["ssd_stateT"][d_, g * 8:(g + 1) * 8].rearrange("h n p -> n h p"),
                              writes=["sshT%d" % d_])
                    S.op("act", lambda e: e.activation(hTb[:].rearrange("p d h q -> p (d h q)"), hT[:].rearrange("p d h q -> p (d h q)"), AF.Identity),
                         reads=["sshT0", "sshT1"], writes=["sshTb0", "sshTb1"])
                else:
                    S.op("dve", lambda e: e.memset(hT[:], 0.0), writes=["sshT0", "sshT1"])
                    S.op("dve", lambda e: e.memset(hTb[:], 0.0), writes=["sshTb0", "sshTb1"])
            yb = [PS(5), PS(6)]
            sbank_[0] ^= 1
            sps7, spk7 = PS(3 + sbank_[0])
            w8, w8k = w8r.next()
            c0_ = dr * 64 + g * 8
            S.op("dve", lambda e: e.tensor_tensor(w8[:], Lm[:, :, qe], dtt[:, 0, c, c0_:c0_ + 8], ALU.mult),
                 reads=[Lmk, "ssdtt"], writes=[w8k])
            per_head = []
            for hh in range(8):
                col = dr * 64 + g * 8 + hh
                dcol = dtt[:, 0, c, col:col + 1]
                MT, MTk = MTr.next()
                Cd, Cdk = Cdr.next()
                Bw, Bwk = Bwr.next()
                S.op("dve", lambda e, MT=MT, hh=hh, dcol=dcol:
                     e.scalar_tensor_tensor(MT[:], Lm[:, hh, :], dcol, cb[:, 0, :], ALU.mult, ALU.mult),
                     reads=[Lmk, cbk, "ssdtt"], writes=[MTk])
                S.op("act", lambda e, Bw=Bw, hh=hh: e.activation(Bw[:], Bk[:, c, :], AF.Identity, scale=w8[:, hh:hh + 1]),
                     reads=["ssBk", w8k], writes=[Bwk])
                per_head.append((MT, MTk, Cd, Cdk, Bw, Bwk))
            for hh in range(8):
                Cd, Cdk = per_head[hh][2], per_head[hh][3]
                S.op("dve", lambda e, Cd=Cd, hh=hh: e.tensor_tensor(Cd[:], CT[:, tsl], Ed[:, hh, :], ALU.mult),
                     reads=["ssCT", Edk], writes=[Cdk])
            for hh in range(8):
                MT, MTk, Cd, Cdk, Bw, Bwk = per_head[hh]
                ybk = yb[hh // 4]
                yreg = ybk[0][0:64, (hh % 4) * 128:(hh % 4 + 1) * 128]
                xs = Xk[:, c, hh * 64:(hh + 1) * 64]
                S.op("pe", lambda e, Bw=Bw, xs=xs, hh=hh: e.matmul(sps7[:, hh * 64:(hh + 1) * 64], Bw[:], xs, start=True, stop=True),
                     reads=[Bwk, "ssXk"], writes=[spk7])
                S.op("pe", lambda e, yreg=yreg, xs=xs, MT=MT: e.matmul(yreg, xs, MT[:], start=True, stop=False),
                     reads=["ssXk", MTk], writes=[ybk[1]])
                if dr == 0:
                    S.op("pe", lambda e, yreg=yreg, xs=xs, hh=hh: e.matmul(yreg, xs, idsk[:, hh, :], start=False, stop=False),
                         reads=["ssXk", "ssid"], writes=[ybk[1]])
                S.op("pe", lambda e, yreg=yreg, hh=hh, Cd=Cd: e.matmul(yreg, hTb[:, dr, hh, :], Cd[:], start=False, stop=True),
                     reads=[hbk, Cdk], writes=[ybk[1]])
            for half in range(2):
                yk = "ssyb%d_%d" % (c, half)
                ydst = ybuf[:, half * 4:(half + 1) * 4, tsl]
                ysrc = yb[half][0][0:64, :].rearrange("p (h q) -> p h q", h=4)
                if dr == 0:
                    S.op("act", lambda e, ydst=ydst, ysrc=ysrc: e.activation(ydst, ysrc, AF.Identity), reads=[yb[half][1]], writes=[yk])
                else:
                    S.op("dve", lambda e, ydst=ydst, ysrc=ysrc: e.tensor_tensor(ydst, ysrc, ydst, ALU.add), reads=[yb[half][1], yk], writes=[yk])
                    ykeys.append(yk)
            s1["sps7"], s1["spk7"] = sps7, spk7

        def stage2b(it, s1):
            g, si, dr, c = it["g"], it["si"], it["dr"], it["c"]
            Ed, Edk, sps7, spk7 = s1["Ed"], s1["Edk"], s1["sps7"], s1["spk7"]
            qe = 127 if dr == 0 else 0
            hk, hbk = "sshT%d" % dr, "sshTb%d" % dr
            for hh in range(8):
                S.op("dve", lambda e, hh=hh:
                     e.scalar_tensor_tensor(hT[:, dr, hh, :], hT[:, dr, hh, :], Ed[:, hh, qe:qe + 1], sps7[:, hh * 64:(hh + 1) * 64],
                                            ALU.mult, ALU.add),
                     reads=[hk, Edk, spk7], writes=[hk])
            S.op("act", lambda e: e.activation(hTb[:, dr, :, :].rearrange("p h q -> p (h q)"),
                                               hT[:, dr, :, :].rearrange("p h q -> p (h q)"), AF.Identity),
                 reads=[hk], writes=[hbk])
            if it["sweep_end"] and si > 0:
                S.dma("sp", ctx.out["new_stateT"][si - 1, dr, g * 8:(g + 1) * 8].rearrange("h n p -> n h p"), hT[:, dr, :, :], reads=[hk])
            if it["group_end"]:
                for hh in range(8):
                    S.dma("sp", ctx.yT[g * 4 + hh // 2, (hh % 2) * 64:(hh % 2) * 64 + 64, :], ybuf[:, hh, :], reads=list(ykeys))
                del ykeys[:]

        iters = []
        for g in range(8):
            for si, (a0, n) in enumerate(SEQS):
                c_lo, c_hi = a0 // 128, (a0 + n) // 128
                for dr in range(2):
                    order = list(range(c_lo, c_hi)) if dr == 0 else list(range(c_hi - 1, c_lo - 1, -1))
                    for j, c in enumerate(order):
                        iters.append(dict(g=g, si=si, dr=dr, c=c, seq_start=(dr == 0 and j == 0), sweep_end=(j == len(order) - 1),
                                          group_start=(si == 0 and dr == 0 and j == 0), group_end=False))
            iters[-1]["group_end"] = True
        for g in range(8):
            its = [it for it in iters if it["g"] == g]
            group_setup(g)
            n = len(its)
            s1 = {0: stage1(its[0])}
            if n > 1:
                s1[1] = stage1(its[1])
            for i in range(n):
                stage2(its[i], s1[i])
                if i + 2 < n:
                    s1[i + 2] = stage1(its[i + 2])
                stage2b(its[i], s1.pop(i))
        ctx.flush()


def build(cfg):
    nc = bass.Bass("TRN2", target_bir_lowering=False)
    ctx = Ctx()
    ctx.nc = nc
    kinds = cfg["kinds"]

    def din(name, shape, dt=F32):
        return nc.dram_tensor(name, list(shape), dt, kind="ExternalInput").ap()

    def dout(name, shape, dt=F32):
        return nc.dram_tensor(name, list(shape), dt, kind="ExternalOutput").ap()

    def dscr(name, shape, dt):
        return nc.dram_tensor(name, list(shape), dt, kind="Internal").ap()

    xT_in = din("xT_in", [DC, 128, T])
    pvec = din("pvec", [128, cfg["npv"]])
    cbf = din("cbf", [128, NCBF], BF16)
    ctx.w = {}
    for name, shape in cfg["wshapes"].items():
        ctx.w[name] = din(name, shape)
    ctx.c = {}
    for name, shape in cfg["cshapes"].items():
        ctx.c[name] = din(name, shape, BF16 if cfg["cdtypes"][name] == "bfloat16" else F32)
    ctx.out = {}
    for name, shape in cfg["oshapes"].items():
        ctx.out[name] = dout(name, shape)
    yT = dout("yT", [DC, 128, T])
    ctx.xT = dscr("xT", [DC, 128, T], F32)
    ctx.hT = dscr("hT", [DC, 128, T], BF16)
    ctx.aT = dscr("aT", [FC, 128, T], BF16)
    ctx.condT = dscr("condT", [128, DC, 2], BF16)
    ctx.qkvT = dscr("qkvT", [48, 128, T], BF16)
    ctx.vtok = dscr("vtok", [T + 512, 2048], BF16)
    ctx.oacc = dscr("oacc", [16, 256, T], F32)
    ctx.oT = dscr("oT", [DC, 128, T], BF16)
    if 3 in kinds:
        ctx.szT = dscr("szT", [32, 128, T], BF16)
        ctx.xbcraw = dscr("xbcraw", [48, 128, T], BF16)
        ctx.dtraw = dscr("dtraw", [128, T], F32)
        ctx.xcT = dscr("xcT", [32, 128, T], BF16)
        ctx.bcT = dscr("bcT", [16, 128, T], BF16)
        ctx.dttok = dscr("dttok", [128, 2, 20, 128], F32)
        ctx.xtok = dscr("xtok", [T, 4096], BF16)
        ctx.btok = dscr("btok", [T, 1024], BF16)
        ctx.yT = dscr("yT_ssd", [32, 128, T], BF16)
        ctx.ygT = dscr("ygT", [32, 128, T], BF16)
    if 2 in kinds:
        ctx.cqT = dscr("cqT", [4, 128, T], F32)
        ctx.cqnT = dscr("cqnT", [4, 128, T], BF16)
        ctx.mlaq = dscr("mlaq", [16, 192, T], BF16)
        ctx.ckvraw = dscr("ckvraw", [2, 128, T], F32)
        ctx.ckvT = dscr("ckvT", [2, 128, T + 512], BF16)
        ctx.krT = dscr("krT", [64, T + 512], BF16)
        ctx.mlakv = dscr("mlakv", [32, 128, T + 512], BF16)
    ctx.pvcols = cfg["pvcols"]

    with contextlib.ExitStack() as st:
        E = st.enter_context
        ctx.S = Sched(nc, st)
        S = ctx.S
        S.max_flush = cfg.get("max_flush")
        ctx.flush = S.flush
        ctx.pvt = E(nc.sbuf_tensor("pvt", [128, cfg["npv"]], F32))
        ctx.cbt = E(nc.sbuf_tensor("cbt", [128, NCBF], BF16))
        ctx.ones_bf = ctx.cbt[:, 0:128]
        ctx.ident_bf = ctx.cbt[:, 128:256]
        ctx.R128 = ctx.cbt[:, 256:384]
        ctx.R64 = ctx.cbt[0:64, 384:448]
        ctx.modt = E(nc.sbuf_tensor("modt", [128, 144, 2], F32))
        ctx.A = E(nc.sbuf_tensor("modA", [128, 3, 16, 2], F32))
        ctx.B = E(nc.sbuf_tensor("modB", [128, 3, 16, 2], F32))
        ctx.G = E(nc.sbuf_tensor("modG", [128, 3, 16, 2], F32))
        ctx.epst = E(nc.sbuf_tensor("epst", [128, 2], F32))
        ctx.eps_ap = ctx.epst[:, 0:1]
        ctx.one_ap = ctx.epst[:, 1:2]
        S.op("dve", lambda e: e.memset(ctx.epst[:, 0:1], EPS), writes=["const"])
        S.op("dve", lambda e: e.memset(ctx.epst[:, 1:2], 1.0), writes=["const"])
        if 3 in kinds:
            ctx.idf = E(nc.sbuf_tensor("identf", [128, 128], F32))
            ctx.ident_f32 = ctx.idf[:]
            S.dma("sp", ctx.idf[:], ctx.c["ssdc"][:, 640:768], writes=["constf"])
        ctx.ps = [E(nc.psum_tensor("psb%d" % i, [128, 512], F32)) for i in range(8)]
        ctx.ps_i = 0

        S.dma("sp", ctx.pvt[:], pvec, writes=["pv"])
        S.dma("sp", ctx.cbt[:], cbf, writes=["const"])
        with contextlib.ExitStack() as st2:
            ct = st2.enter_context(nc.sbuf_tensor("condt", [128, 32], BF16))
            S.op("act", lambda e: e.activation(ct[:], pvap(ctx, "cond", 0, 32), AF.Silu), reads=["pv"], writes=["ct"])
            S.dma("sp", ctx.condT, ct[:].rearrange("p (c t) -> p c t", t=2), reads=["ct"])
            xr = Ring(ctx, st2, "px", [128, DC, 512], F32, 2)
            for t0 in range(0, T, 512):
                xt, xk = xr.next()
                S.dma("sp", xt[:], xT_in[:, :, t0:t0 + 512].rearrange("c p t -> p c t"), writes=[xk])
                S.dma("sp", ctx.xT[:, :, t0:t0 + 512].rearrange("c p t -> p c t"), xt[:], reads=[xk])
            S.flush()

        for layer, kind in enumerate(kinds):
            mod_phase(ctx, layer)
            ffn_phase(ctx, layer, 0)
            lidx = cfg.get("layer_index", list(range(len(kinds))))[layer]
            if kind == 0:
                diff_layer(ctx, lidx)
            elif kind == 1:
                na_layer(ctx, lidx)
            elif kind == 2:
                mla_layer(ctx, lidx)
            elif kind == 3:
                ssd_layer(ctx, lidx)
            ffn_phase(ctx, layer, 1)

        rms_phase(ctx, ctx.xT, DC, D, yT, lambda c, cond: pvap(ctx, "final_w", c), None, out_f32=True)
    return nc, ctx


NCBF = 448
WNAMES = ["ada_w", "ffn_w_gu", "ffn_w_down", "diff_w_qkv", "diff_w_o"]


def rope_tables(d):
    hd = d // 2
    hh = hd // 2
    t = np.arange(2048)
    tab = np.zeros((2, d, 2048), np.float32)
    R = np.zeros((d, d), np.float32)
    for p in range(d):
        half, i = p // hd, p % hd
        j, first = i % hh, i < hh
        inv = np.float32(10000.0) ** (-np.float32(j) / np.float32(hh))
        pos = (t // 64) if half == 0 else (t % 64)
        ang = pos.astype(np.float32) * np.float32(inv)
        tab[0, p] = np.cos(ang)
        tab[1, p] = -np.sin(ang) if first else np.sin(ang)
        partner = p + hh if first else p - hh
        R[partner, p] = 1.0
    return tab, R


_CONST_CACHE = {}


def host_consts():
    if "cb" not in _CONST_CACHE:
        cb = np.zeros((128, NCBF), dtype=np.float32)
        cb[:, 0:128] = 1.0
        cb[:, 128:256] = np.eye(128, dtype=np.float32)
        t128, R128 = rope_tables(128)
        t64, R64 = rope_tables(64)
        cb[:, 256:384] = R128
        cb[0:64, 384:448] = R64
        _CONST_CACHE["cb"] = cb.astype(ml_dtypes.bfloat16)
        _CONST_CACHE["rope128"] = t128
        _CONST_CACHE["rope64"] = t64
    return _CONST_CACHE


def ssd_consts():
    k = np.arange(128)[:, None]
    j = np.arange(128)[None, :]
    c = np.zeros((128, 768), np.float32)
    c[:, 0:128] = (k > j)
    c[:, 128:256] = (k < j)
    c[:, 256:384] = (k <= j)
    c[:, 384:512] = (k >= j)
    c[:, 512:640] = 1.0
    c[:, 640:768] = np.eye(128)
    return c


def make_inputs(inp, core, kinds):
    xs = inp["x_sample"][core]
    xp = inp["x_prompt"][2 * core:2 * core + 2].reshape(512, D)
    x = np.concatenate([xs, xp], 0)
    xT = np.ascontiguousarray(x.T).reshape(DC, 128, T)
    pv = pv_layout(inp, core, kinds)
    hc = host_consts()
    m = {"xT_in": xT, "pvec": pv.build(), "cbf": hc["cb"]}
    wn = ["ada_w", "ffn_w_gu", "ffn_w_down"]
    cs = {}
    outs = {}
    if 0 in kinds:
        wn += ["diff_w_qkv", "diff_w_o"]
        cs["rope128"] = hc["rope128"]
        ck = inp["cache_diff_k"][core, 0]
        cs["cache_diff_kT"] = np.ascontiguousarray(ck.transpose(1, 2, 3, 0)).reshape(16, 128, 512)
        cs["cache_diff_v"] = np.ascontiguousarray(inp["cache_diff_v"][core, 0].reshape(512, 2048))
        outs["new_diff_kT"] = (16, 128, 512)
        outs["new_diff_vT"] = (16, 128, 512)
    if 1 in kinds:
        wn += ["na_w_qkv", "na_w_o"]
        if "na_tabs" not in hc or hc.get("na_rpb_id") is not inp["na_rpb"]:
            hc["na_tabs"] = na_host_tables(inp["na_rpb"][0])
            hc["na_rpb_id"] = inp["na_rpb"]
        cs["na_bias"], cs["na_mask"] = hc["na_tabs"]
        ck = inp["cache_na_k"][core, 0]
        cs["cache_na_kT"] = np.ascontiguousarray(ck.transpose(1, 2, 0))
        cs["cache_na_v"] = np.ascontiguousarray(inp["cache_na_v"][core, 0].reshape(512, 2048))
        outs["new_na_kT"] = (16, 128, 512)
        outs["new_na_vT"] = (16, 128, 512)
    if 2 in kinds:
        wn += ["mla_w_dq", "mla_w_uq", "mla_w_dkv", "mla_w_ukv", "mla_w_o"]
        cs["rope64"] = hc["rope64"]
        cs["cache_mla_ckvT"] = np.ascontiguousarray(inp["cache_mla_ckv"][core, 0].T).reshape(2, 128, 512)
        cs["cache_mla_kropeT"] = np.ascontiguousarray(inp["cache_mla_krope"][core, 0].T)
        outs["new_mla_ckvT"] = (2, 128, 512)
        outs["new_mla_kropeT"] = (64, 512)
    if 3 in kinds:
        wn += ["ssd_w_in", "ssd_w_out"]
        cs["ssdc"] = ssd_consts()
        cs["ssd_stateT"] = np.ascontiguousarray(inp["state_ssd"][core, 0].transpose(0, 1, 3, 2))
        outs["new_stateT"] = (2, 2, 64, 128, 64)
    for n in wn:
        m[n] = inp[n]
    m.update(cs)
    cfg = dict(npv=pv.n, pvcols=pv.cols, kinds=kinds, wshapes={k: inp[k].shape for k in wn},
               cshapes={k: v.shape for k, v in cs.items()}, cdtypes={k: str(v.dtype) for k, v in cs.items()}, oshapes=outs)
    return m, cfg


def kernel(**inp):
    inp = {k: np.asarray(v) for k, v in inp.items()}
    n = 8
    kinds = [0, 1, 2, 3]
    maps = []
    cfg = None
    for core in range(n):
        m, cfg = make_inputs(inp, core, kinds)
        maps.append(m)
    nc, ctx = build(cfg)
    res = run_bass_kernel_spmd(nc, maps, core_ids=list(range(n)))
    return assemble(res.results, kinds)


def assemble(results, kinds):
    n = len(results)
    ys = np.zeros((n, 2048, D), np.float32)
    yp = np.zeros((2 * n, 256, D), np.float32)
    o = {}
    for core in range(n):
        r = results[core]
        y = r["yT"].reshape(D, T).T
        ys[core] = y[:TS]
        yp[2 * core:2 * core + 2] = y[TS:].reshape(2, 256, D)

        def tokmajor(name, nf):
            return r[name].reshape(nf, 512).T.reshape(2, 256, nf)
        if 0 in kinds:
            o.setdefault("dk", []).append(tokmajor("new_diff_kT", 2048).reshape(2, 1, 256, 8, 2, 128))
            o.setdefault("dv", []).append(tokmajor("new_diff_vT", 2048).reshape(2, 1, 256, 8, 256))
        if 1 in kinds:
            o.setdefault("nk", []).append(tokmajor("new_na_kT", 2048).reshape(2, 1, 256, 16, 128))
            o.setdefault("nv", []).append(tokmajor("new_na_vT", 2048).reshape(2, 1, 256, 16, 128))
        if 2 in kinds:
            o.setdefault("ckv", []).append(tokmajor("new_mla_ckvT", 256).reshape(2, 1, 256, 256))
            o.setdefault("kr", []).append(tokmajor("new_mla_kropeT", 64).reshape(2, 1, 256, 64))
        if 3 in kinds:
            o.setdefault("hs", []).append(np.ascontiguousarray(r["new_stateT"].reshape(2, 1, 2, 64, 128, 64).transpose(0, 1, 2, 3, 5, 4)))
    res = [yp, ys]
    for k in ("dk", "dv", "nk", "nv", "ckv", "kr", "hs"):
        if k in o:
            res.append(np.concatenate(o[k], 0))
    return tuple(res)
```

```python
import contextlib
import numpy as np
import ml_dtypes
import concourse.bass as bass
import concourse.mybir as mybir
from concourse.bass_utils import run_bass_kernel_spmd

F32 = mybir.dt.float32
BF16 = mybir.dt.bfloat16
AF = mybir.ActivationFunctionType
ALU = mybir.AluOpType

D = 2048
DC = 16
T = 2560
TS = 2048
DFF = 5632
FC = 44
EPS = 1e-6
DEPTH = 4

ENGS = ("pe", "act", "dve", "pool", "sp")
N_DMA_SEMS = 12


class _Op:
    __slots__ = ("eng", "fn", "deps", "dma", "signal", "sem", "val")

    def __init__(self, eng, fn, dma):
        self.eng, self.fn, self.dma = eng, fn, dma
        self.deps = []
        self.signal = dma
        self.sem = None
        self.val = 0


class Sched:
    def __init__(self, nc, stack):
        self.nc = nc
        self.esem = {e: stack.enter_context(nc.semaphore("s_" + e)) for e in ENGS}
        self.ecount = {e: 0 for e in ENGS}
        qs = ("sp", "pool", "act")
        self.dsem = {q: [stack.enter_context(nc.semaphore("d_%s%d" % (q, i)))
                         for i in range(N_DMA_SEMS)] for q in qs}
        self.dcount = {q: [0] * N_DMA_SEMS for q in qs}
        self.dlast = {q: [None] * N_DMA_SEMS for q in qs}
        self.drr = {q: 0 for q in qs}
        self.waited = {e: {} for e in ENGS}
        self.ops = []
        self.lastw = {}
        self.readers = {}
        self.n_ins = 0

    def op(self, eng, fn, reads=(), writes=(), dma=False):
        o = _Op(eng, fn, dma)
        deps = set()
        for r in reads:
            w = self.lastw.get(r)
            if w is not None:
                deps.add(w)
            if isinstance(r, str) and r.startswith("ps"):
                for rd in self.readers.get(r, ()):
                    if rd.eng != eng:
                        deps.add(rd)
        for w_ in writes:
            w = self.lastw.get(w_)
            if w is not None:
                deps.add(w)
            for rd in self.readers.get(w_, ()):
                deps.add(rd)
        if dma:
            q = eng
            i = self.drr[q]
            self.drr[q] = (i + 1) % N_DMA_SEMS
            prev = self.dlast[q][i]
            if prev is not None:
                deps.add(prev)
            self.dcount[q][i] += 16
            o.sem, o.val = self.dsem[q][i], self.dcount[q][i]
            self.dlast[q][i] = o
        for d in deps:
            if d is o:
                continue
            if (not d.dma) and (not dma) and d.eng == "pe" and eng == "pe":
                continue
            o.deps.append(d)
            d.signal = True
        for r in reads:
            self.readers.setdefault(r, []).append(o)
        for w_ in writes:
            self.lastw[w_] = o
            self.readers[w_] = []
        self.ops.append(o)
        return o

    def dma(self, q, out, in_, reads=(), writes=()):
        return self.op(q, lambda e: e.dma_start(out=out, in_=in_), reads, writes, dma=True)

    def flush(self):
        nc = self.nc
        ops = self.ops
        self.ops = []
        if not ops:
            return
        self.nflush = getattr(self, "nflush", 0) + 1
        mf = getattr(self, "max_flush", None)
        if mf is not None and self.nflush > mf:
            self.lastw = {}
            self.readers = {}
            return
        for o in ops:
            if not o.dma and o.signal:
                self.ecount[o.eng] += 1
                o.sem, o.val = self.esem[o.eng], self.ecount[o.eng]
        by_eng = {e: [o for o in ops if o.eng == e] for e in ENGS}
        for e in ENGS:
            comp = [o for o in by_eng[e] if not o.dma]
            if comp and not comp[-1].signal:
                o = comp[-1]
                o.signal = True
                self.ecount[e] += 1
                o.sem, o.val = self.esem[e], self.ecount[e]
        targets = [(self.esem[e], self.ecount[e]) for e in ENGS if self.ecount[e] > 0]
        for q in self.dsem:
            for i in range(N_DMA_SEMS):
                if self.dcount[q][i] > 0:
                    targets.append((self.dsem[q][i], self.dcount[q][i]))
        for q in self.dlast:
            self.dlast[q] = [None] * N_DMA_SEMS

        def emit(eng_name):
            def body(e):
                wd = self.waited[eng_name]
                for o in by_eng[eng_name]:
                    for d in o.deps:
                        if wd.get(d.sem, 0) < d.val:
                            e.wait_ge(d.sem, d.val)
                            wd[d.sem] = d.val
                    ins = o.fn(e)
                    self.n_ins += 1
                    if o.signal:
                        ins.then_inc(o.sem, 16 if o.dma else 1)
                for s, v in targets:
                    if wd.get(s, 0) < v:
                        e.wait_ge(s, v)
                        wd[s] = v
            return body

        with nc.Block() as block:
            block.tensor(emit("pe"))
            block.scalar(emit("act"))
            block.vector(emit("dve"))
            block.gpsimd(emit("pool"))
            block.sync(emit("sp"))
        self.lastw = {}
        self.readers = {}


_uid = [0]


def uid():
    _uid[0] += 1
    return _uid[0]


class Ring:
    def __init__(self, ctx, st, name, shape, dtype, n):
        u = uid()
        self.tiles = [st.enter_context(ctx.nc.sbuf_tensor("%s_%d_%d" % (name, u, i), shape, dtype))
                      for i in range(n)]
        self.keys = ["%s_%d_%d" % (name, u, i) for i in range(n)]
        self.i = 0
        self.n = n

    def next(self):
        t, k = self.tiles[self.i], self.keys[self.i]
        self.i = (self.i + 1) % self.n
        return t, k


class Ctx:
    pass


class PV:
    def __init__(self):
        self.cols = {}
        self.n = 0
        self.parts = []

    def add(self, name, arr):
        arr = np.ascontiguousarray(arr, dtype=np.float32)
        assert arr.shape[0] == 128
        self.cols[name] = (self.n, arr.shape[1])
        self.n += arr.shape[1]
        self.parts.append(arr)

    def build(self):
        return np.concatenate(self.parts, axis=1)


def fm(v):
    v = np.asarray(v, dtype=np.float32)
    c = v.shape[-1] // 128
    return np.ascontiguousarray(v.reshape(-1, 128).T)


def pv_layout(inp, core, kinds=()):
    pv = PV()
    pv.add("ada_b", fm(inp["ada_b"]))
    pv.add("norm_w", fm(inp["norm_w"]))
    pv.add("final_w", fm(inp["final_norm_w"]))
    cond = np.stack([inp["c"][core], inp["c_ctx"]], 0)
    pv.add("cond", fm(cond).reshape(128, 2, 16).transpose(0, 2, 1).reshape(128, 32))
    if 0 in kinds:
        pv.add("dlam", np.broadcast_to(inp["diff_lambda"][0].reshape(1, 512), (128, 512)))
        pv.add("subln", fm(inp["diff_subln_w"][0]))
    if 3 in kinds:
        cw = np.stack([fm(inp["ssd_conv_w"][0][w]) for w in range(7)], -1)
        pv.add("conv_w", cw.reshape(128, 48 * 7))
        pv.add("conv_b", fm(inp["ssd_conv_b"][0]))
        pv.add("ssd_nw", fm(inp["ssd_norm_w"][0]))
        pv.add("dt_bias", inp["ssd_dt_bias"][0].reshape(128, 1))
        pv.add("a_log", inp["ssd_a_log"][0].reshape(128, 1))
        pv.add("ssd_d", np.broadcast_to(inp["ssd_d"][0].reshape(1, 64), (128, 64)))
    if 2 in kinds:
        pv.add("q_norm", fm(inp["mla_q_norm"][0]))
        pv.add("kv_norm", fm(inp["mla_kv_norm"][0]))
    return pv


def pvap(ctx, name, col, n=1):
    o, k = ctx.pvcols[name]
    assert col + n <= k
    return ctx.pvt[:, o + col:o + col + n]


def fmv(ap):
    return ap.rearrange("c p t -> p c t")


def psum_next(ctx, lo=0, hi=6):
    i = ctx.ps_i
    if i < lo or i >= hi:
        i = lo
    ctx.ps_i = i + 1 if i + 1 < hi else lo
    return ctx.ps[i], "ps%d" % i


def gemm(ctx, W, KC, blocks, src, tok_tiles, epi, Wc, nw=3, tsub=512):
    S = ctx.S
    with contextlib.ExitStack() as st:
        maxT = max(tn for _, tn in tok_tiles)
        wring = Ring(ctx, st, "gw", [128, KC, Wc], BF16, nw)
        sring = Ring(ctx, st, "gs", [128, KC, maxT], BF16, 2 if KC * maxT * 4 <= 70000 else 1)
        for (t0, tn) in tok_tiles:
            stile, sk = sring.next()
            if KC >= 32:
                hkc = KC // 2
                S.dma("sp", stile[:, :hkc, :tn], src[:, :hkc, t0:t0 + tn], writes=[sk + "a"])
                S.dma("act", stile[:, hkc:, :tn], src[:, hkc:, t0:t0 + tn], writes=[sk + "b"])
                sk = [sk + "a", sk + "b"]
            else:
                S.dma("sp", stile[:, :, :tn], src[:, :, t0:t0 + tn], writes=[sk])
                sk = [sk]
            for segs, items in blocks:
                wt, wk0 = wring.next()
                o = 0
                wk = []
                for si, (c0, ncol) in enumerate(segs):
                    wk.append("%s_%d" % (wk0, si))
                    S.dma("pool", wt[:, :, o:o + ncol],
                          W[:, c0:c0 + ncol].rearrange("(c p) n -> p c n", p=128), writes=[wk[-1]])
                    o += ncol
                for (ci, off, w) in items:
                    for s0 in range(0, tn, tsub):
                        sn = min(tsub, tn - s0)
                        ps, pk = psum_next(ctx)
                        for kc in range(KC):
                            S.op("pe", lambda e, ps=ps, wt=wt, stile=stile, kc=kc, off=off, w=w, s0=s0, sn=sn:
                                 e.matmul(ps[:w, :sn], wt[:, kc, off:off + w], stile[:, kc, s0:s0 + sn],
                                          start=(kc == 0), stop=(kc == KC - 1)),
                                 reads=wk + sk, writes=[pk])
                        epi(ci, w, t0 + s0, sn, ps, pk)
        ctx.flush()


def simple_blocks(chunks, Wc):
    blocks = []
    cur = None
    for ci, (c0, w) in enumerate(chunks):
        if cur is not None and cur[0][0][0] + cur[0][0][1] == c0 and cur[0][0][1] + w <= Wc:
            segs, items = cur
            items.append((ci, segs[0][1], w))
            segs[0] = (segs[0][0], segs[0][1] + w)
        else:
            cur = ([(c0, w)], [(ci, 0, w)])
            blocks.append(cur)
    return blocks


def rms_phase(ctx, src, C, nfeat, dst, a_fn, b_fn, tok0=0, tok1=T, out_f32=False, dst2=None, dst2_range=None, dst_off=0):
    S = ctx.S
    with contextlib.ExitStack() as st:
        xr = Ring(ctx, st, "rx", [128, C, 512], F32, 2 if (out_f32 and C > 8) else 3)
        sr = Ring(ctx, st, "rsq", [128, C, 512], BF16, 1)
        hr = Ring(ctx, st, "rh", [128, C, 512], F32 if out_f32 else BF16, 2)
        rr = Ring(ctx, st, "rr", [128, 512], F32, 2)
        tiles = [(t0, min(512, tok1 - t0)) for t0 in range(tok0, tok1, 512)]

        def stage_a(t0, tn):
            xt, xk = xr.next()
            sq, sqk = sr.next()
            rs, rk = rr.next()
            xks = ["%s_%d" % (xk, c) for c in range(C)]
            S.dma("sp", xt[:, :, :tn], src[:, :, t0:t0 + tn].rearrange("c p t -> p c t"), writes=xks)
            S.op("act", lambda e: e.activation(sq[:, :, :tn], xt[:, :, :tn], AF.Square), reads=xks, writes=[sqk])
            ps, pk = psum_next(ctx, 6, 8)
            for c in range(C):
                S.op("pe", lambda e, c=c: e.matmul(ps[:, :tn], ctx.ones_bf[:], sq[:, c, :tn], start=(c == 0), stop=(c == C - 1)),
                     reads=[sqk, "const"], writes=[pk])
            S.op("act", lambda e: e.activation(rs[:, :tn], ps[:, :tn], AF.Sqrt, bias=ctx.eps_ap, scale=1.0 / nfeat),
                 reads=[pk, "const"], writes=[rk])
            S.op("dve", lambda e: e.reciprocal(rs[:, :tn], rs[:, :tn]), reads=[rk], writes=[rk])
            return (xt, xks, rs, rk)

        def stage_b(t0, tn, a_out):
            xt, xks, rs, rk = a_out
            cond = 1 if t0 >= TS else 0
            ht, hk = hr.next()
            hks = ["%s_%d" % (hk, c) for c in range(C)]
            for c in range(C):
                if b_fn is None:
                    S.op("dve", lambda e, c=c: e.scalar_tensor_tensor(ht[:, c, :tn], xt[:, c, :tn], a_fn(c, cond), rs[:, :tn], ALU.mult, ALU.mult),
                         reads=[xks[c], rk, "mod"], writes=[hks[c]])
                else:
                    S.op("dve", lambda e, c=c: e.scalar_tensor_tensor(xt[:, c, :tn], xt[:, c, :tn], a_fn(c, cond), rs[:, :tn], ALU.mult, ALU.mult),
                         reads=[xks[c], rk, "mod"], writes=[xks[c]])
                    S.op("act", lambda e, c=c: e.activation(ht[:, c, :tn], xt[:, c, :tn], AF.Identity, bias=b_fn(c, cond)),
                         reads=[xks[c], "mod"], writes=[hks[c]])
            S.dma("act", dst[:, :, t0 - dst_off:t0 - dst_off + tn].rearrange("c p t -> p c t"), ht[:, :, :tn], reads=hks)

        prev = None
        for (t0, tn) in tiles:
            cur = (t0, tn, stage_a(t0, tn))
            if prev is not None:
                stage_b(*prev)
            prev = cur
        stage_b(*prev)
        ctx.flush()


def resid_epi(ctx, st, gate_fn):
    S = ctx.S
    xr = Ring(ctx, st, "ex", [128, 512], F32, 4)

    def epi(ci, w, t0, tn, ps, pk):
        cond = 1 if t0 >= TS else 0
        xt, xk = xr.next()
        S.dma("sp", xt[:, :tn], ctx.xT[ci, :, t0:t0 + tn], writes=[xk])
        S.op("dve", lambda e: e.scalar_tensor_tensor(xt[:, :tn], ps[:, :tn], gate_fn(ci, cond), xt[:, :tn],
                                                      ALU.mult, ALU.add),
             reads=[pk, xk, "mod"], writes=[xk])
        S.dma("act", ctx.xT[ci, :, t0:t0 + tn], xt[:, :tn], reads=[xk])
    return epi


def mod_phase(ctx, layer):
    S = ctx.S
    W = ctx.w["ada_w"][layer]

    def epi(ci, w, t0, tn, ps, pk):
        S.op("dve", lambda e: e.tensor_scalar(ctx.modt[:, ci, :], ps[:, 0:2], pvap(ctx, "ada_b", layer * 144 + ci), None, ALU.add),
             reads=[pk, "pv"], writes=["modraw"])
    chunks = [(ci * 128, 128) for ci in range(144)]
    gemm(ctx, W, DC, simple_blocks(chunks, 512), ctx.condT, [(0, 2)], epi, Wc=512)
    for k in range(3):
        for cond in range(2):
            nw = pvap(ctx, "norm_w", (layer * 3 + k) * 16, 16)
            sc = ctx.modt[:, k * 48 + 16:k * 48 + 32, cond]
            sh = ctx.modt[:, k * 48 + 0:k * 48 + 16, cond]
            gt = ctx.modt[:, k * 48 + 32:k * 48 + 48, cond]
            S.op("dve", lambda e, k=k, cond=cond, sc=sc, nw=nw:
                 e.scalar_tensor_tensor(ctx.A[:, k, :, cond], sc, 1.0, nw, ALU.add, ALU.mult),
                 reads=["modraw", "pv"], writes=["mod"])
            S.op("dve", lambda e, k=k, cond=cond, sh=sh: e.tensor_copy(ctx.B[:, k, :, cond], sh),
                 reads=["modraw"], writes=["mod"])
            S.op("dve", lambda e, k=k, cond=cond, gt=gt:
                 e.tensor_scalar(ctx.G[:, k, :, cond], gt, 0.5 if k != 1 else 1.0, None, ALU.mult),
                 reads=["modraw"], writes=["mod"])
    ctx.flush()


def ffn_phase(ctx, layer, which):
    S = ctx.S
    k = 0 if which == 0 else 2
    rms_phase(ctx, ctx.xT, DC, D, ctx.hT,
              lambda c, cond: ctx.A[:, k, c, cond:cond + 1], lambda c, cond: ctx.B[:, k, c, cond:cond + 1])
    Wgu = ctx.w["ffn_w_gu"][layer, which]
    blocks = []
    for j2 in range(FC // 2):
        segs = [(j2 * 256, 256), (DFF + j2 * 256, 256)]
        items = []
        for jj in range(2):
            items.append((2 * (2 * j2 + jj), jj * 128, 128))
            items.append((2 * (2 * j2 + jj) + 1, 256 + jj * 128, 128))
        blocks.append((segs, items))
    with contextlib.ExitStack() as st:
        sgr = Ring(ctx, st, "sg", [128, T], F32, 2)
        ar = Ring(ctx, st, "fa", [128, T], BF16, 2)
        state = {}

        def epi(ci, w, t0, tn, ps, pk):
            j = ci // 2
            if ci % 2 == 0:
                if t0 == 0:
                    state["sg"] = sgr.next()
                sg, sgk = state["sg"]
                S.op("act", lambda e: e.activation(sg[:, t0:t0 + tn], ps[:, :tn], AF.Silu), reads=[pk], writes=[sgk])
            else:
                sg, sgk = state["sg"]
                if t0 == 0:
                    state["a"] = ar.next()
                at, ak = state["a"]
                S.op("dve", lambda e: e.tensor_tensor(at[:, t0:t0 + tn], sg[:, t0:t0 + tn], ps[:, :tn], ALU.mult),
                     reads=[pk, sgk], writes=[ak])
                if t0 + tn == T:
                    S.dma("sp", ctx.aT[j], at[:], reads=[ak])
        gemm(ctx, Wgu, DC, blocks, fmv(ctx.hT), [(0, T)], epi, Wc=512)
    with contextlib.ExitStack() as st:
        epi = resid_epi(ctx, st, lambda ci, cond: ctx.G[:, k, ci, cond:cond + 1])
        Wd = ctx.w["ffn_w_down"][layer, which]
        chunks = [(ci * 128, 128) for ci in range(DC)]
        gemm(ctx, Wd, FC, simple_blocks(chunks, 256), fmv(ctx.aT), [(0, 1024), (1024, 1024), (2048, 512)], epi, Wc=256, nw=2)


def store_epi(ctx, st, dst_fn):
    S = ctx.S
    rb = Ring(ctx, st, "seb", [128, 512], BF16, 4)
    rf = Ring(ctx, st, "sef", [128, 512], F32, 3)
    cnt = [0]

    def epi(ci, w, t0, tn, ps, pk):
        dsts = sorted(dst_fn(ci, t0, tn), key=lambda x: 0 if x[1] == F32 else 1)
        first = None
        for (dst, dt) in dsts:
            tl, tk = (rb if dt == BF16 else rf).next()
            if first is None:
                cnt[0] += 1
                if cnt[0] % 2 == 0:
                    S.op("act", lambda e, tl=tl: e.activation(tl[:w, :tn], ps[:w, :tn], AF.Identity), reads=[pk], writes=[tk])
                else:
                    S.op("dve", lambda e, tl=tl: e.tensor_copy(tl[:w, :tn], ps[:w, :tn]), reads=[pk], writes=[tk])
                first = (tl, tk)
            else:
                S.op("pool", lambda e, tl=tl, f=first[0]: e.tensor_copy(tl[:w, :tn], f[:w, :tn]), reads=[first[1]], writes=[tk])
            S.dma("act", dst, tl[:w, :tn], reads=[tk])
    return epi


def vtrans_phase(ctx, srcT, nchunks, ntok, vtok, tok_off=0, col_off=0):
    S = ctx.S
    with contextlib.ExitStack() as st:
        vt = st.enter_context(ctx.nc.sbuf_tensor("vtT_%d" % uid(), [128, nchunks, ntok], BF16))
        rr = Ring(ctx, st, "vrow", [128, nchunks * 128], BF16, 3)
        half = (nchunks + 1) // 2
        for c0 in range(0, nchunks, half):
            S.dma("sp", vt[:, c0:c0 + half, :], srcT[:, c0:c0 + half, :], writes=["vt%d" % c0])
        vkeys = ["vt%d" % c0 for c0 in range(0, nchunks, half)]
        for tb in range(ntok // 128):
            row, rk = rr.next()
            for c8 in range(0, nchunks, 8):
                ps, pk = psum_next(ctx, 0, 8)
                psb = ps.bitcast(BF16)
                n8 = min(8, nchunks - c8)
                for c in range(c8, c8 + n8):
                    S.op("pe", lambda e, psb=psb, c=c, c8=c8, tb=tb:
                         e.transpose(psb[:, (c - c8) * 128:(c - c8 + 1) * 128], vt[:, c, tb * 128:(tb + 1) * 128], ctx.ident_bf),
                         reads=vkeys + ["const"], writes=[pk])
                if (c8 // 8) % 2 == 0:
                    S.op("dve", lambda e, psb=psb, row=row, c8=c8, n8=n8:
                         e.tensor_copy(row[:, c8 * 128:(c8 + n8) * 128], psb[:, :n8 * 128]), reads=[pk], writes=[rk + "a"])
                else:
                    S.op("act", lambda e, psb=psb, row=row, c8=c8, n8=n8:
                         e.activation(row[:, c8 * 128:(c8 + n8) * 128], psb[:, :n8 * 128], AF.Identity), reads=[pk], writes=[rk + "b"])
            S.dma("sp", vtok[tok_off + tb * 128:tok_off + (tb + 1) * 128, col_off:col_off + nchunks * 128], row[:], reads=[rk + "a", rk + "b"])
        ctx.flush()


def rope_phase(ctx, items, rope_tab, R, d):
    S = ctx.S
    with contextlib.ExitStack() as st:
        cs = st.enter_context(ctx.nc.sbuf_tensor("ropecs_%d" % uid(), [d, 2, TS], F32))
        S.dma("sp", cs[:, 0, :], rope_tab[0], writes=["cs0"])
        S.dma("sp", cs[:, 1, :], rope_tab[1], writes=["cs1"])
        xr = Ring(ctx, st, "ropx", [d, TS], BF16, 3)
        t1r = Ring(ctx, st, "ropt1", [d, 512], F32, 3)
        t2r = Ring(ctx, st, "ropt2", [d, 512], F32, 3)
        orr = Ring(ctx, st, "ropo", [d, TS], BF16, 3)
        for ap in items:
            xt, xk = xr.next()
            ot, ok = orr.next()
            S.dma("sp", xt[:], ap, writes=[xk])
            for s0 in range(0, TS, 512):
                ps, pk = psum_next(ctx, 0, 8)
                t1, k1 = t1r.next()
                t2, k2 = t2r.next()
                S.op("pe", lambda e, ps=ps, xt=xt, s0=s0: e.matmul(ps[:d, :512], R, xt[:, s0:s0 + 512], start=True, stop=True),
                     reads=[xk, "const"], writes=[pk])
                S.op("pool", lambda e, t1=t1, xt=xt, s0=s0: e.tensor_tensor(t1[:], xt[:, s0:s0 + 512], cs[:, 0, s0:s0 + 512], ALU.mult),
                     reads=[xk, "cs0"], writes=[k1])
                S.op("dve", lambda e, t2=t2, ps=ps, s0=s0: e.tensor_tensor(t2[:], ps[:d, :512], cs[:, 1, s0:s0 + 512], ALU.mult),
                     reads=[pk, "cs1"], writes=[k2])
                S.op("dve", lambda e, ot=ot, t1=t1, t2=t2, s0=s0: e.tensor_tensor(ot[:, s0:s0 + 512], t1[:], t2[:], ALU.add),
                     reads=[k1, k2], writes=[ok])
            S.dma("sp", ap, ot[:], reads=[ok])
        ctx.flush()


def attention(ctx, units, scale, out_dt):
    S = ctx.S
    with contextlib.ExitStack() as st:
        npc = max(len(u["pieces"]) for u in units)
        nkmax = max(sum(v.shape[0] for v in u["vsegs"]) for u in units)
        nqmax = max(u["nq"] for u in units)
        dvmax = max(u["dv"] for u in units)
        kr = [Ring(ctx, st, "atk%d" % i, [128, nkmax], BF16, 2) for i in range(npc)]
        qr = [Ring(ctx, st, "atq%d" % i, [128, nqmax], BF16, 2) for i in range(npc)]
        vr = Ring(ctx, st, "atv", [128, nkmax // 128, dvmax], BF16, 2)
        pr = Ring(ctx, st, "atp", [128, 512], BF16, 4)
        rdr = Ring(ctx, st, "atrd", [128, 512], F32, 2)
        otr = Ring(ctx, st, "ato", [128, dvmax // 128, 512], out_dt, 2)
        sbank = [0]
        aset = [0]

        def ld(dst, src, key):
            q = "sp" if src.dtype == BF16 else "pool"
            S.dma(q, dst, src, writes=[key])

        def load_unit(u):
            nq, dv = u["nq"], u["dv"]
            L = dict(ktiles=[], qtiles=[], kkeys=[], qkeys=[])
            for pi, (d, q_ap, k_aps) in enumerate(u["pieces"]):
                kt, kk = kr[pi].next()
                qt, qk = qr[pi].next()
                o = 0
                kks = []
                for si, ka in enumerate(k_aps):
                    n = ka.shape[1]
                    ld(kt[:d, o:o + n], ka, "%s_%d" % (kk, si))
                    kks.append("%s_%d" % (kk, si))
                    o += n
                ld(qt[:d, :nq], q_ap, qk)
                L["ktiles"].append(kt); L["qtiles"].append(qt); L["kkeys"].append(kks); L["qkeys"].append(qk)
            vt, vk = vr.next()
            o = 0
            vks = []
            for si, va in enumerate(u["vsegs"]):
                n = va.shape[0]
                ld(vt[:, o // 128:(o + n) // 128, :dv], va.rearrange("(c p) d -> p c d", p=128), "%s_%d" % (vk, si))
                vks.append("%s_%d" % (vk, si))
                o += n
            L["vt"], L["vks"] = vt, vks
            return L

        steps = []
        for ui, u in enumerate(units):
            nk = sum(v.shape[0] for v in u["vsegs"])
            for q0 in range(0, u["nq"], 512):
                for kc in range(nk // 128):
                    steps.append((ui, q0, kc, nk // 128))
        loaded = {}
        state = {}

        def emit_s(i):
            ui, q0, kc, nkc = steps[i]
            u = units[ui]
            if ui not in loaded:
                loaded[ui] = load_unit(u)
                loaded.pop(ui - 2, None)
            L = loaded[ui]
            qn = min(512, u["nq"] - q0)
            sps, spk = ctx.ps[sbank[0]], "ps%d" % sbank[0]
            sbank[0] ^= 1
            npieces = len(u["pieces"])
            for pi, (d, _, _) in enumerate(u["pieces"]):
                S.op("pe", lambda e, kt=L["ktiles"][pi], qt=L["qtiles"][pi], d=d, pi=pi:
                     e.matmul(sps[:, :qn], kt[:d, kc * 128:(kc + 1) * 128], qt[:d, q0:q0 + qn],
                              start=(pi == 0), stop=(pi == npieces - 1)),
                     reads=L["kkeys"][pi] + [L["qkeys"][pi]], writes=[spk])
            state[i] = (sps, spk, qn)

        def emit_rest(i):
            ui, q0, kc, nkc = steps[i]
            u = units[ui]
            L = loaded[ui]
            sps, spk, qn = state.pop(i)
            ndv = u["dv"] // 128
            if kc == 0:
                base = 2 + 3 * aset[0]
                aset[0] ^= 1
                state["acc"] = ([(ctx.ps[base + j], "ps%d" % (base + j)) for j in range(ndv)], (ctx.ps[base + 2], "ps%d" % (base + 2)))
            obanks, dbank = state["acc"]
            pt, pk_ = pr.next()
            S.op("act", lambda e: e.activation(pt[:, :qn], sps[:, :qn], AF.Exp, scale=scale), reads=[spk], writes=[pk_])
            vt = L["vt"]
            for j in range(ndv):
                S.op("pe", lambda e, j=j, ob=obanks[j][0]:
                     e.matmul(ob[:, :qn], vt[:, kc, j * 128:(j + 1) * 128], pt[:, :qn], start=(kc == 0), stop=(kc == nkc - 1)),
                     reads=L["vks"] + [pk_], writes=[obanks[j][1]])
            S.op("pe", lambda e, db=dbank[0]: e.matmul(db[:, :qn], ctx.ones_bf, pt[:, :qn], start=(kc == 0), stop=(kc == nkc - 1)),
                 reads=[pk_, "const"], writes=[dbank[1]])
            if kc == nkc - 1:
                rd, rdk = rdr.next()
                ot, otk = otr.next()
                S.op("dve", lambda e, db=dbank[0]: e.reciprocal(rd[:, :qn], db[:, :qn]), reads=[dbank[1]], writes=[rdk])
                for j in range(ndv):
                    S.op("dve", lambda e, j=j, ob=obanks[j][0]: e.tensor_tensor(ot[:, j, :qn], ob[:, :qn], rd[:, :qn], ALU.mult),
                         reads=[obanks[j][1], rdk], writes=[otk])
                S.dma("sp", u["out"][:, q0:q0 + qn].rearrange("(c p) t -> p c t", p=128), ot[:, :ndv, :qn], reads=[otk])

        n = len(steps)
        emit_s(0)
        for i in range(n):
            if i + 1 < n:
                emit_s(i + 1)
            emit_rest(i)
        ctx.flush()


def wo_phase(ctx, W, KC, src, k_gate=1):
    with contextlib.ExitStack() as st:
        epi = resid_epi(ctx, st, lambda ci, cond: ctx.G[:, k_gate, ci, cond:cond + 1])
        chunks = [(ci * 128, 128) for ci in range(DC)]
        tiles = [(0, T)] if KC <= 16 else [(0, 1280), (1280, 768), (2048, 512)]
        gemm(ctx, W, KC, simple_blocks(chunks, 512 if KC <= 16 else 256), fmv(src), tiles, epi, Wc=512 if KC <= 16 else 256)


def qkv_phase(ctx, W, kout, vout):
    with contextlib.ExitStack() as st:
        def dst(ci, t0, tn):
            r = [(ctx.qkvT[ci, :, t0:t0 + tn], BF16)]
            if ci >= 16 and t0 >= TS:
                o = kout if ci < 32 else vout
                r.append((o[(ci - 16) % 16, :, t0 - TS:t0 - TS + tn], F32))
            return r
        epi = store_epi(ctx, st, dst)
        chunks = [(ci * 128, 128) for ci in range(48)]
        gemm(ctx, W, DC, simple_blocks(chunks, 512), fmv(ctx.hT), [(0, T)], epi, Wc=512)


def diff_layer(ctx, layer):
    S = ctx.S
    nc = ctx.nc
    lam_init = 0.8 - 0.6 * float(np.exp(-0.3 * layer))
    rms_phase(ctx, ctx.xT, DC, D, ctx.hT,
              lambda c, cond: ctx.A[:, 1, c, cond:cond + 1], lambda c, cond: ctx.B[:, 1, c, cond:cond + 1])
    qkv_phase(ctx, ctx.w["diff_w_qkv"][0], ctx.out["new_diff_kT"], ctx.out["new_diff_vT"])
    rope_phase(ctx, [ctx.qkvT[ci, :, 0:TS] for ci in range(32)], ctx.c["rope128"], ctx.R128, 128)
    vtrans_phase(ctx, fmv(ctx.qkvT)[:, 32:48, :], 16, T, ctx.vtok)
    scale = 128 ** -0.5
    units = []
    ck = ctx.c["cache_diff_kT"]
    cv = ctx.c["cache_diff_v"]
    for u in range(16):
        h = u // 2
        units.append(dict(pieces=[(128, ctx.qkvT[u, :, 0:TS], [ctx.qkvT[16 + u, :, 0:TS], ck[u]])],
                          vsegs=[ctx.vtok[0:TS, h * 256:(h + 1) * 256], cv[:, h * 256:(h + 1) * 256]],
                          nq=TS, dv=256, out=ctx.oacc[u, :, 0:TS]))
    for p in range(2):
        a, b = TS + p * 256, TS + (p + 1) * 256
        for u in range(16):
            h = u // 2
            units.append(dict(pieces=[(128, ctx.qkvT[u, :, a:b], [ctx.qkvT[16 + u, :, a:b]])],
                              vsegs=[ctx.vtok[a:b, h * 256:(h + 1) * 256]],
                              nq=256, dv=256, out=ctx.oacc[u, :, a:b]))
    attention(ctx, units, scale, F32)
    with contextlib.ExitStack() as st:
        E = st.enter_context
        sm = E(nc.sbuf_tensor("dlsm_%d" % uid(), [128, 8], F32))
        tmp = E(nc.sbuf_tensor("dltmp_%d" % uid(), [128, 256], F32))
        dl = lambda i: pvap(ctx, "dlam", i * 128, 128)
        S.op("dve", lambda e: e.tensor_tensor(tmp[:, 0:128], dl(0), dl(1), ALU.mult), reads=["pv"], writes=["dltmp"])
        S.op("dve", lambda e: e.tensor_tensor(tmp[:, 128:256], dl(2), dl(3), ALU.mult), reads=["pv"], writes=["dltmp"])
        S.op("dve", lambda e: e.reduce_sum(sm[:, 0:1], tmp[:, 0:128], mybir.AxisListType.X), reads=["dltmp"], writes=["sm0"])
        S.op("dve", lambda e: e.reduce_sum(sm[:, 1:2], tmp[:, 128:256], mybir.AxisListType.X), reads=["dltmp"], writes=["sm1"])
        S.op("act", lambda e: e.activation(sm[:, 2:4], sm[:, 0:2], AF.Exp), reads=["sm0", "sm1"], writes=["sm2"])
        S.op("dve", lambda e: e.scalar_tensor_tensor(sm[:, 4:5], sm[:, 3:4], -lam_init, sm[:, 2:3], ALU.add, ALU.subtract),
             reads=["sm2"], writes=["sm4"])
        S.op("dve", lambda e: e.tensor_scalar(sm[:, 5:7], pvap(ctx, "subln", 0, 2), 1.0 - lam_init, None, ALU.mult),
             reads=["pv"], writes=["sm5"])
        o0r = Ring(ctx, st, "dlo0", [128, 2, 512], F32, 2)
        o1r = Ring(ctx, st, "dlo1", [128, 2, 512], F32, 2)
        sqr = Ring(ctx, st, "dlsq", [128, 2, 512], BF16, 2)
        rsr = Ring(ctx, st, "dlrs", [128, 512], F32, 2)
        obr = Ring(ctx, st, "dlob", [128, 2, 512], BF16, 2)
        for h in range(8):
            for t0 in range(0, T, 512):
                o0, k0 = o0r.next(); o1, k1 = o1r.next(); sq, sqk = sqr.next(); rs, rk = rsr.next(); ob, obk = obr.next()
                S.dma("sp", o0[:], ctx.oacc[2 * h, :, t0:t0 + 512].rearrange("(c p) t -> p c t", p=128), writes=[k0])
                S.dma("sp", o1[:], ctx.oacc[2 * h + 1, :, t0:t0 + 512].rearrange("(c p) t -> p c t", p=128), writes=[k1])
                S.op("dve", lambda e, o0=o0, o1=o1: e.scalar_tensor_tensor(o0[:], o1[:], sm[:, 4:5], o0[:], ALU.mult, ALU.add),
                     reads=[k0, k1, "sm4"], writes=[k0])
                S.op("act", lambda e, sq=sq, o0=o0: e.activation(sq[:], o0[:], AF.Square), reads=[k0], writes=[sqk])
                ps, pk = psum_next(ctx, 6, 8)
                for c in range(2):
                    S.op("pe", lambda e, ps=ps, sq=sq, c=c: e.matmul(ps[:, :512], ctx.ones_bf, sq[:, c, :], start=(c == 0), stop=(c == 1)),
                         reads=[sqk, "const"], writes=[pk])
                S.op("act", lambda e, rs=rs, ps=ps: e.activation(rs[:], ps[:, :512], AF.Sqrt, bias=ctx.eps_ap, scale=1.0 / 256),
                     reads=[pk, "const"], writes=[rk])
                S.op("dve", lambda e, rs=rs: e.reciprocal(rs[:], rs[:]), reads=[rk], writes=[rk])
                for c in range(2):
                    S.op("dve", lambda e, ob=ob, o0=o0, rs=rs, c=c:
                         e.scalar_tensor_tensor(ob[:, c, :], o0[:, c, :], sm[:, 5 + c:6 + c], rs[:], ALU.mult, ALU.mult),
                         reads=[k0, rk, "sm5"], writes=[obk])
                S.dma("sp", fmv(ctx.oT)[:, 2 * h:2 * h + 2, t0:t0 + 512], ob[:], reads=[obk])
        ctx.flush()
    wo_phase(ctx, ctx.w["diff_w_o"][0], DC, ctx.oT)


def na_classes():
    out = []
    for r in range(32):
        rs = min(max(r - 4, 0), 24)
        cls = r if r < 4 else (4 if r < 28 else 5 + (r - 28))
        out.append((rs, cls))
    return out


def na_host_tables(rpb):
    kc = np.arange(64)[:, None]
    qc = np.arange(64)[None, :]
    dc = np.clip(kc - qc + 15, 0, 30)
    cstart = np.clip(qc - 8, 0, 48)
    ok = (kc >= cstart) & (kc < cstart + 16)
    bias = np.zeros((16, 128, 9, 4, 64), np.float32)
    mask = np.zeros((128, 9, 4, 64), np.float32)
    cls_r = {}
    for r, (rs, cls) in enumerate(na_classes()):
        cls_r[cls] = (r, rs)
    for cls, (r, rs) in cls_r.items():
        for j in range(4):
            for w2 in range(2):
                wr = 2 * j + w2
                ridx = rs + wr - r + 7
                bias[:, w2 * 64:(w2 + 1) * 64, cls, j, :] = rpb[:, ridx][:, dc]
                mask[w2 * 64:(w2 + 1) * 64, cls, j, :] = ok
    return bias.reshape(16, 128, 2304), mask.reshape(128, 2304).astype(ml_dtypes.bfloat16)


def na_layer(ctx, layer):
    S = ctx.S
    nc = ctx.nc
    rms_phase(ctx, ctx.xT, DC, D, ctx.hT,
              lambda c, cond: ctx.A[:, 1, c, cond:cond + 1], lambda c, cond: ctx.B[:, 1, c, cond:cond + 1])
    qkv_phase(ctx, ctx.w["na_w_qkv"][0], ctx.out["new_na_kT"], ctx.out["new_na_vT"])
    vtrans_phase(ctx, fmv(ctx.qkvT)[:, 32:48, :], 16, T, ctx.vtok)
    scale = 128 ** -0.5
    units = []
    for p in range(2):
        a, b = TS + p * 256, TS + (p + 1) * 256
        for h in range(16):
            units.append(dict(pieces=[(128, ctx.qkvT[h, :, a:b], [ctx.qkvT[16 + h, :, a:b]])],
                              vsegs=[ctx.vtok[a:b, h * 128:(h + 1) * 128]],
                              nq=256, dv=128, out=ctx.oT[h, :, a:b]))
    attention(ctx, units, scale, BF16)
    rows = na_classes()
    with contextlib.ExitStack() as st:
        E = st.enter_context
        maskt = E(nc.sbuf_tensor("namask_%d" % uid(), [128, 2304], BF16))
        S.dma("sp", maskt[:], ctx.c["na_mask"], writes=["namask"])
        qr = Ring(ctx, st, "naq", [128, TS], BF16, 2)
        kr = Ring(ctx, st, "nak", [128, TS], BF16, 2)
        ver = Ring(ctx, st, "nave", [128, 16, 128], BF16, 2)
        vor = Ring(ctx, st, "navo", [128, 15, 128], BF16, 2)
        ckr = Ring(ctx, st, "nack", [128, 512], BF16, 2)
        cvr = Ring(ctx, st, "nacv", [128, 4, 128], BF16, 2)
        br = Ring(ctx, st, "nab", [128, 2304], F32, 1)
        er = Ring(ctx, st, "nae", [128, 2304], BF16, 2)
        pr = Ring(ctx, st, "nap", [128, 512], BF16, 4)
        rdr = Ring(ctx, st, "nard", [128, 512], F32, 2)
        otr = Ring(ctx, st, "nao", [128, TS], BF16, 2)
        sb = [0]
        aset = [0]
        heads = {}

        def load_head(h):
            qt, qk = qr.next(); kt, kk = kr.next(); ve, vek = ver.next(); vo, vok = vor.next()
            ck, ckk = ckr.next(); cv, cvk = cvr.next(); bt, bk = br.next(); et, ek = er.next(); ot, otk = otr.next()
            S.dma("sp", qt[:], ctx.qkvT[h, :, 0:TS], writes=[qk])
            S.dma("sp", kt[:], ctx.qkvT[16 + h, :, 0:TS], writes=[kk])
            S.dma("sp", ve[:], ctx.vtok[0:TS, h * 128:(h + 1) * 128].rearrange("(c p) d -> p c d", p=128), writes=[vek])
            S.dma("sp", vo[:], ctx.vtok[64:TS - 64, h * 128:(h + 1) * 128].rearrange("(c p) d -> p c d", p=128), writes=[vok])
            S.dma("pool", ck[:], ctx.c["cache_na_kT"][h], writes=[ckk])
            S.dma("pool", cv[:], ctx.c["cache_na_v"][:, h * 128:(h + 1) * 128].rearrange("(c p) d -> p c d", p=128), writes=[cvk])
            S.dma("sp", bt[:], ctx.c["na_bias"][h], writes=[bk])
            S.op("act", lambda e: e.activation(et[:], bt[:], AF.Exp), reads=[bk], writes=[ek])
            S.op("dve", lambda e: e.tensor_tensor(et[:], et[:], maskt[:], ALU.mult), reads=[ek, "namask"], writes=[ek])
            return dict(qt=qt, qk=qk, kt=kt, kk=kk, ve=ve, vek=vek, vo=vo, vok=vok, ck=ck, ckk=ckk, cv=cv, cvk=cvk, et=et, ek=ek, ot=ot, otk=otk)

        steps = [(h, r) for h in range(16) for r in range(32)]
        pend = {}

        def emit_s(i):
            h, r = steps[i]
            if h not in heads:
                heads[h] = load_head(h)
                heads.pop(h - 2, None)
            H = heads[h]
            rs, cls = rows[r]
            sps, spk = ctx.ps[sb[0]], "ps%d" % sb[0]
            sb[0] ^= 1
            qs = H["qt"][:, r * 64:(r + 1) * 64]
            vch = []
            for j in range(4):
                k0 = (rs + 2 * j) * 64
                S.op("pe", lambda e, k0=k0, j=j: e.matmul(sps[:, j * 64:(j + 1) * 64], H["kt"][:, k0:k0 + 128], qs, start=True, stop=True),
                     reads=[H["kk"], H["qk"]], writes=[spk])
                rr_ = rs + 2 * j
                vch.append((H["ve"][:, rr_ // 2, :], H["vek"]) if rr_ % 2 == 0 else (H["vo"][:, (rr_ - 1) // 2, :], H["vok"]))
            for j in range(4):
                S.op("pe", lambda e, j=j: e.matmul(sps[:, (4 + j) * 64:(5 + j) * 64], H["ck"][:, j * 128:(j + 1) * 128], qs, start=True, stop=True),
                     reads=[H["ckk"], H["qk"]], writes=[spk])
                vch.append((H["cv"][:, j, :], H["cvk"]))
            pend[i] = (sps, spk, vch)

        def emit_rest(i):
            h, r = steps[i]
            H = heads[h]
            rs, cls = rows[r]
            sps, spk, vch = pend.pop(i)
            q8, r8 = r // 8, r % 8
            if r8 == 0:
                base = 2 + 2 * aset[0]
                aset[0] ^= 1
                pend["acc"] = ((ctx.ps[base], "ps%d" % base), (ctx.ps[base + 1], "ps%d" % (base + 1)))
            (ob, obk), (db, dbk) = pend["acc"]
            pt, pk_ = pr.next()
            et, ek = H["et"], H["ek"]
            S.op("act", lambda e: e.activation(pt[:], sps[:], AF.Exp, scale=scale), reads=[spk], writes=[pk_])
            S.op("dve", lambda e: e.tensor_tensor(pt[:, 0:256], pt[:, 0:256], et[:, cls * 256:(cls + 1) * 256], ALU.mult),
                 reads=[pk_, ek], writes=[pk_])
            for j in range(8):
                S.op("pe", lambda e, v=vch[j][0], j=j: e.matmul(ob[:, r8 * 64:(r8 + 1) * 64], v, pt[:, j * 64:(j + 1) * 64], start=(j == 0), stop=(j == 7)),
                     reads=[vch[j][1], pk_], writes=[obk])
            for j in range(8):
                S.op("pe", lambda e, j=j: e.matmul(db[:, r8 * 64:(r8 + 1) * 64], ctx.ones_bf, pt[:, j * 64:(j + 1) * 64], start=(j == 0), stop=(j == 7)),
                     reads=[pk_, "const"], writes=[dbk])
            if r8 == 7:
                ot, otk = H["ot"], H["otk"]
                rd, rdk = rdr.next()
                S.op("dve", lambda e: e.reciprocal(rd[:], db[:]), reads=[dbk], writes=[rdk])
                S.op("dve", lambda e: e.tensor_tensor(ot[:, q8 * 512:(q8 + 1) * 512], ob[:], rd[:], ALU.mult),
                     reads=[obk, rdk], writes=[otk])
                if r == 31:
                    S.dma("sp", ctx.oT[h, :, 0:TS], ot[:], reads=[otk])

        n = len(steps)
        emit_s(0)
        for i in range(n):
            if i + 1 < n:
                emit_s(i + 1)
            emit_rest(i)
        ctx.flush()
    wo_phase(ctx, ctx.w["na_w_o"][0], DC, ctx.oT)


def mla_layer(ctx, layer):
    S = ctx.S
    nc = ctx.nc
    TK = T + 512
    rms_phase(ctx, ctx.xT, DC, D, ctx.hT,
              lambda c, cond: ctx.A[:, 1, c, cond:cond + 1], lambda c, cond: ctx.B[:, 1, c, cond:cond + 1])
    with contextlib.ExitStack() as st:
        epi = store_epi(ctx, st, lambda ci, t0, tn: [(ctx.cqT[ci, :, t0:t0 + tn], F32)])
        gemm(ctx, ctx.w["mla_w_dq"][0], DC, simple_blocks([(c * 128, 128) for c in range(4)], 512), fmv(ctx.hT), [(0, T)], epi, Wc=512)
    rms_phase(ctx, ctx.cqT, 4, 512, ctx.cqnT, lambda c, cond: pvap(ctx, "q_norm", c), None)
    with contextlib.ExitStack() as st:
        def dstq(ci, t0, tn):
            h = ci // 2
            if ci % 2 == 0:
                return [(ctx.mlaq[h, 0:128, t0:t0 + tn], BF16)]
            return [(ctx.mlaq[h, 128:192, t0:t0 + tn], BF16)]
        epi = store_epi(ctx, st, dstq)
        chunks = []
        for h in range(16):
            chunks += [(h * 192, 128), (h * 192 + 128, 64)]
        gemm(ctx, ctx.w["mla_w_uq"][0], 4, simple_blocks(chunks, 384), fmv(ctx.cqnT), [(0, T)], epi, Wc=384)
    with contextlib.ExitStack() as st:
        def dstkv(ci, t0, tn):
            if ci < 2:
                return [(ctx.ckvraw[ci, :, t0:t0 + tn], F32)]
            r = [(ctx.krT[0:64, t0:t0 + tn], BF16)]
            if t0 >= TS:
                r.append((ctx.out["new_mla_kropeT"][0:64, t0 - TS:t0 - TS + tn], F32))
            return r
        epi = store_epi(ctx, st, dstkv)
        gemm(ctx, ctx.w["mla_w_dkv"][0], DC, simple_blocks([(0, 128), (128, 128), (256, 64)], 320), fmv(ctx.hT), [(0, T)], epi, Wc=320)
    rms_phase(ctx, ctx.ckvraw, 2, 256, ctx.ckvT, lambda c, cond: pvap(ctx, "kv_norm", c), None)
    rms_phase(ctx, ctx.ckvraw, 2, 256, ctx.out["new_mla_ckvT"], lambda c, cond: pvap(ctx, "kv_norm", c), None,
              tok0=TS, tok1=T, out_f32=True, dst_off=TS)
    with contextlib.ExitStack() as st:
        t1 = st.enter_context(nc.sbuf_tensor("mlac1_%d" % uid(), [128, 2, 512], BF16))
        t2 = st.enter_context(nc.sbuf_tensor("mlac2_%d" % uid(), [64, 512], BF16))
        S.dma("pool", t1[:], fmv(ctx.c["cache_mla_ckvT"]), writes=["mc1"])
        S.dma("pool", t2[:], ctx.c["cache_mla_kropeT"], writes=["mc2"])
        S.dma("sp", fmv(ctx.ckvT)[:, :, T:TK], t1[:], reads=["mc1"])
        S.dma("sp", ctx.krT[0:64, T:TK], t2[:], reads=["mc2"])
        ctx.flush()
    rope_phase(ctx, [ctx.mlaq[h, 128:192, 0:TS] for h in range(16)] + [ctx.krT[0:64, 0:TS]], ctx.c["rope64"], ctx.R64, 64)
    with contextlib.ExitStack() as st:
        def dstx(ci, t0, tn):
            h = ci // 2
            return [(ctx.mlakv[h if ci % 2 == 0 else 16 + h, :, t0:t0 + tn], BF16)]
        epi = store_epi(ctx, st, dstx)
        chunks = [(c * 128, 128) for c in range(32)]
        gemm(ctx, ctx.w["mla_w_ukv"][0], 2, simple_blocks(chunks, 512), fmv(ctx.ckvT), [(0, TK)], epi, Wc=512)
    vtrans_phase(ctx, fmv(ctx.mlakv)[:, 16:32, :], 16, TK, ctx.vtok)
    scale = 192 ** -0.5
    units = []
    for h in range(16):
        units.append(dict(pieces=[(128, ctx.mlaq[h, 0:128, 0:TS], [ctx.mlakv[h, :, 0:TS], ctx.mlakv[h, :, T:TK]]),
                                  (64, ctx.mlaq[h, 128:192, 0:TS], [ctx.krT[0:64, 0:TS], ctx.krT[0:64, T:TK]])],
                          vsegs=[ctx.vtok[0:TS, h * 128:(h + 1) * 128], ctx.vtok[T:TK, h * 128:(h + 1) * 128]],
                          nq=TS, dv=128, out=ctx.oT[h, :, 0:TS]))
    for p in range(2):
        a, b = TS + p * 256, TS + (p + 1) * 256
        for h in range(16):
            units.append(dict(pieces=[(128, ctx.mlaq[h, 0:128, a:b], [ctx.mlakv[h, :, a:b]]),
                                      (64, ctx.mlaq[h, 128:192, a:b], [ctx.krT[0:64, a:b]])],
                              vsegs=[ctx.vtok[a:b, h * 128:(h + 1) * 128]],
                              nq=256, dv=128, out=ctx.oT[h, :, a:b]))
    attention(ctx, units, scale, BF16)
    wo_phase(ctx, ctx.w["mla_w_o"][0], DC, ctx.oT)


SEQS = [(0, 2048), (2048, 256), (2304, 256)]
PADOFF = [3, 2054, 2313]
TP = T + 12


def ssd_layer(ctx, layer):
    S = ctx.S
    nc = ctx.nc
    rms_phase(ctx, ctx.xT, DC, D, ctx.hT,
              lambda c, cond: ctx.A[:, 1, c, cond:cond + 1], lambda c, cond: ctx.B[:, 1, c, cond:cond + 1])
    with contextlib.ExitStack() as st:
        rb = Ring(ctx, st, "sib", [128, 512], BF16, 3)
        rb2 = Ring(ctx, st, "sib2", [128, 512], BF16, 3)
        rf = Ring(ctx, st, "sif", [128, 512], F32, 2)

        def epi(ci, w, t0, tn, ps, pk):
            if ci < 32:
                tl, tk = rb.next()
                S.op("act", lambda e: e.activation(tl[:, :tn], ps[:, :tn], AF.Silu), reads=[pk], writes=[tk])
                S.dma("sp", ctx.szT[ci, :, t0:t0 + tn], tl[:, :tn], reads=[tk])
            elif ci < 80:
                tl, tk = rb2.next()
                S.op("dve", lambda e: e.tensor_copy(tl[:, :tn], ps[:, :tn]), reads=[pk], writes=[tk])
                S.dma("sp", ctx.xbcraw[ci - 32, :, t0:t0 + tn], tl[:, :tn], reads=[tk])
            else:
                tl, tk = rf.next()
                S.op("dve", lambda e: e.tensor_copy(tl[:, :tn], ps[:, :tn]), reads=[pk], writes=[tk])
                S.dma("sp", ctx.dtraw[:, t0:t0 + tn], tl[:, :tn], reads=[tk])
        chunks = [(c * 128, 128) for c in range(81)]
        gemm(ctx, ctx.w["ssd_w_in"][0], DC, simple_blocks(chunks, 512), fmv(ctx.hT), [(0, T)], epi, Wc=512)
    with contextlib.ExitStack() as st:
        rr = Ring(ctx, st, "cvr", [128, TP], BF16, 3)
        orr = Ring(ctx, st, "cvo", [128, TP], BF16, 3)
        dgr = Ring(ctx, st, "cvd", [128, 7, 128], BF16, 2)
        for i in range(3):
            S.op("dve", lambda e, t=rr.tiles[i]: e.memset(t[:], 0.0), writes=[rr.keys[i]] + [rr.keys[i] + "s%d" % j for j in range(3)])
        R0, R1 = 3, TP - 3
        for c in range(48):
            rt, rk = rr.next(); ot, ok = orr.next(); dg, dgk = dgr.next()
            lk = []
            for si, (a, n) in enumerate(SEQS):
                S.dma("sp", rt[:, PADOFF[si]:PADOFF[si] + n], ctx.xbcraw[c, :, a:a + n], writes=[rk + "s%d" % si])
                lk.append(rk + "s%d" % si)
            for w in range(7):
                S.op("dve", lambda e, dg=dg, w=w, c=c: e.tensor_scalar(dg[:, w, :], ctx.ident_bf, pvap(ctx, "conv_w", c * 7 + w), None, ALU.mult),
                     reads=["const", "pv"], writes=[dgk])
            for s0 in range(R0, R1, 512):
                sn = min(512, R1 - s0)
                ps, pk = psum_next(ctx, 0, 8)
                for w in range(7):
                    S.op("pe", lambda e, ps=ps, dg=dg, rt=rt, w=w, s0=s0, sn=sn:
                         e.matmul(ps[:, :sn], dg[:, w, :], rt[:, s0 + w - 3:s0 + w - 3 + sn], start=(w == 0), stop=(w == 6)),
                         reads=lk + [rk, dgk], writes=[pk])
                S.op("act", lambda e, ot=ot, ps=ps, c=c, s0=s0, sn=sn:
                     e.activation(ot[:, s0:s0 + sn], ps[:, :sn], AF.Silu, bias=pvap(ctx, "conv_b", c)),
                     reads=[pk, "pv"], writes=[ok])
            for si, (a, n) in enumerate(SEQS):
                dst = ctx.xcT[c, :, a:a + n] if c < 32 else ctx.bcT[c - 32, :, a:a + n]
                S.dma("sp", dst, ot[:, PADOFF[si]:PADOFF[si] + n], reads=[ok])
        ctx.flush()
    with contextlib.ExitStack() as st:
        E = st.enter_context
        raw = E(nc.sbuf_tensor("dtr_%d" % uid(), [128, T], F32))
        sp = E(nc.sbuf_tensor("dts_%d" % uid(), [128, T], F32))
        da = E(nc.sbuf_tensor("dta_%d" % uid(), [128, T], F32))
        ea = E(nc.sbuf_tensor("dte_%d" % uid(), [128, 2], F32))
        tk_ = E(nc.sbuf_tensor("dtt_%d" % uid(), [128, 2, 20, 128], F32))
        S.dma("sp", raw[:], ctx.dtraw, writes=["dtr"])
        S.op("act", lambda e: e.activation(raw[:], raw[:], AF.Exp, bias=pvap(ctx, "dt_bias", 0)), reads=["dtr", "pv"], writes=["dtr"])
        S.op("act", lambda e: e.activation(sp[:], raw[:], AF.Ln, bias=ctx.one_ap), reads=["dtr", "const"], writes=["dts"])
        S.op("act", lambda e: e.activation(ea[:, 0:1], pvap(ctx, "a_log", 0), AF.Exp), reads=["pv"], writes=["dte"])
        S.op("dve", lambda e: e.tensor_scalar(da[:], sp[:], ea[:, 0:1], -1.0, ALU.mult, ALU.mult), reads=["dts", "dte"], writes=["dta"])
        for which, (src, sk) in enumerate(((sp, "dts"), (da, "dta"))):
            for c4 in range(0, 20, 4):
                ps, pk = psum_next(ctx, 0, 8)
                for c in range(c4, c4 + 4):
                    S.op("pe", lambda e, ps=ps, src=src, c=c, c4=c4:
                         e.transpose(ps[:, (c - c4) * 128:(c - c4 + 1) * 128], src[:, c * 128:(c + 1) * 128], ctx.ident_f32),
                         reads=[sk, "constf"], writes=[pk])
                S.op("dve", lambda e, ps=ps, which=which, c4=c4:
                     e.tensor_copy(tk_[:, which, c4:c4 + 4, :].rearrange("p c j -> p (c j)"), ps[:, :512]), reads=[pk], writes=["dtt"])
        S.dma("sp", ctx.dttok, tk_[:], reads=["dtt"])
        ctx.flush()
    vtrans_phase(ctx, fmv(ctx.xcT)[:, 0:16, :], 16, T, ctx.xtok, col_off=0)
    vtrans_phase(ctx, fmv(ctx.xcT)[:, 16:32, :], 16, T, ctx.xtok, col_off=2048)
    vtrans_phase(ctx, fmv(ctx.bcT)[:, 0:8, :], 8, T, ctx.btok)
    ssd_scan(ctx)
    with contextlib.ExitStack() as st:
        TN = 256
        yr = Ring(ctx, st, "gny", [128, 32, TN], BF16, 2)
        zr = Ring(ctx, st, "gnz", [128, 32, TN], BF16, 2)
        gr = Ring(ctx, st, "gng", [128, 32, TN], F32, 1)
        sr = Ring(ctx, st, "gns", [128, 32, TN], BF16, 1)
        orr = Ring(ctx, st, "gno", [128, 32, TN], BF16, 2)
        rsr = Ring(ctx, st, "gnr", [128, TN], F32, 2)
        for t0 in range(0, T, TN):
            yt, yk = yr.next(); zt, zk = zr.next(); gt, gk = gr.next(); sq, sqk = sr.next(); ot, ok = orr.next(); rs, rk = rsr.next()
            S.dma("sp", yt[:], fmv(ctx.yT)[:, :, t0:t0 + TN], writes=[yk])
            S.dma("sp", zt[:], fmv(ctx.szT)[:, :, t0:t0 + TN], writes=[zk])
            S.op("dve", lambda e, gt=gt, yt=yt, zt=zt: e.tensor_tensor(gt[:], yt[:], zt[:], ALU.mult), reads=[yk, zk], writes=[gk])
            S.op("act", lambda e, sq=sq, gt=gt: e.activation(sq[:], gt[:], AF.Square), reads=[gk], writes=[sqk])
            ps, pk = psum_next(ctx, 6, 8)
            for c in range(32):
                S.op("pe", lambda e, ps=ps, sq=sq, c=c: e.matmul(ps[:, :TN], ctx.ones_bf, sq[:, c, :], start=(c == 0), stop=(c == 31)),
                     reads=[sqk, "const"], writes=[pk])
            S.op("act", lambda e, rs=rs, ps=ps: e.activation(rs[:], ps[:, :TN], AF.Sqrt, bias=ctx.eps_ap, scale=1.0 / 4096),
                 reads=[pk, "const"], writes=[rk])
            S.op("dve", lambda e, rs=rs: e.reciprocal(rs[:], rs[:]), reads=[rk], writes=[rk])
            for c in range(32):
                eng = "dve"
                S.op(eng, lambda e, ot=ot, gt=gt, rs=rs, c=c:
                     e.scalar_tensor_tensor(ot[:, c, :], gt[:, c, :], pvap(ctx, "ssd_nw", c), rs[:], ALU.mult, ALU.mult),
                     reads=[gk, rk, "pv"], writes=[ok + eng])
            S.dma("sp", fmv(ctx.ygT)[:, :, t0:t0 + TN], ot[:], reads=[ok + "dve"])
        ctx.flush()
    wo_phase(ctx, ctx.w["ssd_w_out"][0], 32, ctx.ygT)


def ssd_scan(ctx):
    S = ctx.S
    nc = ctx.nc
    with contextlib.ExitStack() as st:
        E = st.enter_context
        cst = E(nc.sbuf_tensor("sscst_%d" % uid(), [128, 640], F32))
        S.dma("sp", cst[:], ctx.c["ssdc"][:, 0:640], writes=["sscst"])
        U = [cst[:, 0:128], cst[:, 128:256]]
        Tm = [cst[:, 256:384], cst[:, 384:512]]
        dtt = E(nc.sbuf_tensor("ssdtt_%d" % uid(), [128, 2, 20, 128], F32))
        S.dma("sp", dtt[:], ctx.dttok, writes=["ssdtt"])
        BT = E(nc.sbuf_tensor("ssBT_%d" % uid(), [128, T], BF16))
        CT = E(nc.sbuf_tensor("ssCT_%d" % uid(), [128, T], BF16))
        Bk = E(nc.sbuf_tensor("ssBk_%d" % uid(), [128, 20, 128], BF16))
        Xk = E(nc.sbuf_tensor("ssXk_%d" % uid(), [128, 20, 512], BF16))
        ybuf = E(nc.sbuf_tensor("ssyb_%d" % uid(), [64, 8, T], BF16))
        idsk = E(nc.sbuf_tensor("ssid_%d" % uid(), [128, 8, 128], BF16))
        hT = E(nc.sbuf_tensor("sshT_%d" % uid(), [128, 2, 8, 64], F32))
        hTb = E(nc.sbuf_tensor("sshTb_%d" % uid(), [128, 2, 8, 64], BF16))
        aTr = Ring(ctx, st, "ssaT", [128, 8, 128], F32, 3)
        Lmr = Ring(ctx, st, "ssLm", [128, 8, 128], F32, 3)
        Edr = Ring(ctx, st, "ssEd", [128, 8, 128], F32, 3)
        cbr = Ring(ctx, st, "sscb", [128, 1, 128], F32, 3)
        MTr = Ring(ctx, st, "ssMT", [128, 128], BF16, 10)
        Cdr = Ring(ctx, st, "ssCd", [128, 128], BF16, 10)
        Bwr = Ring(ctx, st, "ssBw", [128, 128], BF16, 10)
        w8r = Ring(ctx, st, "ssw8", [128, 8], F32, 3)
        PS = lambda i: (ctx.ps[i], "ps%d" % i)

        def group_setup(g):
            S.dma("sp", BT[:], ctx.bcT[g], writes=["ssBT"])
            S.dma("sp", CT[:], ctx.bcT[8 + g], writes=["ssCT"])
            S.dma("sp", Bk[:], ctx.btok[:, g * 128:(g + 1) * 128].rearrange("(c p) n -> p c n", p=128), writes=["ssBk"])
            S.dma("sp", Xk[:], ctx.xtok[:, g * 512:(g + 1) * 512].rearrange("(c p) n -> p c n", p=128), writes=["ssXk"])
            for hh in range(8):
                S.op("dve", lambda e, hh=hh: e.tensor_scalar(idsk[:, hh, :], ctx.ident_bf, pvap(ctx, "ssd_d", g * 8 + hh), None, ALU.mult),
                     reads=["const", "pv"], writes=["ssid"])

        def stage1(it):
            g, si, dr, c = it["g"], it["si"], it["dr"], it["c"]
            tsl = slice(c * 128, (c + 1) * 128)
            cps, cpk = PS(2)
            cb, cbk = cbr.next()
            S.op("pe", lambda e: e.matmul(cps[:, 0:128], BT[:, tsl], CT[:, tsl], start=True, stop=True),
                 reads=["ssBT", "ssCT"], writes=[cpk])
            S.op("dve", lambda e: e.tensor_tensor(cb[:, 0, :], cps[:, 0:128], Tm[dr], ALU.mult), reads=[cpk, "sscst"], writes=[cbk])
            aT, aTk = aTr.next()
            for hh in range(8):
                col = dr * 64 + g * 8 + hh
                S.op("act", lambda e, hh=hh, col=col: e.activation(aT[:, hh, :], Tm[dr], AF.Identity, scale=dtt[:, 1, c, col:col + 1]),
                     reads=["sscst", "ssdtt"], writes=[aTk])
            Lm, Lmk = Lmr.next()
            Ed, Edk = Edr.next()
            for half in range(2):
                sps, spk = PS(0 if half == 0 else 7)
                aps, apk = PS(1 if half == 0 else 0)
                rhs = aT[:, half * 4:(half + 1) * 4, :].rearrange("p h q -> p (h q)")
                S.op("pe", lambda e, sps=sps, rhs=rhs: e.matmul(sps[:, :512], U[dr], rhs, start=True, stop=True),
                     reads=[aTk, "sscst"], writes=[spk])
                S.op("pe", lambda e, aps=aps, rhs=rhs: e.matmul(aps[:, :512], cst[:, 512:640], rhs, start=True, stop=True),
                     reads=[aTk, "sscst"], writes=[apk])
                S.op("act", lambda e, sps=sps, half=half:
                     e.activation(Lm[:, half * 4:(half + 1) * 4, :].rearrange("p h q -> p (h q)"), sps[:, :512], AF.Exp),
                     reads=[spk], writes=[Lmk])
                S.op("act", lambda e, aps=aps, half=half:
                     e.activation(Ed[:, half * 4:(half + 1) * 4, :].rearrange("p h q -> p (h q)"), aps[:, :512], AF.Exp),
                     reads=[apk], writes=[Edk])
            return dict(cb=cb, cbk=cbk, Lm=Lm, Lmk=Lmk, Ed=Ed, Edk=Edk)

        ykeys = []
        sbank_ = [0]

        def stage2(it, s1):
            g, si, dr, c = it["g"], it["si"], it["dr"], it["c"]
            tsl = slice(c * 128, (c + 1) * 128)
            cb, cbk, Lm, Lmk, Ed, Edk = s1["cb"], s1["cbk"], s1["Lm"], s1["Lmk"], s1["Ed"], s1["Edk"]
            qe = 127 if dr == 0 else 0
            hk, hbk = "sshT%d" % dr, "sshTb%d" % dr
            if it["seq_start"]:
                if si == 0:
                    for d_ in range(2):
                        S.dma("sp", hT[:, d_, :, :], ctx.c["ssd_stateT"][d_, g * 8:(g + 1) * 8].rearrange("h n p -> n h p"),
                              writes=["sshT%d" % d_])
                    S.op("act", lambda e: e.activation(hTb[:].rearrange("p d h q -> p (d h q)"), hT[:].rearrange("p d h q -> p (d h q)"), AF.Identity),
                         reads=["sshT0", "sshT1"], writes=["sshTb0", "sshTb1"])
                else:
                    S.op("dve", lambda e: e.memset(hT[:], 0.0), writes=["sshT0", "sshT1"])
                    S.op("dve", lambda e: e.memset(hTb[:], 0.0), writes=["sshTb0", "sshTb1"])
            yb = [PS(5), PS(6)]
            sbank_[0] ^= 1
            sps7, spk7 = PS(3 + sbank_[0])
            w8, w8k = w8r.next()
            c0_ = dr * 64 + g * 8
            S.op("dve", lambda e: e.tensor_tensor(w8[:], Lm[:, :, qe], dtt[:, 0, c, c0_:c0_ + 8], ALU.mult),
                 reads=[Lmk, "ssdtt"], writes=[w8k])
            per_head = []
            for hh in range(8):
                col = dr * 64 + g * 8 + hh
                dcol = dtt[:, 0, c, col:col + 1]
                MT, MTk = MTr.next()
                Cd, Cdk = Cdr.next()
                Bw, Bwk = Bwr.next()
                S.op("dve", lambda e, MT=MT, hh=hh, dcol=dcol:
                     e.scalar_tensor_tensor(MT[:], Lm[:, hh, :], dcol, cb[:, 0, :], ALU.mult, ALU.mult),
                     reads=[Lmk, cbk, "ssdtt"], writes=[MTk])
                S.op("act", lambda e, Bw=Bw, hh=hh: e.activation(Bw[:], Bk[:, c, :], AF.Identity, scale=w8[:, hh:hh + 1]),
                     reads=["ssBk", w8k], writes=[Bwk])
                per_head.append((MT, MTk, Cd, Cdk, Bw, Bwk))
            for hh in range(8):
                Cd, Cdk = per_head[hh][2], per_head[hh][3]
                S.op("dve", lambda e, Cd=Cd, hh=hh: e.tensor_tensor(Cd[:], CT[:, tsl], Ed[:, hh, :], ALU.mult),
                     reads=["ssCT", Edk], writes=[Cdk])
            for hh in range(8):
                MT, MTk, Cd, Cdk, Bw, Bwk = per_head[hh]
                ybk = yb[hh // 4]
                yreg = ybk[0][0:64, (hh % 4) * 128:(hh % 4 + 1) * 128]
                xs = Xk[:, c, hh * 64:(hh + 1) * 64]
                S.op("pe", lambda e, Bw=Bw, xs=xs, hh=hh: e.matmul(sps7[:, hh * 64:(hh + 1) * 64], Bw[:], xs, start=True, stop=True),
                     reads=[Bwk, "ssXk"], writes=[spk7])
                S.op("pe", lambda e, yreg=yreg, xs=xs, MT=MT: e.matmul(yreg, xs, MT[:], start=True, stop=False),
                     reads=["ssXk", MTk], writes=[ybk[1]])
                if dr == 0:
                    S.op("pe", lambda e, yreg=yreg, xs=xs, hh=hh: e.matmul(yreg, xs, idsk[:, hh, :], start=False, stop=False),
                         reads=["ssXk", "ssid"], writes=[ybk[1]])
                S.op("pe", lambda e, yreg=yreg, hh=hh, Cd=Cd: e.matmul(yreg, hTb[:, dr, hh, :], Cd[:], start=False, stop=True),
                     reads=[hbk, Cdk], writes=[ybk[1]])
            for half in range(2):
                yk = "ssyb%d_%d" % (c, half)
                ydst = ybuf[:, half * 4:(half + 1) * 4, tsl]
                ysrc = yb[half][0][0:64, :].rearrange("p (h q) -> p h q", h=4)
                if dr == 0:
                    S.op("act", lambda e, ydst=ydst, ysrc=ysrc: e.activation(ydst, ysrc, AF.Identity), reads=[yb[half][1]], writes=[yk])
                else:
                    S.op("dve", lambda e, ydst=ydst, ysrc=ysrc: e.tensor_tensor(ydst, ysrc, ydst, ALU.add), reads=[yb[half][1], yk], writes=[yk])
                    ykeys.append(yk)
            s1["sps7"], s1["spk7"] = sps7, spk7

        def stage2b(it, s1):
            g, si, dr, c = it["g"], it["si"], it["dr"], it["c"]
            Ed, Edk, sps7, spk7 = s1["Ed"], s1["Edk"], s1["sps7"], s1["spk7"]
            qe = 127 if dr == 0 else 0
            hk, hbk = "sshT%d" % dr, "sshTb%d" % dr
            for hh in range(8):
                S.op("dve", lambda e, hh=hh:
                     e.scalar_tensor_tensor(hT[:, dr, hh, :], hT[:, dr, hh, :], Ed[:, hh, qe:qe + 1], sps7[:, hh * 64:(hh + 1) * 64],
                                            ALU.mult, ALU.add),
                     reads=[hk, Edk, spk7], writes=[hk])
            S.op("act", lambda e: e.activation(hTb[:, dr, :, :].rearrange("p h q -> p (h q)"),
                                               hT[:, dr, :, :].rearrange("p h q -> p (h q)"), AF.Identity),
                 reads=[hk], writes=[hbk])
            if it["sweep_end"] and si > 0:
                S.dma("sp", ctx.out["new_stateT"][si - 1, dr, g * 8:(g + 1) * 8].rearrange("h n p -> n h p"), hT[:, dr, :, :], reads=[hk])
            if it["group_end"]:
                for hh in range(8):
                    S.dma("sp", ctx.yT[g * 4 + hh // 2, (hh % 2) * 64:(hh % 2) * 64 + 64, :], ybuf[:, hh, :], reads=list(ykeys))
                del ykeys[:]

        iters = []
        for g in range(8):
            for si, (a0, n) in enumerate(SEQS):
                c_lo, c_hi = a0 // 128, (a0 + n) // 128
                for dr in range(2):
                    order = list(range(c_lo, c_hi)) if dr == 0 else list(range(c_hi - 1, c_lo - 1, -1))
                    for j, c in enumerate(order):
                        iters.append(dict(g=g, si=si, dr=dr, c=c, seq_start=(dr == 0 and j == 0), sweep_end=(j == len(order) - 1),
                                          group_start=(si == 0 and dr == 0 and j == 0), group_end=False))
            iters[-1]["group_end"] = True
        for g in range(8):
            its = [it for it in iters if it["g"] == g]
            group_setup(g)
            n = len(its)
            s1 = {0: stage1(its[0])}
            if n > 1:
                s1[1] = stage1(its[1])
            for i in range(n):
                stage2(its[i], s1[i])
                if i + 2 < n:
                    s1[i + 2] = stage1(its[i + 2])
                stage2b(its[i], s1.pop(i))
        ctx.flush()


def build(cfg):
    nc = bass.Bass("TRN2", target_bir_lowering=False)
    ctx = Ctx()
    ctx.nc = nc
    kinds = cfg["kinds"]

    def din(name, shape, dt=F32):
        return nc.dram_tensor(name, list(shape), dt, kind="ExternalInput").ap()

    def dout(name, shape, dt=F32):
        return nc.dram_tensor(name, list(shape), dt, kind="ExternalOutput").ap()

    def dscr(name, shape, dt):
        return nc.dram_tensor(name, list(shape), dt, kind="Internal").ap()

    xT_in = din("xT_in", [DC, 128, T])
    pvec = din("pvec", [128, cfg["npv"]])
    cbf = din("cbf", [128, NCBF], BF16)
    ctx.w = {}
    for name, shape in cfg["wshapes"].items():
        ctx.w[name] = din(name, shape)
    ctx.c = {}
    for name, shape in cfg["cshapes"].items():
        ctx.c[name] = din(name, shape, BF16 if cfg["cdtypes"][name] == "bfloat16" else F32)
    ctx.out = {}
    for name, shape in cfg["oshapes"].items():
        ctx.out[name] = dout(name, shape)
    yT = dout("yT", [DC, 128, T])
    ctx.xT = dscr("xT", [DC, 128, T], F32)
    ctx.hT = dscr("hT", [DC, 128, T], BF16)
    ctx.aT = dscr("aT", [FC, 128, T], BF16)
    ctx.condT = dscr("condT", [128, DC, 2], BF16)
    ctx.qkvT = dscr("qkvT", [48, 128, T], BF16)
    ctx.vtok = dscr("vtok", [T + 512, 2048], BF16)
    ctx.oacc = dscr("oacc", [16, 256, T], F32)
    ctx.oT = dscr("oT", [DC, 128, T], BF16)
    if 3 in kinds:
        ctx.szT = dscr("szT", [32, 128, T], BF16)
        ctx.xbcraw = dscr("xbcraw", [48, 128, T], BF16)
        ctx.dtraw = dscr("dtraw", [128, T], F32)
        ctx.xcT = dscr("xcT", [32, 128, T], BF16)
        ctx.bcT = dscr("bcT", [16, 128, T], BF16)
        ctx.dttok = dscr("dttok", [128, 2, 20, 128], F32)
        ctx.xtok = dscr("xtok", [T, 4096], BF16)
        ctx.btok = dscr("btok", [T, 1024], BF16)
        ctx.yT = dscr("yT_ssd", [32, 128, T], BF16)
        ctx.ygT = dscr("ygT", [32, 128, T], BF16)
    if 2 in kinds:
        ctx.cqT = dscr("cqT", [4, 128, T], F32)
        ctx.cqnT = dscr("cqnT", [4, 128, T], BF16)
        ctx.mlaq = dscr("mlaq", [16, 192, T], BF16)
        ctx.ckvraw = dscr("ckvraw", [2, 128, T], F32)
        ctx.ckvT = dscr("ckvT", [2, 128, T + 512], BF16)
        ctx.krT = dscr("krT", [64, T + 512], BF16)
        ctx.mlakv = dscr("mlakv", [32, 128, T + 512], BF16)
    ctx.pvcols = cfg["pvcols"]

    with contextlib.ExitStack() as st:
        E = st.enter_context
        ctx.S = Sched(nc, st)
        S = ctx.S
        S.max_flush = cfg.get("max_flush")
        ctx.flush = S.flush
        ctx.pvt = E(nc.sbuf_tensor("pvt", [128, cfg["npv"]], F32))
        ctx.cbt = E(nc.sbuf_tensor("cbt", [128, NCBF], BF16))
        ctx.ones_bf = ctx.cbt[:, 0:128]
        ctx.ident_bf = ctx.cbt[:, 128:256]
        ctx.R128 = ctx.cbt[:, 256:384]
        ctx.R64 = ctx.cbt[0:64, 384:448]
        ctx.modt = E(nc.sbuf_tensor("modt", [128, 144, 2], F32))
        ctx.A = E(nc.sbuf_tensor("modA", [128, 3, 16, 2], F32))
        ctx.B = E(nc.sbuf_tensor("modB", [128, 3, 16, 2], F32))
        ctx.G = E(nc.sbuf_tensor("modG", [128, 3, 16, 2], F32))
        ctx.epst = E(nc.sbuf_tensor("epst", [128, 2], F32))
        ctx.eps_ap = ctx.epst[:, 0:1]
        ctx.one_ap = ctx.epst[:, 1:2]
        S.op("dve", lambda e: e.memset(ctx.epst[:, 0:1], EPS), writes=["const"])
        S.op("dve", lambda e: e.memset(ctx.epst[:, 1:2], 1.0), writes=["const"])
        if 3 in kinds:
            ctx.idf = E(nc.sbuf_tensor("identf", [128, 128], F32))
            ctx.ident_f32 = ctx.idf[:]
            S.dma("sp", ctx.idf[:], ctx.c["ssdc"][:, 640:768], writes=["constf"])
        ctx.ps = [E(nc.psum_tensor("psb%d" % i, [128, 512], F32)) for i in range(8)]
        ctx.ps_i = 0

        S.dma("sp", ctx.pvt[:], pvec, writes=["pv"])
        S.dma("sp", ctx.cbt[:], cbf, writes=["const"])
        with contextlib.ExitStack() as st2:
            ct = st2.enter_context(nc.sbuf_tensor("condt", [128, 32], BF16))
            S.op("act", lambda e: e.activation(ct[:], pvap(ctx, "cond", 0, 32), AF.Silu), reads=["pv"], writes=["ct"])
            S.dma("sp", ctx.condT, ct[:].rearrange("p (c t) -> p c t", t=2), reads=["ct"])
            xr = Ring(ctx, st2, "px", [128, DC, 512], F32, 2)
            for t0 in range(0, T, 512):
                xt, xk = xr.next()
                S.dma("sp", xt[:], xT_in[:, :, t0:t0 + 512].rearrange("c p t -> p c t"), writes=[xk])
                S.dma("sp", ctx.xT[:, :, t0:t0 + 512].rearrange("c p t -> p c t"), xt[:], reads=[xk])
            S.flush()

        for layer, kind in enumerate(kinds):
            mod_phase(ctx, layer)
            ffn_phase(ctx, layer, 0)
            lidx = cfg.get("layer_index", list(range(len(kinds))))[layer]
            if kind == 0:
                diff_layer(ctx, lidx)
            elif kind == 1:
                na_layer(ctx, lidx)
            elif kind == 2:
                mla_layer(ctx, lidx)
            elif kind == 3:
                ssd_layer(ctx, lidx)
            ffn_phase(ctx, layer, 1)

        rms_phase(ctx, ctx.xT, DC, D, yT, lambda c, cond: pvap(ctx, "final_w", c), None, out_f32=True)
    return nc, ctx


NCBF = 448
WNAMES = ["ada_w", "ffn_w_gu", "ffn_w_down", "diff_w_qkv", "diff_w_o"]


def rope_tables(d):
    hd = d // 2
    hh = hd // 2
    t = np.arange(2048)
    tab = np.zeros((2, d, 2048), np.float32)
    R = np.zeros((d, d), np.float32)
    for p in range(d):
        half, i = p // hd, p % hd
        j, first = i % hh, i < hh
        inv = np.float32(10000.0) ** (-np.float32(j) / np.float32(hh))
        pos = (t // 64) if half == 0 else (t % 64)
        ang = pos.astype(np.float32) * np.float32(inv)
        tab[0, p] = np.cos(ang)
        tab[1, p] = -np.sin(ang) if first else np.sin(ang)
        partner = p + hh if first else p - hh
        R[partner, p] = 1.0
    return tab, R


_CONST_CACHE = {}


def host_consts():
    if "cb" not in _CONST_CACHE:
        cb = np.zeros((128, NCBF), dtype=np.float32)
        cb[:, 0:128] = 1.0
        cb[:, 128:256] = np.eye(128, dtype=np.float32)
        t128, R128 = rope_tables(128)
        t64, R64 = rope_tables(64)
        cb[:, 256:384] = R128
        cb[0:64, 384:448] = R64
        _CONST_CACHE["cb"] = cb.astype(ml_dtypes.bfloat16)
        _CONST_CACHE["rope128"] = t128
        _CONST_CACHE["rope64"] = t64
    return _CONST_CACHE


def ssd_consts():
    k = np.arange(128)[:, None]
    j = np.arange(128)[None, :]
    c = np.zeros((128, 768), np.float32)
    c[:, 0:128] = (k > j)
    c[:, 128:256] = (k < j)
    c[:, 256:384] = (k <= j)
    c[:, 384:512] = (k >= j)
    c[:, 512:640] = 1.0
    c[:, 640:768] = np.eye(128)
    return c


def make_inputs(inp, core, kinds):
    xs = inp["x_sample"][core]
    xp = inp["x_prompt"][2 * core:2 * core + 2].reshape(512, D)
    x = np.concatenate([xs, xp], 0)
    xT = np.ascontiguousarray(x.T).reshape(DC, 128, T)
    pv = pv_layout(inp, core, kinds)
    hc = host_consts()
    m = {"xT_in": xT, "pvec": pv.build(), "cbf": hc["cb"]}
    wn = ["ada_w", "ffn_w_gu", "ffn_w_down"]
    cs = {}
    outs = {}
    if 0 in kinds:
        wn += ["diff_w_qkv", "diff_w_o"]
        cs["rope128"] = hc["rope128"]
        ck = inp["cache_diff_k"][core, 0]
        cs["cache_diff_kT"] = np.ascontiguousarray(ck.transpose(1, 2, 3, 0)).reshape(16, 128, 512)
        cs["cache_diff_v"] = np.ascontiguousarray(inp["cache_diff_v"][core, 0].reshape(512, 2048))
        outs["new_diff_kT"] = (16, 128, 512)
        outs["new_diff_vT"] = (16, 128, 512)
    if 1 in kinds:
        wn += ["na_w_qkv", "na_w_o"]
        if "na_tabs" not in hc or hc.get("na_rpb_id") is not inp["na_rpb"]:
            hc["na_tabs"] = na_host_tables(inp["na_rpb"][0])
            hc["na_rpb_id"] = inp["na_rpb"]
        cs["na_bias"], cs["na_mask"] = hc["na_tabs"]
        ck = inp["cache_na_k"][core, 0]
        cs["cache_na_kT"] = np.ascontiguousarray(ck.transpose(1, 2, 0))
        cs["cache_na_v"] = np.ascontiguousarray(inp["cache_na_v"][core, 0].reshape(512, 2048))
        outs["new_na_kT"] = (16, 128, 512)
        outs["new_na_vT"] = (16, 128, 512)
    if 2 in kinds:
        wn += ["mla_w_dq", "mla_w_uq", "mla_w_dkv", "mla_w_ukv", "mla_w_o"]
        cs["rope64"] = hc["rope64"]
        cs["cache_mla_ckvT"] = np.ascontiguousarray(inp["cache_mla_ckv"][core, 0].T).reshape(2, 128, 512)
        cs["cache_mla_kropeT"] = np.ascontiguousarray(inp["cache_mla_krope"][core, 0].T)
        outs["new_mla_ckvT"] = (2, 128, 512)
        outs["new_mla_kropeT"] = (64, 512)
    if 3 in kinds:
        wn += ["ssd_w_in", "ssd_w_out"]
        cs["ssdc"] = ssd_consts()
        cs["ssd_stateT"] = np.ascontiguousarray(inp["state_ssd"][core, 0].transpose(0, 1, 3, 2))
        outs["new_stateT"] = (2, 2, 64, 128, 64)
    for n in wn:
        m[n] = inp[n]
    m.update(cs)
    cfg = dict(npv=pv.n, pvcols=pv.cols, kinds=kinds, wshapes={k: inp[k].shape for k in wn},
               cshapes={k: v.shape for k, v in cs.items()}, cdtypes={k: str(v.dtype) for k, v in cs.items()}, oshapes=outs)
    return m, cfg


def kernel(**inp):
    inp = {k: np.asarray(v) for k, v in inp.items()}
    n = 8
    kinds = [0, 1, 2, 3]
    maps = []
    cfg = None
    for core in range(n):
        m, cfg = make_inputs(inp, core, kinds)
        maps.append(m)
    nc, ctx = build(cfg)
    res = run_bass_kernel_spmd(nc, maps, core_ids=list(range(n)))
    return assemble(res.results, kinds)


def assemble(results, kinds):
    n = len(results)
    ys = np.zeros((n, 2048, D), np.float32)
    yp = np.zeros((2 * n, 256, D), np.float32)
    o = {}
    for core in range(n):
        r = results[core]
        y = r["yT"].reshape(D, T).T
        ys[core] = y[:TS]
        yp[2 * core:2 * core + 2] = y[TS:].reshape(2, 256, D)

        def tokmajor(name, nf):
            return r[name].reshape(nf, 512).T.reshape(2, 256, nf)
        if 0 in kinds:
            o.setdefault("dk", []).append(tokmajor("new_diff_kT", 2048).reshape(2, 1, 256, 8, 2, 128))
            o.setdefault("dv", []).append(tokmajor("new_diff_vT", 2048).reshape(2, 1, 256, 8, 256))
        if 1 in kinds:
            o.setdefault("nk", []).append(tokmajor("new_na_kT", 2048).reshape(2, 1, 256, 16, 128))
            o.setdefault("nv", []).append(tokmajor("new_na_vT", 2048).reshape(2, 1, 256, 16, 128))
        if 2 in kinds:
            o.setdefault("ckv", []).append(tokmajor("new_mla_ckvT", 256).reshape(2, 1, 256, 256))
            o.setdefault("kr", []).append(tokmajor("new_mla_kropeT", 64).reshape(2, 1, 256, 64))
        if 3 in kinds:
            o.setdefault("hs", []).append(np.ascontiguousarray(r["new_stateT"].reshape(2, 1, 2, 64, 128, 64).transpose(0, 1, 2, 3, 5, 4)))
    res = [yp, ys]
    for k in ("dk", "dv", "nk", "nv", "ckv", "kr", "hs"):
        if k in o:
            res.append(np.concatenate(o[k], 0))
    return tuple(res)
```

```python
import contextlib
import numpy as np
import ml_dtypes
import concourse.bass as bass
import concourse.mybir as mybir
from concourse.bass_utils import run_bass_kernel_spmd

F32 = mybir.dt.float32
BF16 = mybir.dt.bfloat16
AF = mybir.ActivationFunctionType
ALU = mybir.AluOpType

D = 2048
DC = 16
T = 2560
TS = 2048
DFF = 5632
FC = 44
EPS = 1e-6
DEPTH = 4

ENGS = ("pe", "act", "dve", "pool", "sp")
N_DMA_SEMS = 12


class _Op:
    __slots__ = ("eng", "fn", "deps", "dma", "signal", "sem", "val")

    def __init__(self, eng, fn, dma):
        self.eng, self.fn, self.dma = eng, fn, dma
        self.deps = []
        self.signal = dma
        self.sem = None
        self.val = 0


class Sched:
    def __init__(self, nc, stack):
        self.nc = nc
        self.esem = {e: stack.enter_context(nc.semaphore("s_" + e)) for e in ENGS}
        self.ecount = {e: 0 for e in ENGS}
        qs = ("sp", "pool", "act")
        self.dsem = {q: [stack.enter_context(nc.semaphore("d_%s%d" % (q, i)))
                         for i in range(N_DMA_SEMS)] for q in qs}
        self.dcount = {q: [0] * N_DMA_SEMS for q in qs}
        self.dlast = {q: [None] * N_DMA_SEMS for q in qs}
        self.drr = {q: 0 for q in qs}
        self.waited = {e: {} for e in ENGS}
        self.ops = []
        self.lastw = {}
        self.readers = {}
        self.n_ins = 0

    def op(self, eng, fn, reads=(), writes=(), dma=False):
        o = _Op(eng, fn, dma)
        deps = set()
        for r in reads:
            w = self.lastw.get(r)
            if w is not None:
                deps.add(w)
            if isinstance(r, str) and r.startswith("ps"):
                for rd in self.readers.get(r, ()):
                    if rd.eng != eng:
                        deps.add(rd)
        for w_ in writes:
            w = self.lastw.get(w_)
            if w is not None:
                deps.add(w)
            for rd in self.readers.get(w_, ()):
                deps.add(rd)
        if dma:
            q = eng
            i = self.drr[q]
            self.drr[q] = (i + 1) % N_DMA_SEMS
            prev = self.dlast[q][i]
            if prev is not None:
                deps.add(prev)
            self.dcount[q][i] += 16
            o.sem, o.val = self.dsem[q][i], self.dcount[q][i]
            self.dlast[q][i] = o
        for d in deps:
            if d is o:
                continue
            if (not d.dma) and (not dma) and d.eng == "pe" and eng == "pe":
                continue
            o.deps.append(d)
            d.signal = True
        for r in reads:
            self.readers.setdefault(r, []).append(o)
        for w_ in writes:
            self.lastw[w_] = o
            self.readers[w_] = []
        self.ops.append(o)
        return o

    def dma(self, q, out, in_, reads=(), writes=()):
        return self.op(q, lambda e: e.dma_start(out=out, in_=in_), reads, writes, dma=True)

    def flush(self):
        nc = self.nc
        ops = self.ops
        self.ops = []
        if not ops:
            return
        self.nflush = getattr(self, "nflush", 0) + 1
        mf = getattr(self, "max_flush", None)
        if mf is not None and self.nflush > mf:
            self.lastw = {}
            self.readers = {}
            return
        for o in ops:
            if not o.dma and o.signal:
                self.ecount[o.eng] += 1
                o.sem, o.val = self.esem[o.eng], self.ecount[o.eng]
        by_eng = {e: [o for o in ops if o.eng == e] for e in ENGS}
        for e in ENGS:
            comp = [o for o in by_eng[e] if not o.dma]
            if comp and not comp[-1].signal:
                o = comp[-1]
                o.signal = True
                self.ecount[e] += 1
                o.sem, o.val = self.esem[e], self.ecount[e]
        targets = [(self.esem[e], self.ecount[e]) for e in ENGS if self.ecount[e] > 0]
        for q in self.dsem:
            for i in range(N_DMA_SEMS):
                if self.dcount[q][i] > 0:
                    targets.append((self.dsem[q][i], self.dcount[q][i]))
        for q in self.dlast:
            self.dlast[q] = [None] * N_DMA_SEMS

        def emit(eng_name):
            def body(e):
                wd = self.waited[eng_name]
                for o in by_eng[eng_name]:
                    for d in o.deps:
                        if wd.get(d.sem, 0) < d.val:
                            e.wait_ge(d.sem, d.val)
                            wd[d.sem] = d.val
                    ins = o.fn(e)
                    self.n_ins += 1
                    if o.signal:
                        ins.then_inc(o.sem, 16 if o.dma else 1)
                for s, v in targets:
                    if wd.get(s, 0) < v:
                        e.wait_ge(s, v)
                        wd[s] = v
            return body

        with nc.Block() as block:
            block.tensor(emit("pe"))
            block.scalar(emit("act"))
            block.vector(emit("dve"))
            block.gpsimd(emit("pool"))
            block.sync(emit("sp"))
        self.lastw = {}
        self.readers = {}


_uid = [0]


def uid():
    _uid[0] += 1
    return _uid[0]


class Ring:
    def __init__(self, ctx, st, name, shape, dtype, n):
        u = uid()
        self.tiles = [st.enter_context(ctx.nc.sbuf_tensor("%s_%d_%d" % (name, u, i), shape, dtype))
                      for i in range(n)]
        self.keys = ["%s_%d_%d" % (name, u, i) for i in range(n)]
        self.i = 0
        self.n = n

    def next(self):
        t, k = self.tiles[self.i], self.keys[self.i]
        self.i = (self.i + 1) % self.n
        return t, k


class Ctx:
    pass


class PV:
    def __init__(self):
        self.cols = {}
        self.n = 0
        self.parts = []

    def add(self, name, arr):
        arr = np.ascontiguousarray(arr, dtype=np.float32)
        assert arr.shape[0] == 128
        self.cols[name] = (self.n, arr.shape[1])
        self.n += arr.shape[1]
        self.parts.append(arr)

    def build(self):
        return np.concatenate(self.parts, axis=1)


def fm(v):
    v = np.asarray(v, dtype=np.float32)
    c = v.shape[-1] // 128
    return np.ascontiguousarray(v.reshape(-1, 128).T)


def pv_layout(inp, core, kinds=()):
    pv = PV()
    pv.add("ada_b", fm(inp["ada_b"]))
    pv.add("norm_w", fm(inp["norm_w"]))
    pv.add("final_w", fm(inp["final_norm_w"]))
    cond = np.stack([inp["c"][core], inp["c_ctx"]], 0)
    pv.add("cond", fm(cond).reshape(128, 2, 16).transpose(0, 2, 1).reshape(128, 32))
    if 0 in kinds:
        pv.add("dlam", np.broadcast_to(inp["diff_lambda"][0].reshape(1, 512), (128, 512)))
        pv.add("subln", fm(inp["diff_subln_w"][0]))
    if 3 in kinds:
        cw = np.stack([fm(inp["ssd_conv_w"][0][w]) for w in range(7)], -1)
        pv.add("conv_w", cw.reshape(128, 48 * 7))
        pv.add("conv_b", fm(inp["ssd_conv_b"][0]))
        pv.add("ssd_nw", fm(inp["ssd_norm_w"][0]))
        pv.add("dt_bias", inp["ssd_dt_bias"][0].reshape(128, 1))
        pv.add("a_log", inp["ssd_a_log"][0].reshape(128, 1))
        pv.add("ssd_d", np.broadcast_to(inp["ssd_d"][0].reshape(1, 64), (128, 64)))
    if 2 in kinds:
        pv.add("q_norm", fm(inp["mla_q_norm"][0]))
        pv.add("kv_norm", fm(inp["mla_kv_norm"][0]))
    return pv


def pvap(ctx, name, col, n=1):
    o, k = ctx.pvcols[name]
    assert col + n <= k
    return ctx.pvt[:, o + col:o + col + n]


def fmv(ap):
    return ap.rearrange("c p t -> p c t")


def psum_next(ctx, lo=0, hi=6):
    i = ctx.ps_i
    if i < lo or i >= hi:
        i = lo
    ctx.ps_i = i + 1 if i + 1 < hi else lo
    return ctx.ps[i], "ps%d" % i


def gemm(ctx, W, KC, blocks, src, tok_tiles, epi, Wc, nw=3, tsub=512):
    S = ctx.S
    with contextlib.ExitStack() as st:
        maxT = max(tn for _, tn in tok_tiles)
        wring = Ring(ctx, st, "gw", [128, KC, Wc], BF16, nw)
        sring = Ring(ctx, st, "gs", [128, KC, maxT], BF16, 2 if KC * maxT * 4 <= 70000 else 1)
        for (t0, tn) in tok_tiles:
            stile, sk = sring.next()
            if KC >= 32:
                hkc = KC // 2
                S.dma("sp", stile[:, :hkc, :tn], src[:, :hkc, t0:t0 + tn], writes=[sk + "a"])
                S.dma("act", stile[:, hkc:, :tn], src[:, hkc:, t0:t0 + tn], writes=[sk + "b"])
                sk = [sk + "a", sk + "b"]
            else:
                S.dma("sp", stile[:, :, :tn], src[:, :, t0:t0 + tn], writes=[sk])
                sk = [sk]
            for segs, items in blocks:
                wt, wk0 = wring.next()
                o = 0
                wk = []
                for si, (c0, ncol) in enumerate(segs):
                    wk.append("%s_%d" % (wk0, si))
                    S.dma("pool", wt[:, :, o:o + ncol],
                          W[:, c0:c0 + ncol].rearrange("(c p) n -> p c n", p=128), writes=[wk[-1]])
                    o += ncol
                for (ci, off, w) in items:
                    for s0 in range(0, tn, tsub):
                        sn = min(tsub, tn - s0)
                        ps, pk = psum_next(ctx)
                        for kc in range(KC):
                            S.op("pe", lambda e, ps=ps, wt=wt, stile=stile, kc=kc, off=off, w=w, s0=s0, sn=sn:
                                 e.matmul(ps[:w, :sn], wt[:, kc, off:off + w], stile[:, kc, s0:s0 + sn],
                                          start=(kc == 0), stop=(kc == KC - 1)),
                                 reads=wk + sk, writes=[pk])
                        epi(ci, w, t0 + s0, sn, ps, pk)
        ctx.flush()


def simple_blocks(chunks, Wc):
    blocks = []
    cur = None
    for ci, (c0, w) in enumerate(chunks):
        if cur is not None and cur[0][0][0] + cur[0][0][1] == c0 and cur[0][0][1] + w <= Wc:
            segs, items = cur
            items.append((ci, segs[0][1], w))
            segs[0] = (segs[0][0], segs[0][1] + w)
        else:
            cur = ([(c0, w)], [(ci, 0, w)])
            blocks.append(cur)
    return blocks


def rms_phase(ctx, src, C, nfeat, dst, a_fn, b_fn, tok0=0, tok1=T, out_f32=False, dst2=None, dst2_range=None, dst_off=0):
    S = ctx.S
    with contextlib.ExitStack() as st:
        xr = Ring(ctx, st, "rx", [128, C, 512], F32, 2 if (out_f32 and C > 8) else 3)
        sr = Ring(ctx, st, "rsq", [128, C, 512], BF16, 1)
        hr = Ring(ctx, st, "rh", [128, C, 512], F32 if out_f32 else BF16, 2)
        rr = Ring(ctx, st, "rr", [128, 512], F32, 2)
        tiles = [(t0, min(512, tok1 - t0)) for t0 in range(tok0, tok1, 512)]

        def stage_a(t0, tn):
            xt, xk = xr.next()
            sq, sqk = sr.next()
            rs, rk = rr.next()
            xks = ["%s_%d" % (xk, c) for c in range(C)]
            S.dma("sp", xt[:, :, :tn], src[:, :, t0:t0 + tn].rearrange("c p t -> p c t"), writes=xks)
            S.op("act", lambda e: e.activation(sq[:, :, :tn], xt[:, :, :tn], AF.Square), reads=xks, writes=[sqk])
            ps, pk = psum_next(ctx, 6, 8)
            for c in range(C):
                S.op("pe", lambda e, c=c: e.matmul(ps[:, :tn], ctx.ones_bf[:], sq[:, c, :tn], start=(c == 0), stop=(c == C - 1)),
                     reads=[sqk, "const"], writes=[pk])
            S.op("act", lambda e: e.activation(rs[:, :tn], ps[:, :tn], AF.Sqrt, bias=ctx.eps_ap, scale=1.0 / nfeat),
                 reads=[pk, "const"], writes=[rk])
            S.op("dve", lambda e: e.reciprocal(rs[:, :tn], rs[:, :tn]), reads=[rk], writes=[rk])
            return (xt, xks, rs, rk)

        def stage_b(t0, tn, a_out):
            xt, xks, rs, rk = a_out
            cond = 1 if t0 >= TS else 0
            ht, hk = hr.next()
            hks = ["%s_%d" % (hk, c) for c in range(C)]
            for c in range(C):
                if b_fn is None:
                    S.op("dve", lambda e, c=c: e.scalar_tensor_tensor(ht[:, c, :tn], xt[:, c, :tn], a_fn(c, cond), rs[:, :tn], ALU.mult, ALU.mult),
                         reads=[xks[c], rk, "mod"], writes=[hks[c]])
                else:
                    S.op("dve", lambda e, c=c: e.scalar_tensor_tensor(xt[:, c, :tn], xt[:, c, :tn], a_fn(c, cond), rs[:, :tn], ALU.mult, ALU.mult),
                         reads=[xks[c], rk, "mod"], writes=[xks[c]])
                    S.op("act", lambda e, c=c: e.activation(ht[:, c, :tn], xt[:, c, :tn], AF.Identity, bias=b_fn(c, cond)),
                         reads=[xks[c], "mod"], writes=[hks[c]])
            S.dma("act", dst[:, :, t0 - dst_off:t0 - dst_off + tn].rearrange("c p t -> p c t"), ht[:, :, :tn], reads=hks)

        prev = None
        for (t0, tn) in tiles:
            cur = (t0, tn, stage_a(t0, tn))
            if prev is not None:
                stage_b(*prev)
            prev = cur
        stage_b(*prev)
        ctx.flush()


def resid_epi(ctx, st, gate_fn):
    S = ctx.S
    xr = Ring(ctx, st, "ex", [128, 512], F32, 4)

    def epi(ci, w, t0, tn, ps, pk):
        cond = 1 if t0 >= TS else 0
        xt, xk = xr.next()
        S.dma("sp", xt[:, :tn], ctx.xT[ci, :, t0:t0 + tn], writes=[xk])
        S.op("dve", lambda e: e.scalar_tensor_tensor(xt[:, :tn], ps[:, :tn], gate_fn(ci, cond), xt[:, :tn],
                                                      ALU.mult, ALU.add),
             reads=[pk, xk, "mod"], writes=[xk])
        S.dma("act", ctx.xT[ci, :, t0:t0 + tn], xt[:, :tn], reads=[xk])
    return epi


def mod_phase(ctx, layer):
    S = ctx.S
    W = ctx.w["ada_w"][layer]

    def epi(ci, w, t0, tn, ps, pk):
        S.op("dve", lambda e: e.tensor_scalar(ctx.modt[:, ci, :], ps[:, 0:2], pvap(ctx, "ada_b", layer * 144 + ci), None, ALU.add),
             reads=[pk, "pv"], writes=["modraw"])
    chunks = [(ci * 128, 128) for ci in range(144)]
    gemm(ctx, W, DC, simple_blocks(chunks, 512), ctx.condT, [(0, 2)], epi, Wc=512)
    for k in range(3):
        for cond in range(2):
            nw = pvap(ctx, "norm_w", (layer * 3 + k) * 16, 16)
            sc = ctx.modt[:, k * 48 + 16:k * 48 + 32, cond]
            sh = ctx.modt[:, k * 48 + 0:k * 48 + 16, cond]
            gt = ctx.modt[:, k * 48 + 32:k * 48 + 48, cond]
            S.op("dve", lambda e, k=k, cond=cond, sc=sc, nw=nw:
                 e.scalar_tensor_tensor(ctx.A[:, k, :, cond], sc, 1.0, nw, ALU.add, ALU.mult),
                 reads=["modraw", "pv"], writes=["mod"])
            S.op("dve", lambda e, k=k, cond=cond, sh=sh: e.tensor_copy(ctx.B[:, k, :, cond], sh),
                 reads=["modraw"], writes=["mod"])
            S.op("dve", lambda e, k=k, cond=cond, gt=gt:
                 e.tensor_scalar(ctx.G[:, k, :, cond], gt, 0.5 if k != 1 else 1.0, None, ALU.mult),
                 reads=["modraw"], writes=["mod"])
    ctx.flush()


def ffn_phase(ctx, layer, which):
    S = ctx.S
    k = 0 if which == 0 else 2
    rms_phase(ctx, ctx.xT, DC, D, ctx.hT,
              lambda c, cond: ctx.A[:, k, c, cond:cond + 1], lambda c, cond: ctx.B[:, k, c, cond:cond + 1])
    Wgu = ctx.w["ffn_w_gu"][layer, which]
    blocks = []
    for j2 in range(FC // 2):
        segs = [(j2 * 256, 256), (DFF + j2 * 256, 256)]
        items = []
        for jj in range(2):
            items.append((2 * (2 * j2 + jj), jj * 128, 128))
            items.append((2 * (2 * j2 + jj) + 1, 256 + jj * 128, 128))
        blocks.append((segs, items))
    with contextlib.ExitStack() as st:
        sgr = Ring(ctx, st, "sg", [128, T], F32, 2)
        ar = Ring(ctx, st, "fa", [128, T], BF16, 2)
        state = {}

        def epi(ci, w, t0, tn, ps, pk):
            j = ci // 2
            if ci % 2 == 0:
                if t0 == 0:
                    state["sg"] = sgr.next()
                sg, sgk = state["sg"]
                S.op("act", lambda e: e.activation(sg[:, t0:t0 + tn], ps[:, :tn], AF.Silu), reads=[pk], writes=[sgk])
            else:
                sg, sgk = state["sg"]
                if t0 == 0:
                    state["a"] = ar.next()
                at, ak = state["a"]
                S.op("dve", lambda e: e.tensor_tensor(at[:, t0:t0 + tn], sg[:, t0:t0 + tn], ps[:, :tn], ALU.mult),
                     reads=[pk, sgk], writes=[ak])
                if t0 + tn == T:
                    S.dma("sp", ctx.aT[j], at[:], reads=[ak])
        gemm(ctx, Wgu, DC, blocks, fmv(ctx.hT), [(0, T)], epi, Wc=512)
    with contextlib.ExitStack() as st:
        epi = resid_epi(ctx, st, lambda ci, cond: ctx.G[:, k, ci, cond:cond + 1])
        Wd = ctx.w["ffn_w_down"][layer, which]
        chunks = [(ci * 128, 128) for ci in range(DC)]
        gemm(ctx, Wd, FC, simple_blocks(chunks, 256), fmv(ctx.aT), [(0, 1024), (1024, 1024), (2048, 512)], epi, Wc=256, nw=2)


def store_epi(ctx, st, dst_fn):
    S = ctx.S
    rb = Ring(ctx, st, "seb", [128, 512], BF16, 4)
    rf = Ring(ctx, st, "sef", [128, 512], F32, 3)
    cnt = [0]

    def epi(ci, w, t0, tn, ps, pk):
        dsts = sorted(dst_fn(ci, t0, tn), key=lambda x: 0 if x[1] == F32 else 1)
        first = None
        for (dst, dt) in dsts:
            tl, tk = (rb if dt == BF16 else rf).next()
            if first is None:
                cnt[0] += 1
                if cnt[0] % 2 == 0:
                    S.op("act", lambda e, tl=tl: e.activation(tl[:w, :tn], ps[:w, :tn], AF.Identity), reads=[pk], writes=[tk])
                else:
                    S.op("dve", lambda e, tl=tl: e.tensor_copy(tl[:w, :tn], ps[:w, :tn]), reads=[pk], writes=[tk])
                first = (tl, tk)
            else:
                S.op("pool", lambda e, tl=tl, f=first[0]: e.tensor_copy(tl[:w, :tn], f[:w, :tn]), reads=[first[1]], writes=[tk])
            S.dma("act", dst, tl[:w, :tn], reads=[tk])
    return epi


def vtrans_phase(ctx, srcT, nchunks, ntok, vtok, tok_off=0, col_off=0):
    S = ctx.S
    with contextlib.ExitStack() as st:
        vt = st.enter_context(ctx.nc.sbuf_tensor("vtT_%d" % uid(), [128, nchunks, ntok], BF16))
        rr = Ring(ctx, st, "vrow", [128, nchunks * 128], BF16, 3)
        half = (nchunks + 1) // 2
        for c0 in range(0, nchunks, half):
            S.dma("sp", vt[:, c0:c0 + half, :], srcT[:, c0:c0 + half, :], writes=["vt%d" % c0])
        vkeys = ["vt%d" % c0 for c0 in range(0, nchunks, half)]
        for tb in range(ntok // 128):
            row, rk = rr.next()
            for c8 in range(0, nchunks, 8):
                ps, pk = psum_next(ctx, 0, 8)
                psb = ps.bitcast(BF16)
                n8 = min(8, nchunks - c8)
                for c in range(c8, c8 + n8):
                    S.op("pe", lambda e, psb=psb, c=c, c8=c8, tb=tb:
                         e.transpose(psb[:, (c - c8) * 128:(c - c8 + 1) * 128], vt[:, c, tb * 128:(tb + 1) * 128], ctx.ident_bf),
                         reads=vkeys + ["const"], writes=[pk])
                if (c8 // 8) % 2 == 0:
                    S.op("dve", lambda e, psb=psb, row=row, c8=c8, n8=n8:
                         e.tensor_copy(row[:, c8 * 128:(c8 + n8) * 128], psb[:, :n8 * 128]), reads=[pk], writes=[rk + "a"])
                else:
                    S.op("act", lambda e, psb=psb, row=row, c8=c8, n8=n8:
                         e.activation(row[:, c8 * 128:(c8 + n8) * 128], psb[:, :n8 * 128], AF.Identity), reads=[pk], writes=[rk + "b"])
            S.dma("sp", vtok[tok_off + tb * 128:tok_off + (tb + 1) * 128, col_off:col_off + nchunks * 128], row[:], reads=[rk + "a", rk + "b"])
        ctx.flush()


def rope_phase(ctx, items, rope_tab, R, d):
    S = ctx.S
    with contextlib.ExitStack() as st:
        cs = st.enter_context(ctx.nc.sbuf_tensor("ropecs_%d" % uid(), [d, 2, TS], F32))
        S.dma("sp", cs[:, 0, :], rope_tab[0], writes=["cs0"])
        S.dma("sp", cs[:, 1, :], rope_tab[1], writes=["cs1"])
        xr = Ring(ctx, st, "ropx", [d, TS], BF16, 3)
        t1r = Ring(ctx, st, "ropt1", [d, 512], F32, 3)
        t2r = Ring(ctx, st, "ropt2", [d, 512], F32, 3)
        orr = Ring(ctx, st, "ropo", [d, TS], BF16, 3)
        for ap in items:
            xt, xk = xr.next()
            ot, ok = orr.next()
            S.dma("sp", xt[:], ap, writes=[xk])
            for s0 in range(0, TS, 512):
                ps, pk = psum_next(ctx, 0, 8)
                t1, k1 = t1r.next()
                t2, k2 = t2r.next()
                S.op("pe", lambda e, ps=ps, xt=xt, s0=s0: e.matmul(ps[:d, :512], R, xt[:, s0:s0 + 512], start=True, stop=True),
                     reads=[xk, "const"], writes=[pk])
                S.op("pool", lambda e, t1=t1, xt=xt, s0=s0: e.tensor_tensor(t1[:], xt[:, s0:s0 + 512], cs[:, 0, s0:s0 + 512], ALU.mult),
                     reads=[xk, "cs0"], writes=[k1])
                S.op("dve", lambda e, t2=t2, ps=ps, s0=s0: e.tensor_tensor(t2[:], ps[:d, :512], cs[:, 1, s0:s0 + 512], ALU.mult),
                     reads=[pk, "cs1"], writes=[k2])
                S.op("dve", lambda e, ot=ot, t1=t1, t2=t2, s0=s0: e.tensor_tensor(ot[:, s0:s0 + 512], t1[:], t2[:], ALU.add),
                     reads=[k1, k2], writes=[ok])
            S.dma("sp", ap, ot[:], reads=[ok])
        ctx.flush()


def attention(ctx, units, scale, out_dt):
    S = ctx.S
    with contextlib.ExitStack() as st:
        npc = max(len(u["pieces"]) for u in units)
        nkmax = max(sum(v.shape[0] for v in u["vsegs"]) for u in units)
        nqmax = max(u["nq"] for u in units)
        dvmax = max(u["dv"] for u in units)
        kr = [Ring(ctx, st, "atk%d" % i, [128, nkmax], BF16, 2) for i in range(npc)]
        qr = [Ring(ctx, st, "atq%d" % i, [128, nqmax], BF16, 2) for i in range(npc)]
        vr = Ring(ctx, st, "atv", [128, nkmax // 128, dvmax], BF16, 2)
        pr = Ring(ctx, st, "atp", [128, 512], BF16, 4)
        rdr = Ring(ctx, st, "atrd", [128, 512], F32, 2)
        otr = Ring(ctx, st, "ato", [128, dvmax // 128, 512], out_dt, 2)
        sbank = [0]
        aset = [0]

        def ld(dst, src, key):
            q = "sp" if src.dtype == BF16 else "pool"
            S.dma(q, dst, src, writes=[key])

        def load_unit(u):
            nq, dv = u["nq"], u["dv"]
            L = dict(ktiles=[], qtiles=[], kkeys=[], qkeys=[])
            for pi, (d, q_ap, k_aps) in enumerate(u["pieces"]):
                kt, kk = kr[pi].next()
                qt, qk = qr[pi].next()
                o = 0
                kks = []
                for si, ka in enumerate(k_aps):
                    n = ka.shape[1]
                    ld(kt[:d, o:o + n], ka, "%s_%d" % (kk, si))
                    kks.append("%s_%d" % (kk, si))
                    o += n
                ld(qt[:d, :nq], q_ap, qk)
                L["ktiles"].append(kt); L["qtiles"].append(qt); L["kkeys"].append(kks); L["qkeys"].append(qk)
            vt, vk = vr.next()
            o = 0
            vks = []
            for si, va in enumerate(u["vsegs"]):
                n = va.shape[0]
                ld(vt[:, o // 128:(o + n) // 128, :dv], va.rearrange("(c p) d -> p c d", p=128), "%s_%d" % (vk, si))
                vks.append("%s_%d" % (vk, si))
                o += n
            L["vt"], L["vks"] = vt, vks
            return L

        steps = []
        for ui, u in enumerate(units):
            nk = sum(v.shape[0] for v in u["vsegs"])
            for q0 in range(0, u["nq"], 512):
                for kc in range(nk // 128):
                    steps.append((ui, q0, kc, nk // 128))
        loaded = {}
        state = {}

        def emit_s(i):
            ui, q0, kc, nkc = steps[i]
            u = units[ui]
            if ui not in loaded:
                loaded[ui] = load_unit(u)
                loaded.pop(ui - 2, None)
            L = loaded[ui]
            qn = min(512, u["nq"] - q0)
            sps, spk = ctx.ps[sbank[0]], "ps%d" % sbank[0]
            sbank[0] ^= 1
            npieces = len(u["pieces"])
            for pi, (d, _, _) in enumerate(u["pieces"]):
                S.op("pe", lambda e, kt=L["ktiles"][pi], qt=L["qtiles"][pi], d=d, pi=pi:
                     e.matmul(sps[:, :qn], kt[:d, kc * 128:(kc + 1) * 128], qt[:d, q0:q0 + qn],
                              start=(pi == 0), stop=(pi == npieces - 1)),
                     reads=L["kkeys"][pi] + [L["qkeys"][pi]], writes=[spk])
            state[i] = (sps, spk, qn)

        def emit_rest(i):
            ui, q0, kc, nkc = steps[i]
            u = units[ui]
            L = loaded[ui]
            sps, spk, qn = state.pop(i)
            ndv = u["dv"] // 128
            if kc == 0:
                base = 2 + 3 * aset[0]
                aset[0] ^= 1
                state["acc"] = ([(ctx.ps[base + j], "ps%d" % (base + j)) for j in range(ndv)], (ctx.ps[base + 2], "ps%d" % (base + 2)))
            obanks, dbank = state["acc"]
            pt, pk_ = pr.next()
            S.op("act", lambda e: e.activation(pt[:, :qn], sps[:, :qn], AF.Exp, scale=scale), reads=[spk], writes=[pk_])
            vt = L["vt"]
            for j in range(ndv):
                S.op("pe", lambda e, j=j, ob=obanks[j][0]:
                     e.matmul(ob[:, :qn], vt[:, kc, j * 128:(j + 1) * 128], pt[:, :qn], start=(kc == 0), stop=(kc == nkc - 1)),
                     reads=L["vks"] + [pk_], writes=[obanks[j][1]])
            S.op("pe", lambda e, db=dbank[0]: e.matmul(db[:, :qn], ctx.ones_bf, pt[:, :qn], start=(kc == 0), stop=(kc == nkc - 1)),
                 reads=[pk_, "const"], writes=[dbank[1]])
            if kc == nkc - 1:
                rd, rdk = rdr.next()
                ot, otk = otr.next()
                S.op("dve", lambda e, db=dbank[0]: e.reciprocal(rd[:, :qn], db[:, :qn]), reads=[dbank[1]], writes=[rdk])
                for j in range(ndv):
                    S.op("dve", lambda e, j=j, ob=obanks[j][0]: e.tensor_tensor(ot[:, j, :qn], ob[:, :qn], rd[:, :qn], ALU.mult),
                         reads=[obanks[j][1], rdk], writes=[otk])
                S.dma("sp", u["out"][:, q0:q0 + qn].rearrange("(c p) t -> p c t", p=128), ot[:, :ndv, :qn], reads=[otk])

        n = len(steps)
        emit_s(0)
        for i in range(n):
            if i + 1 < n:
                emit_s(i + 1)
            emit_rest(i)
        ctx.flush()


def wo_phase(ctx, W, KC, src, k_gate=1):
    with contextlib.ExitStack() as st:
        epi = resid_epi(ctx, st, lambda ci, cond: ctx.G[:, k_gate, ci, cond:cond + 1])
        chunks = [(ci * 128, 128) for ci in range(DC)]
        tiles = [(0, T)] if KC <= 16 else [(0, 1280), (1280, 768), (2048, 512)]
        gemm(ctx, W, KC, simple_blocks(chunks, 512 if KC <= 16 else 256), fmv(src), tiles, epi, Wc=512 if KC <= 16 else 256)


def qkv_phase(ctx, W, kout, vout):
    with contextlib.ExitStack() as st:
        def dst(ci, t0, tn):
            r = [(ctx.qkvT[ci, :, t0:t0 + tn], BF16)]
            if ci >= 16 and t0 >= TS:
                o = kout if ci < 32 else vout
                r.append((o[(ci - 16) % 16, :, t0 - TS:t0 - TS + tn], F32))
            return r
        epi = store_epi(ctx, st, dst)
        chunks = [(ci * 128, 128) for ci in range(48)]
        gemm(ctx, W, DC, simple_blocks(chunks, 512), fmv(ctx.hT), [(0, T)], epi, Wc=512)


def diff_layer(ctx, layer):
    S = ctx.S
    nc = ctx.nc
    lam_init = 0.8 - 0.6 * float(np.exp(-0.3 * layer))
    rms_phase(ctx, ctx.xT, DC, D, ctx.hT,
              lambda c, cond: ctx.A[:, 1, c, cond:cond + 1], lambda c, cond: ctx.B[:, 1, c, cond:cond + 1])
    qkv_phase(ctx, ctx.w["diff_w_qkv"][0], ctx.out["new_diff_kT"], ctx.out["new_diff_vT"])
    rope_phase(ctx, [ctx.qkvT[ci, :, 0:TS] for ci in range(32)], ctx.c["rope128"], ctx.R128, 128)
    vtrans_phase(ctx, fmv(ctx.qkvT)[:, 32:48, :], 16, T, ctx.vtok)
    scale = 128 ** -0.5
    units = []
    ck = ctx.c["cache_diff_kT"]
    cv = ctx.c["cache_diff_v"]
    for u in range(16):
        h = u // 2
        units.append(dict(pieces=[(128, ctx.qkvT[u, :, 0:TS], [ctx.qkvT[16 + u, :, 0:TS], ck[u]])],
                          vsegs=[ctx.vtok[0:TS, h * 256:(h + 1) * 256], cv[:, h * 256:(h + 1) * 256]],
                          nq=TS, dv=256, out=ctx.oacc[u, :, 0:TS]))
    for p in range(2):
        a, b = TS + p * 256, TS + (p + 1) * 256
        for u in range(16):
            h = u // 2
            units.append(dict(pieces=[(128, ctx.qkvT[u, :, a:b], [ctx.qkvT[16 + u, :, a:b]])],
                              vsegs=[ctx.vtok[a:b, h * 256:(h + 1) * 256]],
                              nq=256, dv=256, out=ctx.oacc[u, :, a:b]))
    attention(ctx, units, scale, F32)
    with contextlib.ExitStack() as st:
        E = st.enter_context
        sm = E(nc.sbuf_tensor("dlsm_%d" % uid(), [128, 8], F32))
        tmp = E(nc.sbuf_tensor("dltmp_%d" % uid(), [128, 256], F32))
        dl = lambda i: pvap(ctx, "dlam", i * 128, 128)
        S.op("dve", lambda e: e.tensor_tensor(tmp[:, 0:128], dl(0), dl(1), ALU.mult), reads=["pv"], writes=["dltmp"])
        S.op("dve", lambda e: e.tensor_tensor(tmp[:, 128:256], dl(2), dl(3), ALU.mult), reads=["pv"], writes=["dltmp"])
        S.op("dve", lambda e: e.reduce_sum(sm[:, 0:1], tmp[:, 0:128], mybir.AxisListType.X), reads=["dltmp"], writes=["sm0"])
        S.op("dve", lambda e: e.reduce_sum(sm[:, 1:2], tmp[:, 128:256], mybir.AxisListType.X), reads=["dltmp"], writes=["sm1"])
        S.op("act", lambda e: e.activation(sm[:, 2:4], sm[:, 0:2], AF.Exp), reads=["sm0", "sm1"], writes=["sm2"])
        S.op("dve", lambda e: e.scalar_tensor_tensor(sm[:, 4:5], sm[:, 3:4], -lam_init, sm[:, 2:3], ALU.add, ALU.subtract),
             reads=["sm2"], writes=["sm4"])
        S.op("dve", lambda e: e.tensor_scalar(sm[:, 5:7], pvap(ctx, "subln", 0, 2), 1.0 - lam_init, None, ALU.mult),
             reads=["pv"], writes=["sm5"])
        o0r = Ring(ctx, st, "dlo0", [128, 2, 512], F32, 3)
        o1r = Ring(ctx, st, "dlo1", [128, 2, 512], F32, 3)
        sqr = Ring(ctx, st, "dlsq", [128, 2, 512], BF16, 2)
        rsr = Ring(ctx, st, "dlrs", [128, 512], F32, 3)
        obr = Ring(ctx, st, "dlob", [128, 2, 512], BF16, 2)
        def stage_a(h, t0):
            o0, k0 = o0r.next(); o1, k1 = o1r.next(); sq, sqk = sqr.next(); rs, rk = rsr.next()
            S.dma("sp", o0[:], ctx.oacc[2 * h, :, t0:t0 + 512].rearrange("(c p) t -> p c t", p=128), writes=[k0])
            S.dma("sp", o1[:], ctx.oacc[2 * h + 1, :, t0:t0 + 512].rearrange("(c p) t -> p c t", p=128), writes=[k1])
            S.op("dve", lambda e: e.scalar_tensor_tensor(o0[:], o1[:], sm[:, 4:5], o0[:], ALU.mult, ALU.add),
                 reads=[k0, k1, "sm4"], writes=[k0])
            S.op("act", lambda e: e.activation(sq[:], o0[:], AF.Square), reads=[k0], writes=[sqk])
            ps, pk = psum_next(ctx, 6, 8)
            for c in range(2):
                S.op("pe", lambda e, c=c: e.matmul(ps[:, :512], ctx.ones_bf, sq[:, c, :], start=(c == 0), stop=(c == 1)),
                     reads=[sqk, "const"], writes=[pk])
            S.op("act", lambda e: e.activation(rs[:], ps[:, :512], AF.Sqrt, bias=ctx.eps_ap, scale=1.0 / 256),
                 reads=[pk, "const"], writes=[rk])
            S.op("dve", lambda e: e.reciprocal(rs[:], rs[:]), reads=[rk], writes=[rk])
            return (h, t0, o0, k0, rs, rk)

        def stage_b(h, t0, o0, k0, rs, rk):
            ob, obk = obr.next()
            for c in range(2):
                S.op("dve", lambda e, c=c: e.scalar_tensor_tensor(ob[:, c, :], o0[:, c, :], sm[:, 5 + c:6 + c], rs[:], ALU.mult, ALU.mult),
                     reads=[k0, rk, "sm5"], writes=[obk])
            S.dma("act", fmv(ctx.oT)[:, 2 * h:2 * h + 2, t0:t0 + 512], ob[:], reads=[obk])

        prev = None
        for h in range(8):
            for t0 in range(0, T, 512):
                cur = stage_a(h, t0)
                if prev is not None:
                    stage_b(*prev)
                prev = cur
        stage_b(*prev)
        ctx.flush()
    wo_phase(ctx, ctx.w["diff_w_o"][0], DC, ctx.oT)


def na_classes():
    out = []
    for r in range(32):
        rs = min(max(r - 4, 0), 24)
        cls = r if r < 4 else (4 if r < 28 else 5 + (r - 28))
        out.append((rs, cls))
    return out


def na_host_tables(rpb):
    kc = np.arange(64)[:, None]
    qc = np.arange(64)[None, :]
    dc = np.clip(kc - qc + 15, 0, 30)
    cstart = np.clip(qc - 8, 0, 48)
    ok = (kc >= cstart) & (kc < cstart + 16)
    bias = np.zeros((16, 128, 9, 4, 64), np.float32)
    mask = np.zeros((128, 9, 4, 64), np.float32)
    cls_r = {}
    for r, (rs, cls) in enumerate(na_classes()):
        cls_r[cls] = (r, rs)
    for cls, (r, rs) in cls_r.items():
        for j in range(4):
            for w2 in range(2):
                wr = 2 * j + w2
                ridx = rs + wr - r + 7
                bias[:, w2 * 64:(w2 + 1) * 64, cls, j, :] = rpb[:, ridx][:, dc]
                mask[w2 * 64:(w2 + 1) * 64, cls, j, :] = ok
    return bias.reshape(16, 128, 2304), mask.reshape(128, 2304).astype(ml_dtypes.bfloat16)


def na_layer(ctx, layer):
    S = ctx.S
    nc = ctx.nc
    rms_phase(ctx, ctx.xT, DC, D, ctx.hT,
              lambda c, cond: ctx.A[:, 1, c, cond:cond + 1], lambda c, cond: ctx.B[:, 1, c, cond:cond + 1])
    qkv_phase(ctx, ctx.w["na_w_qkv"][0], ctx.out["new_na_kT"], ctx.out["new_na_vT"])
    vtrans_phase(ctx, fmv(ctx.qkvT)[:, 32:48, :], 16, T, ctx.vtok)
    scale = 128 ** -0.5
    units = []
    for p in range(2):
        a, b = TS + p * 256, TS + (p + 1) * 256
        for h in range(16):
            units.append(dict(pieces=[(128, ctx.qkvT[h, :, a:b], [ctx.qkvT[16 + h, :, a:b]])],
                              vsegs=[ctx.vtok[a:b, h * 128:(h + 1) * 128]],
                              nq=256, dv=128, out=ctx.oT[h, :, a:b]))
    attention(ctx, units, scale, BF16)
    rows = na_classes()
    with contextlib.ExitStack() as st:
        E = st.enter_context
        maskt = E(nc.sbuf_tensor("namask_%d" % uid(), [128, 2304], BF16))
        S.dma("sp", maskt[:], ctx.c["na_mask"], writes=["namask"])
        qr = Ring(ctx, st, "naq", [128, TS], BF16, 2)
        kr = Ring(ctx, st, "nak", [128, TS], BF16, 2)
        ver = Ring(ctx, st, "nave", [128, 16, 128], BF16, 2)
        vor = Ring(ctx, st, "navo", [128, 15, 128], BF16, 2)
        ckr = Ring(ctx, st, "nack", [128, 512], BF16, 2)
        cvr = Ring(ctx, st, "nacv", [128, 4, 128], BF16, 2)
        br = Ring(ctx, st, "nab", [128, 2304], F32, 1)
        er = Ring(ctx, st, "nae", [128, 2304], BF16, 2)
        pr = Ring(ctx, st, "nap", [128, 512], BF16, 4)
        rdr = Ring(ctx, st, "nard", [128, 512], F32, 2)
        otr = Ring(ctx, st, "nao", [128, TS], BF16, 2)
        sb = [0]
        aset = [0]
        heads = {}

        def load_head(h):
            qt, qk = qr.next(); kt, kk = kr.next(); ve, vek = ver.next(); vo, vok = vor.next()
            ck, ckk = ckr.next(); cv, cvk = cvr.next(); bt, bk = br.next(); et, ek = er.next(); ot, otk = otr.next()
            S.dma("sp", qt[:], ctx.qkvT[h, :, 0:TS], writes=[qk])
            S.dma("sp", kt[:], ctx.qkvT[16 + h, :, 0:TS], writes=[kk])
            S.dma("sp", ve[:], ctx.vtok[0:TS, h * 128:(h + 1) * 128].rearrange("(c p) d -> p c d", p=128), writes=[vek])
            S.dma("sp", vo[:], ctx.vtok[64:TS - 64, h * 128:(h + 1) * 128].rearrange("(c p) d -> p c d", p=128), writes=[vok])
            S.dma("pool", ck[:], ctx.c["cache_na_kT"][h], writes=[ckk])
            S.dma("pool", cv[:], ctx.c["cache_na_v"][:, h * 128:(h + 1) * 128].rearrange("(c p) d -> p c d", p=128), writes=[cvk])
            S.dma("sp", bt[:], ctx.c["na_bias"][h], writes=[bk])
            S.op("act", lambda e: e.activation(et[:], bt[:], AF.Exp), reads=[bk], writes=[ek])
            S.op("dve", lambda e: e.tensor_tensor(et[:], et[:], maskt[:], ALU.mult), reads=[ek, "namask"], writes=[ek])
            return dict(qt=qt, qk=qk, kt=kt, kk=kk, ve=ve, vek=vek, vo=vo, vok=vok, ck=ck, ckk=ckk, cv=cv, cvk=cvk, et=et, ek=ek, ot=ot, otk=otk)

        steps = [(h, r) for h in range(16) for r in range(32)]
        pend = {}

        def emit_s(i):
            h, r = steps[i]
            if h not in heads:
                heads[h] = load_head(h)
                heads.pop(h - 2, None)
            H = heads[h]
            rs, cls = rows[r]
            sps, spk = ctx.ps[sb[0]], "ps%d" % sb[0]
            sb[0] ^= 1
            qs = H["qt"][:, r * 64:(r + 1) * 64]
            vch = []
            for j in range(4):
                k0 = (rs + 2 * j) * 64
                S.op("pe", lambda e, k0=k0, j=j: e.matmul(sps[:, j * 64:(j + 1) * 64], H["kt"][:, k0:k0 + 128], qs, start=True, stop=True),
                     reads=[H["kk"], H["qk"]], writes=[spk])
                rr_ = rs + 2 * j
                vch.append((H["ve"][:, rr_ // 2, :], H["vek"]) if rr_ % 2 == 0 else (H["vo"][:, (rr_ - 1) // 2, :], H["vok"]))
            for j in range(4):
                S.op("pe", lambda e, j=j: e.matmul(sps[:, (4 + j) * 64:(5 + j) * 64], H["ck"][:, j * 128:(j + 1) * 128], qs, start=True, stop=True),
                     reads=[H["ckk"], H["qk"]], writes=[spk])
                vch.append((H["cv"][:, j, :], H["cvk"]))
            pend[i] = (sps, spk, vch)

        def emit_rest(i):
            h, r = steps[i]
            H = heads[h]
            rs, cls = rows[r]
            sps, spk, vch = pend.pop(i)
            q8, r8 = r // 8, r % 8
            if r8 == 0:
                base = 2 + 2 * aset[0]
                aset[0] ^= 1
                pend["acc"] = ((ctx.ps[base], "ps%d" % base), (ctx.ps[base + 1], "ps%d" % (base + 1)))
            (ob, obk), (db, dbk) = pend["acc"]
            pt, pk_ = pr.next()
            et, ek = H["et"], H["ek"]
            S.op("act", lambda e: e.activation(pt[:], sps[:], AF.Exp, scale=scale), reads=[spk], writes=[pk_])
            S.op("dve", lambda e: e.tensor_tensor(pt[:, 0:256], pt[:, 0:256], et[:, cls * 256:(cls + 1) * 256], ALU.mult),
                 reads=[pk_, ek], writes=[pk_])
            for j in range(8):
                S.op("pe", lambda e, v=vch[j][0], j=j: e.matmul(ob[:, r8 * 64:(r8 + 1) * 64], v, pt[:, j * 64:(j + 1) * 64], start=(j == 0), stop=(j == 7)),
                     reads=[vch[j][1], pk_], writes=[obk])
            for j in range(8):
                S.op("pe", lambda e, j=j: e.matmul(db[:, r8 * 64:(r8 + 1) * 64], ctx.ones_bf, pt[:, j * 64:(j + 1) * 64], start=(j == 0), stop=(j == 7)),
                     reads=[pk_, "const"], writes=[dbk])
            if r8 == 7:
                ot, otk = H["ot"], H["otk"]
                rd, rdk = rdr.next()
                S.op("dve", lambda e: e.reciprocal(rd[:], db[:]), reads=[dbk], writes=[rdk])
                S.op("dve", lambda e: e.tensor_tensor(ot[:, q8 * 512:(q8 + 1) * 512], ob[:], rd[:], ALU.mult),
                     reads=[obk, rdk], writes=[otk])
                if r == 31:
                    S.dma("sp", ctx.oT[h, :, 0:TS], ot[:], reads=[otk])

        n = len(steps)
        emit_s(0)
        for i in range(n):
            if i + 1 < n:
                emit_s(i + 1)
            emit_rest(i)
        ctx.flush()
    wo_phase(ctx, ctx.w["na_w_o"][0], DC, ctx.oT)


def mla_layer(ctx, layer):
    S = ctx.S
    nc = ctx.nc
    TK = T + 512
    rms_phase(ctx, ctx.xT, DC, D, ctx.hT,
              lambda c, cond: ctx.A[:, 1, c, cond:cond + 1], lambda c, cond: ctx.B[:, 1, c, cond:cond + 1])
    with contextlib.ExitStack() as st:
        epi = store_epi(ctx, st, lambda ci, t0, tn: [(ctx.cqT[ci, :, t0:t0 + tn], F32)])
        gemm(ctx, ctx.w["mla_w_dq"][0], DC, simple_blocks([(c * 128, 128) for c in range(4)], 512), fmv(ctx.hT), [(0, T)], epi, Wc=512)
    rms_phase(ctx, ctx.cqT, 4, 512, ctx.cqnT, lambda c, cond: pvap(ctx, "q_norm", c), None)
    with contextlib.ExitStack() as st:
        def dstq(ci, t0, tn):
            h = ci // 2
            if ci % 2 == 0:
                return [(ctx.mlaq[h, 0:128, t0:t0 + tn], BF16)]
            return [(ctx.mlaq[h, 128:192, t0:t0 + tn], BF16)]
        epi = store_epi(ctx, st, dstq)
        chunks = []
        for h in range(16):
            chunks += [(h * 192, 128), (h * 192 + 128, 64)]
        gemm(ctx, ctx.w["mla_w_uq"][0], 4, simple_blocks(chunks, 384), fmv(ctx.cqnT), [(0, T)], epi, Wc=384)
    with contextlib.ExitStack() as st:
        def dstkv(ci, t0, tn):
            if ci < 2:
                return [(ctx.ckvraw[ci, :, t0:t0 + tn], F32)]
            r = [(ctx.krT[0:64, t0:t0 + tn], BF16)]
            if t0 >= TS:
                r.append((ctx.out["new_mla_kropeT"][0:64, t0 - TS:t0 - TS + tn], F32))
            return r
        epi = store_epi(ctx, st, dstkv)
        gemm(ctx, ctx.w["mla_w_dkv"][0], DC, simple_blocks([(0, 128), (128, 128), (256, 64)], 320), fmv(ctx.hT), [(0, T)], epi, Wc=320)
    rms_phase(ctx, ctx.ckvraw, 2, 256, ctx.ckvT, lambda c, cond: pvap(ctx, "kv_norm", c), None)
    rms_phase(ctx, ctx.ckvraw, 2, 256, ctx.out["new_mla_ckvT"], lambda c, cond: pvap(ctx, "kv_norm", c), None,
              tok0=TS, tok1=T, out_f32=True, dst_off=TS)
    with contextlib.ExitStack() as st:
        t1 = st.enter_context(nc.sbuf_tensor("mlac1_%d" % uid(), [128, 2, 512], BF16))
        t2 = st.enter_context(nc.sbuf_tensor("mlac2_%d" % uid(), [64, 512], BF16))
        S.dma("pool", t1[:], fmv(ctx.c["cache_mla_ckvT"]), writes=["mc1"])
        S.dma("pool", t2[:], ctx.c["cache_mla_kropeT"], writes=["mc2"])
        S.dma("sp", fmv(ctx.ckvT)[:, :, T:TK], t1[:], reads=["mc1"])
        S.dma("sp", ctx.krT[0:64, T:TK], t2[:], reads=["mc2"])
        ctx.flush()
    rope_phase(ctx, [ctx.mlaq[h, 128:192, 0:TS] for h in range(16)] + [ctx.krT[0:64, 0:TS]], ctx.c["rope64"], ctx.R64, 64)
    with contextlib.ExitStack() as st:
        def dstx(ci, t0, tn):
            h = ci // 2
            return [(ctx.mlakv[h if ci % 2 == 0 else 16 + h, :, t0:t0 + tn], BF16)]
        epi = store_epi(ctx, st, dstx)
        chunks = [(c * 128, 128) for c in range(32)]
        gemm(ctx, ctx.w["mla_w_ukv"][0], 2, simple_blocks(chunks, 512), fmv(ctx.ckvT), [(0, TK)], epi, Wc=512)
    vtrans_phase(ctx, fmv(ctx.mlakv)[:, 16:32, :], 16, TK, ctx.vtok)
    scale = 192 ** -0.5
    units = []
    for h in range(16):
        units.append(dict(pieces=[(128, ctx.mlaq[h, 0:128, 0:TS], [ctx.mlakv[h, :, 0:TS], ctx.mlakv[h, :, T:TK]]),
                                  (64, ctx.mlaq[h, 128:192, 0:TS], [ctx.krT[0:64, 0:TS], ctx.krT[0:64, T:TK]])],
                          vsegs=[ctx.vtok[0:TS, h * 128:(h + 1) * 128], ctx.vtok[T:TK, h * 128:(h + 1) * 128]],
                          nq=TS, dv=128, out=ctx.oT[h, :, 0:TS]))
    for p in range(2):
        a, b = TS + p * 256, TS + (p + 1) * 256
        for h in range(16):
            units.append(dict(pieces=[(128, ctx.mlaq[h, 0:128, a:b], [ctx.mlakv[h, :, a:b]]),
                                      (64, ctx.mlaq[h, 128:192, a:b], [ctx.krT[0:64, a:b]])],
                              vsegs=[ctx.vtok[a:b, h * 128:(h + 1) * 128]],
                              nq=256, dv=128, out=ctx.oT[h, :, a:b]))
    attention(ctx, units, scale, BF16)
    wo_phase(ctx, ctx.w["mla_w_o"][0], DC, ctx.oT)


SEQS = [(0, 2048), (2048, 256), (2304, 256)]
PADOFF = [3, 2054, 2313]
TP = T + 12


def ssd_layer(ctx, layer):
    S = ctx.S
    nc = ctx.nc
    rms_phase(ctx, ctx.xT, DC, D, ctx.hT,
              lambda c, cond: ctx.A[:, 1, c, cond:cond + 1], lambda c, cond: ctx.B[:, 1, c, cond:cond + 1])
    with contextlib.ExitStack() as st:
        rb = Ring(ctx, st, "sib", [128, 512], BF16, 3)
        rb2 = Ring(ctx, st, "sib2", [128, 512], BF16, 3)
        rf = Ring(ctx, st, "sif", [128, 512], F32, 2)

        def epi(ci, w, t0, tn, ps, pk):
            if ci < 32:
                tl, tk = rb.next()
                S.op("act", lambda e: e.activation(tl[:, :tn], ps[:, :tn], AF.Silu), reads=[pk], writes=[tk])
                S.dma("sp", ctx.szT[ci, :, t0:t0 + tn], tl[:, :tn], reads=[tk])
            elif ci < 80:
                tl, tk = rb2.next()
                S.op("dve", lambda e: e.tensor_copy(tl[:, :tn], ps[:, :tn]), reads=[pk], writes=[tk])
                S.dma("sp", ctx.xbcraw[ci - 32, :, t0:t0 + tn], tl[:, :tn], reads=[tk])
            else:
                tl, tk = rf.next()
                S.op("dve", lambda e: e.tensor_copy(tl[:, :tn], ps[:, :tn]), reads=[pk], writes=[tk])
                S.dma("sp", ctx.dtraw[:, t0:t0 + tn], tl[:, :tn], reads=[tk])
        chunks = [(c * 128, 128) for c in range(81)]
        gemm(ctx, ctx.w["ssd_w_in"][0], DC, simple_blocks(chunks, 512), fmv(ctx.hT), [(0, T)], epi, Wc=512)
    with contextlib.ExitStack() as st:
        rr = Ring(ctx, st, "cvr", [128, TP], BF16, 3)
        orr = Ring(ctx, st, "cvo", [128, TP], BF16, 3)
        dgr = Ring(ctx, st, "cvd", [128, 7, 128], BF16, 2)
        for i in range(3):
            S.op("dve", lambda e, t=rr.tiles[i]: e.memset(t[:], 0.0), writes=[rr.keys[i]] + [rr.keys[i] + "s%d" % j for j in range(3)])
        R0, R1 = 3, TP - 3
        for c in range(48):
            rt, rk = rr.next(); ot, ok = orr.next(); dg, dgk = dgr.next()
            lk = []
            for si, (a, n) in enumerate(SEQS):
                S.dma("sp", rt[:, PADOFF[si]:PADOFF[si] + n], ctx.xbcraw[c, :, a:a + n], writes=[rk + "s%d" % si])
                lk.append(rk + "s%d" % si)
            for w in range(7):
                S.op("dve", lambda e, dg=dg, w=w, c=c: e.tensor_scalar(dg[:, w, :], ctx.ident_bf, pvap(ctx, "conv_w", c * 7 + w), None, ALU.mult),
                     reads=["const", "pv"], writes=[dgk])
            for s0 in range(R0, R1, 512):
                sn = min(512, R1 - s0)
                ps, pk = psum_next(ctx, 0, 8)
                for w in range(7):
                    S.op("pe", lambda e, ps=ps, dg=dg, rt=rt, w=w, s0=s0, sn=sn:
                         e.matmul(ps[:, :sn], dg[:, w, :], rt[:, s0 + w - 3:s0 + w - 3 + sn], start=(w == 0), stop=(w == 6)),
                         reads=lk + [rk, dgk], writes=[pk])
                S.op("act", lambda e, ot=ot, ps=ps, c=c, s0=s0, sn=sn:
                     e.activation(ot[:, s0:s0 + sn], ps[:, :sn], AF.Silu, bias=pvap(ctx, "conv_b", c)),
                     reads=[pk, "pv"], writes=[ok])
            for si, (a, n) in enumerate(SEQS):
                dst = ctx.xcT[c, :, a:a + n] if c < 32 else ctx.bcT[c - 32, :, a:a + n]
                S.dma("sp", dst, ot[:, PADOFF[si]:PADOFF[si] + n], reads=[ok])
        ctx.flush()
    with contextlib.ExitStack() as st:
        E = st.enter_context
        raw = E(nc.sbuf_tensor("dtr_%d" % uid(), [128, T], F32))
        sp = E(nc.sbuf_tensor("dts_%d" % uid(), [128, T], F32))
        da = E(nc.sbuf_tensor("dta_%d" % uid(), [128, T], F32))
        ea = E(nc.sbuf_tensor("dte_%d" % uid(), [128, 2], F32))
        tk_ = E(nc.sbuf_tensor("dtt_%d" % uid(), [128, 2, 20, 128], F32))
        S.dma("sp", raw[:], ctx.dtraw, writes=["dtr"])
        S.op("act", lambda e: e.activation(raw[:], raw[:], AF.Exp, bias=pvap(ctx, "dt_bias", 0)), reads=["dtr", "pv"], writes=["dtr"])
        S.op("act", lambda e: e.activation(sp[:], raw[:], AF.Ln, bias=ctx.one_ap), reads=["dtr", "const"], writes=["dts"])
        S.op("act", lambda e: e.activation(ea[:, 0:1], pvap(ctx, "a_log", 0), AF.Exp), reads=["pv"], writes=["dte"])
        S.op("dve", lambda e: e.tensor_scalar(da[:], sp[:], ea[:, 0:1], -1.0, ALU.mult, ALU.mult), reads=["dts", "dte"], writes=["dta"])
        for which, (src, sk) in enumerate(((sp, "dts"), (da, "dta"))):
            for c4 in range(0, 20, 4):
                ps, pk = psum_next(ctx, 0, 8)
                for c in range(c4, c4 + 4):
                    S.op("pe", lambda e, ps=ps, src=src, c=c, c4=c4:
                         e.transpose(ps[:, (c - c4) * 128:(c - c4 + 1) * 128], src[:, c * 128:(c + 1) * 128], ctx.ident_f32),
                         reads=[sk, "constf"], writes=[pk])
                S.op("dve", lambda e, ps=ps, which=which, c4=c4:
                     e.tensor_copy(tk_[:, which, c4:c4 + 4, :].rearrange("p c j -> p (c j)"), ps[:, :512]), reads=[pk], writes=["dtt"])
        S.dma("sp", ctx.dttok, tk_[:], reads=["dtt"])
        ctx.flush()
    vtrans_phase(ctx, fmv(ctx.xcT)[:, 0:16, :], 16, T, ctx.xtok, col_off=0)
    vtrans_phase(ctx, fmv(ctx.xcT)[:, 16:32, :], 16, T, ctx.xtok, col_off=2048)
    vtrans_phase(ctx, fmv(ctx.bcT)[:, 0:8, :], 8, T, ctx.btok)
    ssd_scan(ctx)
    with contextlib.ExitStack() as st:
        TN = 256
        yr = Ring(ctx, st, "gny", [128, 32, TN], BF16, 2)
        zr = Ring(ctx, st, "gnz", [128, 32, TN], BF16, 2)
        gr = Ring(ctx, st, "gng", [128, 32, TN], F32, 1)
        sr = Ring(ctx, st, "gns", [128, 32, TN], BF16, 1)
        orr = Ring(ctx, st, "gno", [128, 32, TN], BF16, 2)
        rsr = Ring(ctx, st, "gnr", [128, TN], F32, 2)
        for t0 in range(0, T, TN):
            yt, yk = yr.next(); zt, zk = zr.next(); gt, gk = gr.next(); sq, sqk = sr.next(); ot, ok = orr.next(); rs, rk = rsr.next()
            S.dma("sp", yt[:], fmv(ctx.yT)[:, :, t0:t0 + TN], writes=[yk])
            S.dma("sp", zt[:], fmv(ctx.szT)[:, :, t0:t0 + TN], writes=[zk])
            S.op("dve", lambda e, gt=gt, yt=yt, zt=zt: e.tensor_tensor(gt[:], yt[:], zt[:], ALU.mult), reads=[yk, zk], writes=[gk])
            S.op("act", lambda e, sq=sq, gt=gt: e.activation(sq[:], gt[:], AF.Square), reads=[gk], writes=[sqk])
            ps, pk = psum_next(ctx, 6, 8)
            for c in range(32):
                S.op("pe", lambda e, ps=ps, sq=sq, c=c: e.matmul(ps[:, :TN], ctx.ones_bf, sq[:, c, :], start=(c == 0), stop=(c == 31)),
                     reads=[sqk, "const"], writes=[pk])
            S.op("act", lambda e, rs=rs, ps=ps: e.activation(rs[:], ps[:, :TN], AF.Sqrt, bias=ctx.eps_ap, scale=1.0 / 4096),
                 reads=[pk, "const"], writes=[rk])
            S.op("dve", lambda e, rs=rs: e.reciprocal(rs[:], rs[:]), reads=[rk], writes=[rk])
            for c in range(32):
                eng = "dve"
                S.op(eng, lambda e, ot=ot, gt=gt, rs=rs, c=c:
                     e.scalar_tensor_tensor(ot[:, c, :], gt[:, c, :], pvap(ctx, "ssd_nw", c), rs[:], ALU.mult, ALU.mult),
                     reads=[gk, rk, "pv"], writes=[ok + eng])
            S.dma("sp", fmv(ctx.ygT)[:, :, t0:t0 + TN], ot[:], reads=[ok + "dve"])
        ctx.flush()
    wo_phase(ctx, ctx.w["ssd_w_out"][0], 32, ctx.ygT)


def ssd_scan(ctx):
    S = ctx.S
    nc = ctx.nc
    with contextlib.ExitStack() as st:
        E = st.enter_context
        cst = E(nc.sbuf_tensor("sscst_%d" % uid(), [128, 640], F32))
        S.dma("sp", cst[:], ctx.c["ssdc"][:, 0:640], writes=["sscst"])
        U = [cst[:, 0:128], cst[:, 128:256]]
        Tm = [cst[:, 256:384], cst[:, 384:512]]
        dtt = E(nc.sbuf_tensor("ssdtt_%d" % uid(), [128, 2, 20, 128], F32))
        S.dma("sp", dtt[:], ctx.dttok, writes=["ssdtt"])
        BT = E(nc.sbuf_tensor("ssBT_%d" % uid(), [128, T], BF16))
        CT = E(nc.sbuf_tensor("ssCT_%d" % uid(), [128, T], BF16))
        Bk = E(nc.sbuf_tensor("ssBk_%d" % uid(), [128, 20, 128], BF16))
        Xk = E(nc.sbuf_tensor("ssXk_%d" % uid(), [128, 20, 512], BF16))
        ybuf = E(nc.sbuf_tensor("ssyb_%d" % uid(), [64, 8, T], BF16))
        idsk = E(nc.sbuf_tensor("ssid_%d" % uid(), [128, 8, 128], BF16))
        hT = E(nc.sbuf_tensor("sshT_%d" % uid(), [128, 2, 8, 64], F32))
        hTb = E(nc.sbuf_tensor("sshTb_%d" % uid(), [128, 2, 8, 64], BF16))
        aTr = Ring(ctx, st, "ssaT", [128, 8, 128], F32, 3)
        Lmr = Ring(ctx, st, "ssLm", [128, 8, 128], F32, 3)
        Edr = Ring(ctx, st, "ssEd", [128, 8, 128], F32, 3)
        cbr = Ring(ctx, st, "sscb", [128, 1, 128], F32, 3)
        MTr = Ring(ctx, st, "ssMT", [128, 128], BF16, 10)
        Cdr = Ring(ctx, st, "ssCd", [128, 128], BF16, 10)
        Bwr = Ring(ctx, st, "ssBw", [128, 128], BF16, 10)
        w8r = Ring(ctx, st, "ssw8", [128, 8], F32, 3)
        PS = lambda i: (ctx.ps[i], "ps%d" % i)

        def group_setup(g):
            S.dma("sp", BT[:], ctx.bcT[g], writes=["ssBT"])
            S.dma("sp", CT[:], ctx.bcT[8 + g], writes=["ssCT"])
            S.dma("sp", Bk[:], ctx.btok[:, g * 128:(g + 1) * 128].rearrange("(c p) n -> p c n", p=128), writes=["ssBk"])
            S.dma("sp", Xk[:], ctx.xtok[:, g * 512:(g + 1) * 512].rearrange("(c p) n -> p c n", p=128), writes=["ssXk"])
            for hh in range(8):
                S.op("dve", lambda e, hh=hh: e.tensor_scalar(idsk[:, hh, :], ctx.ident_bf, pvap(ctx, "ssd_d", g * 8 + hh), None, ALU.mult),
                     reads=["const", "pv"], writes=["ssid"])

        def stage1(it):
            g, si, dr, c = it["g"], it["si"], it["dr"], it["c"]
            tsl = slice(c * 128, (c + 1) * 128)
            cps, cpk = PS(2)
            cb, cbk = cbr.next()
            S.op("pe", lambda e: e.matmul(cps[:, 0:128], BT[:, tsl], CT[:, tsl], start=True, stop=True),
                 reads=["ssBT", "ssCT"], writes=[cpk])
            S.op("dve", lambda e: e.tensor_tensor(cb[:, 0, :], cps[:, 0:128], Tm[dr], ALU.mult), reads=[cpk, "sscst"], writes=[cbk])
            aT, aTk = aTr.next()
            for hh in range(8):
                col = dr * 64 + g * 8 + hh
                S.op("act", lambda e, hh=hh, col=col: e.activation(aT[:, hh, :], Tm[dr], AF.Identity, scale=dtt[:, 1, c, col:col + 1]),
                     reads=["sscst", "ssdtt"], writes=[aTk])
            Lm, Lmk = Lmr.next()
            Ed, Edk = Edr.next()
            for half in range(2):
                sps, spk = PS(0 if half == 0 else 7)
                aps, apk = PS(1 if half == 0 else 0)
                rhs = aT[:, half * 4:(half + 1) * 4, :].rearrange("p h q -> p (h q)")
                S.op("pe", lambda e, sps=sps, rhs=rhs: e.matmul(sps[:, :512], U[dr], rhs, start=True, stop=True),
                     reads=[aTk, "sscst"], writes=[spk])
                S.op("pe", lambda e, aps=aps, rhs=rhs: e.matmul(aps[:, :512], cst[:, 512:640], rhs, start=True, stop=True),
                     reads=[aTk, "sscst"], writes=[apk])
                S.op("act", lambda e, sps=sps, half=half:
                     e.activation(Lm[:, half * 4:(half + 1) * 4, :].rearrange("p h q -> p (h q)"), sps[:, :512], AF.Exp),
                     reads=[spk], writes=[Lmk])
                S.op("act", lambda e, aps=aps, half=half:
                     e.activation(Ed[:, half * 4:(half + 1) * 4, :].rearrange("p h q -> p (h q)"), aps[:, :512], AF.Exp),
                     reads=[apk], writes=[Edk])
            return dict(cb=cb, cbk=cbk, Lm=Lm, Lmk=Lmk, Ed=Ed, Edk=Edk)

        ykeys = []
        sbank_ = [0]

        def stage2(it, s1):
            g, si, dr, c = it["g"], it["si"], it["dr"], it["c"]
            tsl = slice(c * 128, (c + 1) * 128)
            cb, cbk, Lm, Lmk, Ed, Edk = s1["cb"], s1["cbk"], s1["Lm"], s1["Lmk"], s1["Ed"], s1["Edk"]
            qe = 127 if dr == 0 else 0
            hk, hbk = "sshT%d" % dr, "sshTb%d" % dr
            if it["seq_start"]:
                if si == 0:
                    for d_ in range(2):
                        S.dma("sp", hT[:, d_, :, :], ctx.c["ssd_stateT"][d_, g * 8:(g + 1) * 8].rearrange("h n p -> n h p"),
                              writes=["sshT%d" % d_])
                    S.op("act", lambda e: e.activation(hTb[:].rearrange("p d h q -> p (d h q)"), hT[:].rearrange("p d h q -> p (d h q)"), AF.Identity),
                         reads=["sshT0", "sshT1"], writes=["sshTb0", "sshTb1"])
                else:
                    S.op("dve", lambda e: e.memset(hT[:], 0.0), writes=["sshT0", "sshT1"])
                    S.op("dve", lambda e: e.memset(hTb[:], 0.0), writes=["sshTb0", "sshTb1"])
            yb = [PS(5), PS(6)]
            sbank_[0] ^= 1
            sps7, spk7 = PS(3 + sbank_[0])
            w8, w8k = w8r.next()
            c0_ = dr * 64 + g * 8
            S.op("dve", lambda e: e.tensor_tensor(w8[:], Lm[:, :, qe], dtt[:, 0, c, c0_:c0_ + 8], ALU.mult),
                 reads=[Lmk, "ssdtt"], writes=[w8k])
            per_head = []
            for hh in range(8):
                col = dr * 64 + g * 8 + hh
                dcol = dtt[:, 0, c, col:col + 1]
                MT, MTk = MTr.next()
                Cd, Cdk = Cdr.next()
                Bw, Bwk = Bwr.next()
                S.op("dve", lambda e, MT=MT, hh=hh, dcol=dcol:
                     e.scalar_tensor_tensor(MT[:], Lm[:, hh, :], dcol, cb[:, 0, :], ALU.mult, ALU.mult),
                     reads=[Lmk, cbk, "ssdtt"], writes=[MTk])
                S.op("act", lambda e, Bw=Bw, hh=hh: e.activation(Bw[:], Bk[:, c, :], AF.Identity, scale=w8[:, hh:hh + 1]),
                     reads=["ssBk", w8k], writes=[Bwk])
                per_head.append((MT, MTk, Cd, Cdk, Bw, Bwk))
            for hh in range(8):
                Cd, Cdk = per_head[hh][2], per_head[hh][3]
                S.op("dve", lambda e, Cd=Cd, hh=hh: e.tensor_tensor(Cd[:], CT[:, tsl], Ed[:, hh, :], ALU.mult),
                     reads=["ssCT", Edk], writes=[Cdk])
            for hh in range(8):
                MT, MTk, Cd, Cdk, Bw, Bwk = per_head[hh]
                ybk = yb[hh // 4]
                yreg = ybk[0][0:64, (hh % 4) * 128:(hh % 4 + 1) * 128]
                xs = Xk[:, c, hh * 64:(hh + 1) * 64]
                S.op("pe", lambda e, Bw=Bw, xs=xs, hh=hh: e.matmul(sps7[:, hh * 64:(hh + 1) * 64], Bw[:], xs, start=True, stop=True),
                     reads=[Bwk, "ssXk"], writes=[spk7])
                S.op("pe", lambda e, yreg=yreg, xs=xs, MT=MT: e.matmul(yreg, xs, MT[:], start=True, stop=False),
                     reads=["ssXk", MTk], writes=[ybk[1]])
                if dr == 0:
                    S.op("pe", lambda e, yreg=yreg, xs=xs, hh=hh: e.matmul(yreg, xs, idsk[:, hh, :], start=False, stop=False),
                         reads=["ssXk", "ssid"], writes=[ybk[1]])
                S.op("pe", lambda e, yreg=yreg, hh=hh, Cd=Cd: e.matmul(yreg, hTb[:, dr, hh, :], Cd[:], start=False, stop=True),
                     reads=[hbk, Cdk], writes=[ybk[1]])
            for half in range(2):
                yk = "ssyb%d_%d" % (c, half)
                ydst = ybuf[:, half * 4:(half + 1) * 4, tsl]
                ysrc = yb[half][0][0:64, :].rearrange("p (h q) -> p h q", h=4)
                if dr == 0:
                    S.op("act", lambda e, ydst=ydst, ysrc=ysrc: e.activation(ydst, ysrc, AF.Identity), reads=[yb[half][1]], writes=[yk])
                else:
                    S.op("dve", lambda e, ydst=ydst, ysrc=ysrc: e.tensor_tensor(ydst, ysrc, ydst, ALU.add), reads=[yb[half][1], yk], writes=[yk])
                    ykeys.append(yk)
            s1["sps7"], s1["spk7"] = sps7, spk7

        def stage2b(it, s1):
            g, si, dr, c = it["g"], it["si"], it["dr"], it["c"]
            Ed, Edk, sps7, spk7 = s1["Ed"], s1["Edk"], s1["sps7"], s1["spk7"]
            qe = 127 if dr == 0 else 0
            hk, hbk = "sshT%d" % dr, "sshTb%d" % dr
            for hh in range(8):
                S.op("dve", lambda e, hh=hh:
                     e.scalar_tensor_tensor(hT[:, dr, hh, :], hT[:, dr, hh, :], Ed[:, hh, qe:qe + 1], sps7[:, hh * 64:(hh + 1) * 64],
                                            ALU.mult, ALU.add),
                     reads=[hk, Edk, spk7], writes=[hk])
            S.op("act", lambda e: e.activation(hTb[:, dr, :, :].rearrange("p h q -> p (h q)"),
                                               hT[:, dr, :, :].rearrange("p h q -> p (h q)"), AF.Identity),
                 reads=[hk], writes=[hbk])
            if it["sweep_end"] and si > 0:
                S.dma("sp", ctx.out["new_stateT"][si - 1, dr, g * 8:(g + 1) * 8].rearrange("h n p -> n h p"), hT[:, dr, :, :], reads=[hk])
            if it["group_end"]:
                for hh in range(8):
                    S.dma("sp", ctx.yT[g * 4 + hh // 2, (hh % 2) * 64:(hh % 2) * 64 + 64, :], ybuf[:, hh, :], reads=list(ykeys))
                del ykeys[:]

        iters = []
        for g in range(8):
            for si, (a0, n) in enumerate(SEQS):
                c_lo, c_hi = a0 // 128, (a0 + n) // 128
                for dr in range(2):
                    order = list(range(c_lo, c_hi)) if dr == 0 else list(range(c_hi - 1, c_lo - 1, -1))
                    for j, c in enumerate(order):
                        iters.append(dict(g=g, si=si, dr=dr, c=c, seq_start=(dr == 0 and j == 0), sweep_end=(j == len(order) - 1),
                                          group_start=(si == 0 and dr == 0 and j == 0), group_end=False))
            iters[-1]["group_end"] = True
        for g in range(8):
            its = [it for it in iters if it["g"] == g]
            group_setup(g)
            n = len(its)
            s1 = {0: stage1(its[0])}
            if n > 1:
                s1[1] = stage1(its[1])
            for i in range(n):
                stage2(its[i], s1[i])
                if i + 2 < n:
                    s1[i + 2] = stage1(its[i + 2])
                stage2b(its[i], s1.pop(i))
        ctx.flush()


def build(cfg):
    nc = bass.Bass("TRN2", target_bir_lowering=False)
    ctx = Ctx()
    ctx.nc = nc
    kinds = cfg["kinds"]

    def din(name, shape, dt=F32):
        return nc.dram_tensor(name, list(shape), dt, kind="ExternalInput").ap()

    def dout(name, shape, dt=F32):
        return nc.dram_tensor(name, list(shape), dt, kind="ExternalOutput").ap()

    def dscr(name, shape, dt):
        return nc.dram_tensor(name, list(shape), dt, kind="Internal").ap()

    xT_in = din("xT_in", [DC, 128, T])
    pvec = din("pvec", [128, cfg["npv"]])
    cbf = din("cbf", [128, NCBF], BF16)
    ctx.w = {}
    for name, shape in cfg["wshapes"].items():
        ctx.w[name] = din(name, shape)
    ctx.c = {}
    for name, shape in cfg["cshapes"].items():
        ctx.c[name] = din(name, shape, BF16 if cfg["cdtypes"][name] == "bfloat16" else F32)
    ctx.out = {}
    for name, shape in cfg["oshapes"].items():
        ctx.out[name] = dout(name, shape)
    yT = dout("yT", [DC, 128, T])
    ctx.xT = dscr("xT", [DC, 128, T], F32)
    ctx.hT = dscr("hT", [DC, 128, T], BF16)
    ctx.aT = dscr("aT", [FC, 128, T], BF16)
    ctx.condT = dscr("condT", [128, DC, 2], BF16)
    ctx.qkvT = dscr("qkvT", [48, 128, T], BF16)
    ctx.vtok = dscr("vtok", [T + 512, 2048], BF16)
    ctx.oacc = dscr("oacc", [16, 256, T], F32)
    ctx.oT = dscr("oT", [DC, 128, T], BF16)
    if 3 in kinds:
        ctx.szT = dscr("szT", [32, 128, T], BF16)
        ctx.xbcraw = dscr("xbcraw", [48, 128, T], BF16)
        ctx.dtraw = dscr("dtraw", [128, T], F32)
        ctx.xcT = dscr("xcT", [32, 128, T], BF16)
        ctx.bcT = dscr("bcT", [16, 128, T], BF16)
        ctx.dttok = dscr("dttok", [128, 2, 20, 128], F32)
        ctx.xtok = dscr("xtok", [T, 4096], BF16)
        ctx.btok = dscr("btok", [T, 1024], BF16)
        ctx.yT = dscr("yT_ssd", [32, 128, T], BF16)
        ctx.ygT = dscr("ygT", [32, 128, T], BF16)
    if 2 in kinds:
        ctx.cqT = dscr("cqT", [4, 128, T], F32)
        ctx.cqnT = dscr("cqnT", [4, 128, T], BF16)
        ctx.mlaq = dscr("mlaq", [16, 192, T], BF16)
        ctx.ckvraw = dscr("ckvraw", [2, 128, T], F32)
        ctx.ckvT = dscr("ckvT", [2, 128, T + 512], BF16)
        ctx.krT = dscr("krT", [64, T + 512], BF16)
        ctx.mlakv = dscr("mlakv", [32, 128, T + 512], BF16)
    ctx.pvcols = cfg["pvcols"]

    with contextlib.ExitStack() as st:
        E = st.enter_context
        ctx.S = Sched(nc, st)
        S = ctx.S
        S.max_flush = cfg.get("max_flush")
        ctx.flush = S.flush
        ctx.pvt = E(nc.sbuf_tensor("pvt", [128, cfg["npv"]], F32))
        ctx.cbt = E(nc.sbuf_tensor("cbt", [128, NCBF], BF16))
        ctx.ones_bf = ctx.cbt[:, 0:128]
        ctx.ident_bf = ctx.cbt[:, 128:256]
        ctx.R128 = ctx.cbt[:, 256:384]
        ctx.R64 = ctx.cbt[0:64, 384:448]
        ctx.modt = E(nc.sbuf_tensor("modt", [128, 144, 2], F32))
        ctx.A = E(nc.sbuf_tensor("modA", [128, 3, 16, 2], F32))
        ctx.B = E(nc.sbuf_tensor("modB", [128, 3, 16, 2], F32))
        ctx.G = E(nc.sbuf_tensor("modG", [128, 3, 16, 2], F32))
        ctx.epst = E(nc.sbuf_tensor("epst", [128, 2], F32))
        ctx.eps_ap = ctx.epst[:, 0:1]
        ctx.one_ap = ctx.epst[:, 1:2]
        S.op("dve", lambda e: e.memset(ctx.epst[:, 0:1], EPS), writes=["const"])
        S.op("dve", lambda e: e.memset(ctx.epst[:, 1:2], 1.0), writes=["const"])
        if 3 in kinds:
            ctx.idf = E(nc.sbuf_tensor("identf", [128, 128], F32))
            ctx.ident_f32 = ctx.idf[:]
            S.dma("sp", ctx.idf[:], ctx.c["ssdc"][:, 640:768], writes=["constf"])
        ctx.ps = [E(nc.psum_tensor("psb%d" % i, [128, 512], F32)) for i in range(8)]
        ctx.ps_i = 0

        S.dma("sp", ctx.pvt[:], pvec, writes=["pv"])
        S.dma("sp", ctx.cbt[:], cbf, writes=["const"])
        with contextlib.ExitStack() as st2:
            ct = st2.enter_context(nc.sbuf_tensor("condt", [128, 32], BF16))
            S.op("act", lambda e: e.activation(ct[:], pvap(ctx, "cond", 0, 32), AF.Silu), reads=["pv"], writes=["ct"])
            S.dma("sp", ctx.condT, ct[:].rearrange("p (c t) -> p c t", t=2), reads=["ct"])
            xr = Ring(ctx, st2, "px", [128, DC, 512], F32, 2)
            for t0 in range(0, T, 512):
                xt, xk = xr.next()
                S.dma("sp", xt[:], xT_in[:, :, t0:t0 + 512].rearrange("c p t -> p c t"), writes=[xk])
                S.dma("sp", ctx.xT[:, :, t0:t0 + 512].rearrange("c p t -> p c t"), xt[:], reads=[xk])
            S.flush()

        for layer, kind in enumerate(kinds):
            mod_phase(ctx, layer)
            ffn_phase(ctx, layer, 0)
            lidx = cfg.get("layer_index", list(range(len(kinds))))[layer]
            if kind == 0:
                diff_layer(ctx, lidx)
            elif kind == 1:
                na_layer(ctx, lidx)
            elif kind == 2:
                mla_layer(ctx, lidx)
            elif kind == 3:
                ssd_layer(ctx, lidx)
            ffn_phase(ctx, layer, 1)

        rms_phase(ctx, ctx.xT, DC, D, yT, lambda c, cond: pvap(ctx, "final_w", c), None, out_f32=True)
    return nc, ctx


NCBF = 448
WNAMES = ["ada_w", "ffn_w_gu", "ffn_w_down", "diff_w_qkv", "diff_w_o"]


def rope_tables(d):
    hd = d // 2
    hh = hd // 2
    t = np.arange(2048)
    tab = np.zeros((2, d, 2048), np.float32)
    R = np.zeros((d, d), np.float32)
    for p in range(d):
        half, i = p // hd, p % hd
        j, first = i % hh, i < hh
        inv = np.float32(10000.0) ** (-np.float32(j) / np.float32(hh))
        pos = (t // 64) if half == 0 else (t % 64)
        ang = pos.astype(np.float32) * np.float32(inv)
        tab[0, p] = np.cos(ang)
        tab[1, p] = -np.sin(ang) if first else np.sin(ang)
        partner = p + hh if first else p - hh
        R[partner, p] = 1.0
    return tab, R


_CONST_CACHE = {}


def host_consts():
    if "cb" not in _CONST_CACHE:
        cb = np.zeros((128, NCBF), dtype=np.float32)
        cb[:, 0:128] = 1.0
        cb[:, 128:256] = np.eye(128, dtype=np.float32)
        t128, R128 = rope_tables(128)
        t64, R64 = rope_tables(64)
        cb[:, 256:384] = R128
        cb[0:64, 384:448] = R64
        _CONST_CACHE["cb"] = cb.astype(ml_dtypes.bfloat16)
        _CONST_CACHE["rope128"] = t128
        _CONST_CACHE["rope64"] = t64
    return _CONST_CACHE


def ssd_consts():
    k = np.arange(128)[:, None]
    j = np.arange(128)[None, :]
    c = np.zeros((128, 768), np.float32)
    c[:, 0:128] = (k > j)
    c[:, 128:256] = (k < j)
    c[:, 256:384] = (k <= j)
    c[:, 384:512] = (k >= j)
    c[:, 512:640] = 1.0
    c[:, 640:768] = np.eye(128)
    return c


def make_inputs(inp, core, kinds):
    xs = inp["x_sample"][core]
    xp = inp["x_prompt"][2 * core:2 * core + 2].reshape(512, D)
    x = np.concatenate([xs, xp], 0)
    xT = np.ascontiguousarray(x.T).reshape(DC, 128, T)
    pv = pv_layout(inp, core, kinds)
    hc = host_consts()
    m = {"xT_in": xT, "pvec": pv.build(), "cbf": hc["cb"]}
    wn = ["ada_w", "ffn_w_gu", "ffn_w_down"]
    cs = {}
    outs = {}
    if 0 in kinds:
        wn += ["diff_w_qkv", "diff_w_o"]
        cs["rope128"] = hc["rope128"]
        ck = inp["cache_diff_k"][core, 0]
        cs["cache_diff_kT"] = np.ascontiguousarray(ck.transpose(1, 2, 3, 0)).reshape(16, 128, 512)
        cs["cache_diff_v"] = np.ascontiguousarray(inp["cache_diff_v"][core, 0].reshape(512, 2048))
        outs["new_diff_kT"] = (16, 128, 512)
        outs["new_diff_vT"] = (16, 128, 512)
    if 1 in kinds:
        wn += ["na_w_qkv", "na_w_o"]
        if "na_tabs" not in hc or hc.get("na_rpb_id") is not inp["na_rpb"]:
            hc["na_tabs"] = na_host_tables(inp["na_rpb"][0])
            hc["na_rpb_id"] = inp["na_rpb"]
        cs["na_bias"], cs["na_mask"] = hc["na_tabs"]
        ck = inp["cache_na_k"][core, 0]
        cs["cache_na_kT"] = np.ascontiguousarray(ck.transpose(1, 2, 0))
        cs["cache_na_v"] = np.ascontiguousarray(inp["cache_na_v"][core, 0].reshape(512, 2048))
        outs["new_na_kT"] = (16, 128, 512)
        outs["new_na_vT"] = (16, 128, 512)
    if 2 in kinds:
        wn += ["mla_w_dq", "mla_w_uq", "mla_w_dkv", "mla_w_ukv", "mla_w_o"]
        cs["rope64"] = hc["rope64"]
        cs["cache_mla_ckvT"] = np.ascontiguousarray(inp["cache_mla_ckv"][core, 0].T).reshape(2, 128, 512)
        cs["cache_mla_kropeT"] = np.ascontiguousarray(inp["cache_mla_krope"][core, 0].T)
        outs["new_mla_ckvT"] = (2, 128, 512)
        outs["new_mla_kropeT"] = (64, 512)
    if 3 in kinds:
        wn += ["ssd_w_in", "ssd_w_out"]
        cs["ssdc"] = ssd_consts()
        cs["ssd_stateT"] = np.ascontiguousarray(inp["state_ssd"][core, 0].transpose(0, 1, 3, 2))
        outs["new_stateT"] = (2, 2, 64, 128, 64)
    for n in wn:
        m[n] = inp[n]
    m.update(cs)
    cfg = dict(npv=pv.n, pvcols=pv.cols, kinds=kinds, wshapes={k: inp[k].shape for k in wn},
               cshapes={k: v.shape for k, v in cs.items()}, cdtypes={k: str(v.dtype) for k, v in cs.items()}, oshapes=outs)
    return m, cfg


def kernel(**inp):
    inp = {k: np.asarray(v) for k, v in inp.items()}
    n = 8
    kinds = [0, 1, 2, 3]
    maps = []
    cfg = None
    for core in range(n):
        m, cfg = make_inputs(inp, core, kinds)
        maps.append(m)
    nc, ctx = build(cfg)
    res = run_bass_kernel_spmd(nc, maps, core_ids=list(range(n)))
    return assemble(res.results, kinds)


def assemble(results, kinds):
    n = len(results)
    ys = np.zeros((n, 2048, D), np.float32)
    yp = np.zeros((2 * n, 256, D), np.float32)
    o = {}
    for core in range(n):
        r = results[core]
        y = r["yT"].reshape(D, T).T
        ys[core] = y[:TS]
        yp[2 * core:2 * core + 2] = y[TS:].reshape(2, 256, D)

        def tokmajor(name, nf):
            return r[name].reshape(nf, 512).T.reshape(2, 256, nf)
        if 0 in kinds:
            o.setdefault("dk", []).append(tokmajor("new_diff_kT", 2048).reshape(2, 1, 256, 8, 2, 128))
            o.setdefault("dv", []).append(tokmajor("new_diff_vT", 2048).reshape(2, 1, 256, 8, 256))
        if 1 in kinds:
            o.setdefault("nk", []).append(tokmajor("new_na_kT", 2048).reshape(2, 1, 256, 16, 128))
            o.setdefault("nv", []).append(tokmajor("new_na_vT", 2048).reshape(2, 1, 256, 16, 128))
        if 2 in kinds:
            o.setdefault("ckv", []).append(tokmajor("new_mla_ckvT", 256).reshape(2, 1, 256, 256))
            o.setdefault("kr", []).append(tokmajor("new_mla_kropeT", 64).reshape(2, 1, 256, 64))
        if 3 in kinds:
            o.setdefault("hs", []).append(np.ascontiguousarray(r["new_stateT"].reshape(2, 1, 2, 64, 128, 64).transpose(0, 1, 2, 3, 5, 4)))
    res = [yp, ys]
    for k in ("dk", "dv", "nk", "nv", "ckv", "kr", "hs"):
        if k in o:
            res.append(np.concatenate(o[k], 0))
    return tuple(res)
```
